# Optimizing a Trainium2 kernel written in Bass

```python
import math
import jax
import jax.numpy as jnp
from jax import lax
import numpy as np

D_MODEL = 1024
BATCH = 16
SEQ = 4096
DEPTH = 1

NORM_EPS = 1e-6
GDN_HEADS = 4
GDN_HEAD_DIM = 128
GDN_WIDTH = GDN_HEADS * GDN_HEAD_DIM
GDN_CONV = 4
GDN_CHUNK = 64
NSA_HEADS = 8
NSA_KV_GROUPS = 2
NSA_GROUP_HEADS = NSA_HEADS // NSA_KV_GROUPS
NSA_HEAD_DIM = 64
NSA_WIDTH = NSA_HEADS * NSA_HEAD_DIM
NSA_KV_WIDTH = NSA_KV_GROUPS * NSA_HEAD_DIM
CMP_BLOCK = 32
CMP_STRIDE = 16
CMP_HIDDEN = 128
SEL_BLOCK = 64
SEL_COUNT = 16
WINDOW = 512
Q_BLOCK = 128
ROPE_THETA = 500000.0
ROPE_DIM = NSA_HEAD_DIM // 4
FORCED_SCORE = 1000.0
NEG_INF = -1e30
FFN_HIDDEN = -(-8 * D_MODEL // (3 * 256)) * 256
IN_SIZES = (3 * GDN_WIDTH, GDN_WIDTH, GDN_HEADS, GDN_HEADS,
            NSA_WIDTH, NSA_KV_WIDTH, NSA_KV_WIDTH, NSA_KV_WIDTH, NSA_KV_WIDTH, NSA_KV_WIDTH, NSA_KV_WIDTH,
            3 * NSA_HEADS, D_MODEL, D_MODEL)
IN_WIDTH = 4 * GDN_WIDTH + 2 * GDN_HEADS + NSA_WIDTH + 6 * NSA_KV_WIDTH + 3 * NSA_HEADS + 2 * D_MODEL

kernel_name = 'hybrid_gdn_nsa_block'


def rms_norm(x, gain):
    x32 = x.astype(jnp.float32)
    y = x32 * lax.rsqrt(jnp.mean(x32 * x32, axis=-1, keepdims=True) + NORM_EPS)
    return (y * gain.astype(jnp.float32)).astype(x.dtype)


def l2_normalize(x):
    return x * lax.rsqrt(jnp.sum(x * x, axis=-1, keepdims=True) + NORM_EPS)


def causal_depthwise_conv(x, w):
    k_len, ch = w.shape
    return lax.conv_general_dilated(x, w[:, None, :].astype(x.dtype), window_strides=(1,),
                                    padding=[(k_len - 1, 0)], dimension_numbers=('NWC', 'WIO', 'NWC'),
                                    feature_group_count=ch)


def partial_rope(x, pos):
    half = ROPE_DIM // 2
    inv_freq = ROPE_THETA ** (-jnp.arange(half, dtype=jnp.float32) / half)
    ang = pos.astype(jnp.float32)[:, None] * inv_freq
    bshape = (pos.shape[0],) + (1,) * (x.ndim - 3) + (half,)
    cos = jnp.cos(ang).reshape(bshape)
    sin = jnp.sin(ang).reshape(bshape)
    x1 = x[..., :half].astype(jnp.float32)
    x2 = x[..., half:ROPE_DIM].astype(jnp.float32)
    rot = jnp.concatenate([x1 * cos - x2 * sin, x2 * cos + x1 * sin], axis=-1).astype(x.dtype)
    return jnp.concatenate([rot, x[..., ROPE_DIM:]], axis=-1)


def gated_delta_rule_chunked(q, k, v, beta, g):
    b_, s_, h_, dk = q.shape
    dv = v.shape[-1]
    c = GDN_CHUNK
    n = s_ // c

    def to_chunks(a):
        return a.reshape(b_, n, c, h_, a.shape[-1]).transpose(0, 3, 1, 2, 4)

    q, k, v = to_chunks(q), to_chunks(k), to_chunks(v)
    beta = beta.reshape(b_, n, c, h_).transpose(0, 3, 1, 2)
    gc = jnp.cumsum(g.reshape(b_, n, c, h_).transpose(0, 3, 1, 2), axis=-1)
    causal = jnp.tril(jnp.ones((c, c), dtype=bool))
    strict = jnp.tril(jnp.ones((c, c), dtype=bool), -1)
    decay = jnp.exp(jnp.where(causal, gc[..., :, None] - gc[..., None, :], -jnp.inf))
    kb = k * beta[..., None]
    a_low = jnp.where(strict, jnp.einsum('bhnid,bhnjd->bhnij', kb, k) * decay, 0.0)
    ia = a_low + jnp.eye(c, dtype=q.dtype)
    u = lax.linalg.triangular_solve(ia, v * beta[..., None], left_side=True, lower=True, unit_diagonal=True)
    w = lax.linalg.triangular_solve(ia, kb * jnp.exp(gc)[..., None], left_side=True, lower=True, unit_diagonal=True)
    qk = jnp.einsum('bhnid,bhnjd->bhnij', q, k) * decay
    qg = q * jnp.exp(gc)[..., None]
    kd = k * jnp.exp(gc[..., -1:] - gc)[..., None]
    g_last = jnp.exp(gc[..., -1])

    def step(state, inp):
        qk_n, qg_n, kd_n, u_n, w_n, gl_n = inp
        v_new = u_n - jnp.einsum('bhcd,bhde->bhce', w_n, state)
        o_n = jnp.einsum('bhcd,bhde->bhce', qg_n, state) + jnp.einsum('bhij,bhje->bhie', qk_n, v_new)
        state = state * gl_n[..., None, None] + jnp.einsum('bhcd,bhce->bhde', kd_n, v_new)
        return state, o_n

    xs = tuple(jnp.moveaxis(a, 2, 0) for a in (qk, qg, kd, u, w, g_last))
    state0 = jnp.zeros((b_, h_, dk, dv), dtype=q.dtype)
    _, o = lax.scan(step, state0, xs)
    return o.transpose(1, 0, 3, 2, 4).reshape(b_, s_, h_, dv)


def compress_blocks(kv, pos_emb, w1, w2):
    s_ = kv.shape[1]
    n_cmp = (s_ - CMP_BLOCK) // CMP_STRIDE + 1
    idx = jnp.arange(n_cmp)[:, None] * CMP_STRIDE + jnp.arange(CMP_BLOCK)[None, :]
    blocks = kv[:, idx] + pos_emb[:, None, :]
    hid = jax.nn.silu(jnp.einsum('bnlgd,ldh->bgnh', blocks, w1))
    return jnp.einsum('bgnh,hd->bgnd', hid, w2)


def native_sparse_attention(q, k_cmp, v_cmp, k_slc, v_slc, k_win, v_win, gates,
                            cmp_pos_k, cmp_w1_k, cmp_w2_k, cmp_pos_v, cmp_w1_v, cmp_w2_v):
    b_, s_, g_, r_, dh = q.shape
    n_cmp = (s_ - CMP_BLOCK) // CMP_STRIDE + 1
    n_blk = s_ // SEL_BLOCK
    n_top = min(SEL_COUNT, n_blk)
    kc = compress_blocks(k_cmp, cmp_pos_k, cmp_w1_k, cmp_w2_k)
    vc = compress_blocks(v_cmp, cmp_pos_v, cmp_w1_v, cmp_w2_v)
    cmp_start = jnp.arange(n_cmp) * CMP_STRIDE
    cmp_end = cmp_start + CMP_BLOCK - 1
    blk_ids = jnp.arange(n_blk)
    sel_start = blk_ids * SEL_BLOCK
    overlap = ((cmp_start[:, None] <= sel_start[None, :] + SEL_BLOCK - 1)
               & (cmp_end[:, None] >= sel_start[None, :])).astype(jnp.float32)

    def to_blocks(a):
        return a.reshape(b_, n_blk, SEL_BLOCK, g_, dh).transpose(0, 3, 1, 2, 4).reshape(b_ * g_ * n_blk, SEL_BLOCK, dh)

    ks_flat = to_blocks(k_slc)
    vs_flat = to_blocks(v_slc)
    base = ((jnp.arange(b_)[:, None] * g_ + jnp.arange(g_)[None, :]) * n_blk)[:, :, None, None]
    pad = ((0, 0), (WINDOW, 0), (0, 0), (0, 0))
    kw_pad = jnp.pad(k_win, pad)
    vw_pad = jnp.pad(v_win, pad)
    span = WINDOW + Q_BLOCK
    q_off = jnp.arange(Q_BLOCK)
    sel_off = jnp.arange(SEL_BLOCK)
    win_off = jnp.arange(span) - WINDOW

    def query_block(qi):
        s0 = qi * Q_BLOCK
        t = s0 + q_off
        qb = lax.dynamic_slice_in_dim(q, s0, Q_BLOCK, axis=1)
        gb = lax.dynamic_slice_in_dim(gates, s0, Q_BLOCK, axis=1)
        vis = cmp_end[None, :] <= t[:, None]
        sc = jnp.einsum('bqgrd,bgnd->bgrqn', qb, kc).astype(jnp.float32)
        p_cmp = jnp.where(vis, jax.nn.softmax(jnp.where(vis, sc, NEG_INF), axis=-1), 0.0)
        o_cmp = jnp.einsum('bgrqn,bgnd->bqgrd', p_cmp.astype(vc.dtype), vc)
        imp = jnp.einsum('bgrqn,nj->bgqj', p_cmp, overlap)
        cur = (t // SEL_BLOCK)[:, None]
        valid = blk_ids[None, :] <= cur
        forced = (blk_ids[None, :] == 0) | (blk_ids[None, :] == cur) | (blk_ids[None, :] == cur - 1)
        score = jnp.where(valid, jnp.where(forced, FORCED_SCORE, imp), -1.0)
        _, top = lax.top_k(score, n_top)
        flat = base + top
        s_list = []
        for s in range(n_top):
            kg = ks_flat[flat[..., s]]
            kpos = top[..., s, None] * SEL_BLOCK + sel_off
            sc_s = jnp.einsum('bqgrd,bgqkd->bgrqk', qb, kg).astype(jnp.float32)
            s_list.append(jnp.where((kpos <= t[:, None])[:, :, None], sc_s, NEG_INF))
        p_sel = jax.nn.softmax(jnp.concatenate(s_list, axis=-1), axis=-1).astype(vs_flat.dtype)
        o_sel = sum(jnp.einsum('bgrqk,bgqkd->bqgrd', p_sel[..., s * SEL_BLOCK:(s + 1) * SEL_BLOCK],
                               vs_flat[flat[..., s]]) for s in range(n_top))
        kw = lax.dynamic_slice_in_dim(kw_pad, s0, span, axis=1)
        vw = lax.dynamic_slice_in_dim(vw_pad, s0, span, axis=1)
        kpos_w = s0 + win_off
        rel = t[:, None] - kpos_w[None, :]
        ok = (rel >= 0) & (rel < WINDOW) & (kpos_w[None, :] >= 0)
        sc_w = jnp.einsum('bqgrd,bkgd->bgrqk', qb, kw).astype(jnp.float32)
        p_w = jax.nn.softmax(jnp.where(ok, sc_w, NEG_INF), axis=-1)
        o_win = jnp.einsum('bgrqk,bkgd->bqgrd', p_w.astype(vw.dtype), vw)
        return gb[..., 0:1] * o_cmp + gb[..., 1:2] * o_sel + gb[..., 2:3] * o_win

    out = lax.map(query_block, jnp.arange(s_ // Q_BLOCK))
    return out.transpose(1, 0, 2, 3, 4, 5).reshape(b_, s_, g_ * r_ * dh)


def hybrid_layer(x, pos, mix_norm_gain, w_in, gdn_conv_w, gdn_a_log, gdn_dt_bias, gdn_out_norm_gain,
                 cmp_pos_k, cmp_w1_k, cmp_w2_k, cmp_pos_v, cmp_w1_v, cmp_w2_v,
                 w_branch_gdn, w_branch_nsa, w_out, ffn_norm_gain, w_gate_up, w_down):
    f32 = jnp.float32
    b_, s_, _ = x.shape
    h = rms_norm(x, mix_norm_gain)
    proj = jnp.einsum('bsd,de->bse', h, w_in)
    split_at = np.cumsum(IN_SIZES)[:-1].tolist()
    (qkv_a, z_a, beta_a, alpha_a, q_b, kc_b, vc_b, ks_b, vs_b, kw_b, vw_b,
     gate_b, merge_a, merge_b) = jnp.split(proj, split_at, axis=-1)

    qkv_a = jax.nn.silu(causal_depthwise_conv(qkv_a, gdn_conv_w))
    q_a, k_a, v_a = [a.reshape(b_, s_, GDN_HEADS, GDN_HEAD_DIM).astype(f32) for a in jnp.split(qkv_a, 3, axis=-1)]
    q_a = l2_normalize(q_a) * (GDN_HEAD_DIM ** -0.5)
    k_a = l2_normalize(k_a)
    beta = jax.nn.sigmoid(beta_a.astype(f32))
    log_decay = -jnp.exp(gdn_a_log.astype(f32)) * jax.nn.softplus(alpha_a.astype(f32) + gdn_dt_bias.astype(f32))
    o_a = gated_delta_rule_chunked(q_a, k_a, v_a, beta, log_decay)
    o_a = (o_a * lax.rsqrt(jnp.mean(o_a * o_a, axis=-1, keepdims=True) + NORM_EPS)
           * gdn_out_norm_gain.astype(f32)
           * jax.nn.silu(z_a.astype(f32).reshape(b_, s_, GDN_HEADS, GDN_HEAD_DIM)))
    o_a = o_a.reshape(b_, s_, GDN_WIDTH).astype(x.dtype)

    q_n = partial_rope(q_b.reshape(b_, s_, NSA_KV_GROUPS, NSA_GROUP_HEADS, NSA_HEAD_DIM), pos) * (NSA_HEAD_DIM ** -0.5)
    kv_shape = (b_, s_, NSA_KV_GROUPS, NSA_HEAD_DIM)
    k_cmp = partial_rope(kc_b.reshape(kv_shape), pos)
    k_slc = partial_rope(ks_b.reshape(kv_shape), pos)
    k_win = partial_rope(kw_b.reshape(kv_shape), pos)
    gates = jax.nn.sigmoid(gate_b.reshape(b_, s_, NSA_KV_GROUPS, NSA_GROUP_HEADS, 3))
    o_b = native_sparse_attention(q_n, k_cmp, vc_b.reshape(kv_shape), k_slc, vs_b.reshape(kv_shape),
                                  k_win, vw_b.reshape(kv_shape), gates,
                                  cmp_pos_k, cmp_w1_k, cmp_w2_k, cmp_pos_v, cmp_w1_v, cmp_w2_v)

    merged = (jax.nn.sigmoid(merge_a) * jnp.einsum('bse,ed->bsd', o_a, w_branch_gdn)
              + jax.nn.sigmoid(merge_b) * jnp.einsum('bse,ed->bsd', o_b, w_branch_nsa))
    x = x + jnp.einsum('bsd,de->bse', merged, w_out)

    h2 = rms_norm(x, ffn_norm_gain)
    gate, up = jnp.split(jnp.einsum('bsd,df->bsf', h2, w_gate_up), 2, axis=-1)
    return x + jnp.einsum('bsf,fd->bsd', jax.nn.silu(gate) * up, w_down)


def setup_inputs(seed: int = 0) -> dict:
    key = jax.random.key(seed)
    ks = jax.random.split(key, 24)
    f32 = jnp.float32
    L = DEPTH

    def nrm(k, shape, scale):
        return jax.random.normal(k, shape, f32) * scale

    def gain(k, shape):
        return 1.0 + 0.02 * jax.random.normal(k, shape, f32)

    dt = jnp.exp(jax.random.uniform(ks[5], (L, GDN_HEADS), f32, math.log(1e-3), math.log(1e-1)))
    return {
        'x': nrm(ks[0], (BATCH, SEQ, D_MODEL), 1.0),
        'mix_norm_gain': gain(ks[1], (L, D_MODEL)),
        'w_in': nrm(ks[2], (L, D_MODEL, IN_WIDTH), D_MODEL ** -0.5),
        'gdn_conv_w': nrm(ks[3], (L, GDN_CONV, 3 * GDN_WIDTH), GDN_CONV ** -0.5),
        'gdn_a_log': jnp.log(jax.random.uniform(ks[4], (L, GDN_HEADS), f32, 1.0, 16.0)),
        'gdn_dt_bias': dt + jnp.log(-jnp.expm1(-dt)),
        'gdn_out_norm_gain': gain(ks[6], (L, GDN_HEAD_DIM)),
        'cmp_pos_k': nrm(ks[7], (L, CMP_BLOCK, NSA_HEAD_DIM), 0.02),
        'cmp_w1_k': nrm(ks[8], (L, CMP_BLOCK, NSA_HEAD_DIM, CMP_HIDDEN), (CMP_BLOCK * NSA_HEAD_DIM) ** -0.5),
        'cmp_w2_k': nrm(ks[9], (L, CMP_HIDDEN, NSA_HEAD_DIM), CMP_HIDDEN ** -0.5),
        'cmp_pos_v': nrm(ks[10], (L, CMP_BLOCK, NSA_HEAD_DIM), 0.02),
        'cmp_w1_v': nrm(ks[11], (L, CMP_BLOCK, NSA_HEAD_DIM, CMP_HIDDEN), (CMP_BLOCK * NSA_HEAD_DIM) ** -0.5),
        'cmp_w2_v': nrm(ks[12], (L, CMP_HIDDEN, NSA_HEAD_DIM), CMP_HIDDEN ** -0.5),
        'w_branch_gdn': nrm(ks[13], (L, GDN_WIDTH, D_MODEL), GDN_WIDTH ** -0.5),
        'w_branch_nsa': nrm(ks[14], (L, NSA_WIDTH, D_MODEL), NSA_WIDTH ** -0.5),
        'w_out': nrm(ks[15], (L, D_MODEL, D_MODEL), D_MODEL ** -0.5),
        'ffn_norm_gain': gain(ks[16], (L, D_MODEL)),
        'w_gate_up': nrm(ks[17], (L, D_MODEL, 2 * FFN_HIDDEN), D_MODEL ** -0.5),
        'w_down': nrm(ks[18], (L, FFN_HIDDEN, D_MODEL), FFN_HIDDEN ** -0.5),
        'final_norm_gain': gain(ks[19], (D_MODEL,)),
    }


def reference(x, mix_norm_gain, w_in, gdn_conv_w, gdn_a_log, gdn_dt_bias, gdn_out_norm_gain,
              cmp_pos_k, cmp_w1_k, cmp_w2_k, cmp_pos_v, cmp_w1_v, cmp_w2_v,
              w_branch_gdn, w_branch_nsa, w_out, ffn_norm_gain, w_gate_up, w_down, final_norm_gain):
    pos = jnp.arange(x.shape[1])
    for layer in range(DEPTH):
        x = hybrid_layer(x, pos, mix_norm_gain[layer], w_in[layer], gdn_conv_w[layer], gdn_a_log[layer],
                         gdn_dt_bias[layer], gdn_out_norm_gain[layer],
                         cmp_pos_k[layer], cmp_w1_k[layer], cmp_w2_k[layer],
                         cmp_pos_v[layer], cmp_w1_v[layer], cmp_w2_v[layer],
                         w_branch_gdn[layer], w_branch_nsa[layer], w_out[layer],
                         ffn_norm_gain[layer], w_gate_up[layer], w_down[layer])
    return rms_norm(x, final_norm_gain)
```

```python
import numpy as np
import ml_dtypes
import concourse.bass as bass
import concourse.mybir as mybir
from concourse.bass_utils import run_bass_kernel_spmd

F32 = mybir.dt.float32
BF16 = mybir.dt.bfloat16
U8 = mybir.dt.uint8
ALU = mybir.AluOpType
AF = mybir.ActivationFunctionType
DSZ = {F32: 4, BF16: 2, U8: 1}

ENGS = ("pe", "act", "dve", "pool", "sp")
DMA_POOL = 6
NEG = -30000.0


class V:
    __slots__ = ("ap", "reg")

    def __init__(self, ap, reg):
        self.ap = ap
        self.reg = reg

    def w(self, fn):
        return V(fn(self.ap), self.reg)

    def bc(self, shape):
        return V(self.ap.broadcast_to(list(shape)), self.reg)


class T:
    def __init__(self, space, ap, shape, esz, boff):
        self.space = space
        self.ap = ap
        self.shape = tuple(shape)
        self.esz = esz
        self.boff = boff
        fs = [1] * len(shape)
        for i in range(len(shape) - 2, 0, -1):
            fs[i] = fs[i + 1] * shape[i + 1]
        self.fstr = fs

    def __getitem__(self, idx):
        if not isinstance(idx, tuple):
            idx = (idx,)
        idx = idx + (slice(None),) * (len(self.shape) - len(idx))
        lo, hi = [], []
        for ix, n in zip(idx, self.shape):
            if isinstance(ix, slice):
                a, b, st = ix.indices(n)
                assert st > 0 and b > a, (self.space, idx, self.shape)
                lo.append(a)
                hi.append(a + ((b - a - 1) // st) * st)
            else:
                assert 0 <= ix < n, (self.space, idx, self.shape)
                lo.append(ix)
                hi.append(ix)
        f0 = sum(l * s for l, s in zip(lo[1:], self.fstr[1:]))
        f1 = sum(h * s for h, s in zip(hi[1:], self.fstr[1:])) + 1
        p0, p1 = lo[0], hi[0] + 1
        b0, b1 = self.boff + f0 * self.esz, self.boff + f1 * self.esz
        if self.space == "ps":
            p0, p1 = p0 // 32 * 32, (p1 + 31) // 32 * 32
            b0, b1 = b0 // 2048 * 2048, (b1 + 2047) // 2048 * 2048
        return V(self.ap[idx], (self.space, p0, p1, b0, b1))

    def all(self):
        return self[tuple(slice(None) for _ in self.shape)]


def _ovl(a, b):
    return a[0] == b[0] and a[1] < b[2] and b[1] < a[2] and a[3] < b[4] and b[3] < a[4]


def _contains(a, b):
    return a[0] == b[0] and a[1] <= b[1] and b[2] <= a[2] and a[3] <= b[3] and b[4] <= a[4]


class Op:
    __slots__ = ("eng", "emit", "idx", "deps", "dma", "tok", "sig", "eidx")


class Prog:
    def __init__(self, nc, sb_bytes=204288):
        self.nc = nc
        self.ops = []
        self.acc = {}
        self.sb_h = nc.alloc_sbuf_tensor("arena", [128, sb_bytes], U8)
        self.ps_h = nc.alloc_psum_tensor("psarena", [128, 4096], F32)
        self.sb_bytes = sb_bytes
        self.sb_off = 0
        self.marks = []

    def sb(self, shape, dtype, align=32):
        esz = DSZ[dtype]
        n = int(np.prod(shape[1:]))
        off = (self.sb_off + align - 1) // align * align
        nb = n * esz
        assert off + nb <= self.sb_bytes, ("SBUF arena overflow", off, nb)
        self.sb_off = off + nb
        ap = self.sb_h[:, off:off + nb].bitcast(dtype)
        if len(shape) > 2:
            names = " ".join("d%d" % i for i in range(1, len(shape)))
            ap = ap.rearrange("p (%s) -> p %s" % (names, names), **{"d%d" % i: shape[i] for i in range(1, len(shape))})
        return T("sb", ap, [128] + list(shape[1:]), esz, off)

    def mark(self):
        self.marks.append(self.sb_off)

    def release(self):
        self.sb_off = self.marks.pop()

    def ps(self, bank, shape, dtype=F32, boff=0):
        esz = DSZ[dtype]
        n = int(np.prod(shape[1:]))
        nb = n * esz
        assert boff + nb <= 2048
        e0 = (bank * 2048 + boff) // 4
        ap = self.ps_h[:, e0:e0 + (nb + 3) // 4]
        if dtype != F32:
            ap = ap.bitcast(dtype)
        if len(shape) > 2:
            names = " ".join("d%d" % i for i in range(1, len(shape)))
            ap = ap.rearrange("p (%s) -> p %s" % (names, names), **{"d%d" % i: shape[i] for i in range(1, len(shape))})
        return T("ps", ap, [128] + list(shape[1:]), esz, bank * 2048 + boff)

    def dram(self, name, shape, dtype, kind="Internal"):
        h = self.nc.dram_tensor(name, list(shape), dtype, kind=kind)
        return T(name, h, shape, 1, 0)

    def op(self, eng, emit, reads, writes, dma=False):
        o = Op()
        o.eng, o.emit, o.idx, o.dma, o.sig = eng, emit, len(self.ops), dma, False
        deps = set()
        for v in reads:
            r = v.reg
            for (reg, oi, w) in self.acc.setdefault(r[0], []):
                if w and _ovl(reg, r):
                    deps.add(oi)
        for v in writes:
            r = v.reg
            for (reg, oi, w) in self.acc.setdefault(r[0], []):
                if _ovl(reg, r):
                    deps.add(oi)
        deps.discard(o.idx)
        o.deps = deps
        self.ops.append(o)
        for v in writes:
            r = v.reg
            lst = self.acc[r[0]]
            lst[:] = [e for e in lst if not _contains(r, e[0])]
            lst.append((r, o.idx, True))
        for v in reads:
            r = v.reg
            lst = self.acc[r[0]]
            if not dma:
                lst[:] = [e for e in lst if not ((not e[2]) and _contains(r, e[0])
                                                 and (not self.ops[e[1]].dma) and self.ops[e[1]].eng == eng)]
            lst.append((r, o.idx, False))
        return o

    def dma(self, q, out, in_):
        return self.op(q, lambda e: e.dma_start(out=out.ap, in_=in_.ap), [in_], [out], dma=True)

    def mm(self, out, lhsT, rhs, start=True, stop=True, **kw):
        return self.op("pe", lambda e: e.matmul(out.ap, lhsT=lhsT.ap, rhs=rhs.ap, start=start, stop=stop, **kw), [lhsT, rhs], [out])

    def tr(self, out, in_, ident):
        return self.op("pe", lambda e: e.transpose(out.ap, in_.ap, ident.ap), [in_, ident], [out])

    def act(self, out, in_, func, bias=None, scale=None, accum=None, eng="act"):
        rd = [in_]
        kw = {}
        if bias is not None:
            if isinstance(bias, V):
                rd.append(bias)
                kw["bias"] = bias.ap
            else:
                kw["bias"] = float(bias)
        if scale is not None:
            if isinstance(scale, V):
                rd.append(scale)
                kw["scale"] = scale.ap
            else:
                kw["scale"] = float(scale)
        wr = [out]
        if accum is not None:
            wr.append(accum)
            kw["accum_out"] = accum.ap
        return self.op(eng, lambda e: e.activation(out=out.ap, in_=in_.ap, func=func, **kw), rd, wr)

    def tt(self, eng, out, in0, in1, op):
        return self.op(eng, lambda e: e.tensor_tensor(out=out.ap, in0=in0.ap, in1=in1.ap, op=op), [in0, in1], [out])

    def ts(self, eng, out, in0, s1, op0, s2=None, op1=None):
        rd = [in0]
        a1 = s1.ap if isinstance(s1, V) else float(s1)
        if isinstance(s1, V):
            rd.append(s1)
        kw = {}
        if op1 is not None:
            kw["op1"] = op1
            kw["scalar2"] = s2.ap if isinstance(s2, V) else float(s2)
            if isinstance(s2, V):
                rd.append(s2)
        else:
            kw["scalar2"] = None
        return self.op(eng, lambda e: e.tensor_scalar(out=out.ap, in0=in0.ap, scalar1=a1, op0=op0, **kw), rd, [out])

    def stt(self, out, in0, scalar, in1, op0, op1):
        rd = [in0, in1]
        a = scalar.ap if isinstance(scalar, V) else float(scalar)
        if isinstance(scalar, V):
            rd.append(scalar)
        return self.op("dve", lambda e: e.scalar_tensor_tensor(out=out.ap, in0=in0.ap, scalar=a, in1=in1.ap, op0=op0, op1=op1), rd, [out])

    def cp(self, eng, out, in_):
        if eng == "act":
            return self.op("act", lambda e: e.copy(out=out.ap, in_=in_.ap), [in_], [out])
        return self.op(eng, lambda e: e.tensor_copy(out=out.ap, in_=in_.ap), [in_], [out])

    def memset(self, eng, out, val):
        return self.op(eng, lambda e: e.memset(out.ap, val), [], [out])

    def recip(self, out, in_):
        return self.op("dve", lambda e: e.reciprocal(out=out.ap, in_=in_.ap), [in_], [out])

    def finalize(self):
        nc = self.nc
        ops = self.ops

        def pepe(a, b):
            return a.eng == "pe" and b.eng == "pe" and not a.dma and not b.dma

        for o in ops:
            for d in o.deps:
                if not pepe(ops[d], o):
                    ops[d].sig = True
            if o.dma:
                o.sig = True
        eng_sem = {e: nc.alloc_semaphore("s_" + e) for e in ENGS}
        dma_sems = {e: [nc.alloc_semaphore("d_%s_%d" % (e, i)) for i in range(DMA_POOL)] for e in ("sp", "act", "pool")}
        cnt = {e: 0 for e in ENGS}
        dcnt = {e: 0 for e in ENGS}
        for o in ops:
            if o.dma:
                n = dcnt[o.eng]
                dcnt[o.eng] += 1
                o.tok = (("d", o.eng, n % DMA_POOL), 16 * (n // DMA_POOL + 1))
                o.eidx = n
            elif o.sig:
                cnt[o.eng] += 1
                o.tok = (("e", o.eng), cnt[o.eng])
            else:
                o.tok = None

        def semof(k):
            return eng_sem[k[1]] if k[0] == "e" else dma_sems[k[1]][k[2]]

        streams = {e: [] for e in ENGS}
        known = {e: {} for e in ENGS}
        for o in ops:
            need = {}
            for d in o.deps:
                od = ops[d]
                if pepe(od, o):
                    continue
                k, v = od.tok
                if need.get(k, 0) < v:
                    need[k] = v
            if o.dma and o.eidx >= DMA_POOL:
                k = ("d", o.eng, o.eidx % DMA_POOL)
                v = 16 * (o.eidx // DMA_POOL)
                if need.get(k, 0) < v:
                    need[k] = v
            kn = known[o.eng]
            waits = []
            for k, v in need.items():
                if kn.get(k, 0) < v:
                    kn[k] = v
                    waits.append((semof(k), v))
            streams[o.eng].append((waits, o))
        final = []
        for e in ENGS:
            if cnt[e]:
                final.append((eng_sem[e], cnt[e]))
            n = dcnt[e]
            for i in range(min(n, DMA_POOL)):
                last = ((n - 1 - i) // DMA_POOL) * DMA_POOL + i
                final.append((dma_sems[e][i], 16 * (last // DMA_POOL + 1)))
        self.stats = {e: len(streams[e]) for e in ENGS}
        self.stats["sig"] = dict(cnt)
        self.stats["dma"] = dict(dcnt)

        def run(eng_obj, name):
            for waits, o in streams[name]:
                for s, v in waits:
                    eng_obj.wait_ge(s, v)
                ins = o.emit(eng_obj)
                if o.tok is not None:
                    ins.then_inc(semof(o.tok[0]), 16 if o.dma else 1)
            if name == "sp":
                for s, v in final:
                    eng_obj.wait_ge(s, v)

        with nc.Block() as block:
            @block.tensor
            def _(e):
                run(e, "pe")

            @block.scalar
            def _(e):
                run(e, "act")

            @block.vector
            def _(e):
                run(e, "dve")

            @block.gpsimd
            def _(e):
                run(e, "pool")

            @block.sync
            def _(e):
                run(e, "sp")


D = 1024
S = 4096
NT = 32
INW = 5408
FH = 2816
EPS = 1e-6
C_Z = 1536
C_BA = 2048
C_Q = 2056
C_KC = 2568
C_MA = 3360
NW1 = 3360


def _bf(a):
    return np.asarray(a, np.float32).astype(ml_dtypes.bfloat16)


def host_consts():
    p = np.arange(128)
    cf = {}
    cf["identf"] = np.eye(128, dtype=np.float32)
    cf["identfold"] = (p[:, None] % 64 == np.arange(64)[None, :]).astype(np.float32)
    cf["blockones"] = (p[:, None] // 64 == p[None, :] // 64).astype(np.float32)
    cf["ublk"] = ((p[:, None] // 64 == p[None, :] // 64) & (p[:, None] <= p[None, :])).astype(np.float32)
    cm = np.zeros((128, 2, 128), np.float32)
    for c in range(2):
        cm[c * 64:(c + 1) * 64, c, :] = 1.0
    cf["chunkmask"] = cm.reshape(128, 256)
    pl = (p % 64)[:, None]
    xx = np.arange(64)[None, :]
    mA = np.where(xx < pl, 0.0, NEG)
    mAT = np.where(xx > pl, 0.0, NEG)
    mQT = np.where(xx >= pl, 0.0, NEG)
    md = np.stack([np.repeat(m[:, None, :], 4, 1) for m in (mA, mAT, mQT)], 1)
    cf["maskD"] = md.reshape(128, 768).astype(np.float32)
    half = 8
    inv_freq = 500000.0 ** (-np.arange(half, dtype=np.float32) / half)
    pos = np.arange(S, dtype=np.float32)
    ang = pos[:, None] * inv_freq[None, :]
    cos = np.cos(ang).astype(np.float32).reshape(NT, 128, 8).transpose(1, 0, 2).reshape(128, NT * 8)
    sin = np.sin(ang).astype(np.float32).reshape(NT, 128, 8).transpose(1, 0, 2).reshape(128, NT * 8)
    cf["cosk"] = cos
    cf["sink"] = sin
    c = np.arange(128)[None, :]
    jr = c - 64
    cr = (p[:, None] >= 64).astype(np.int64)
    invalid = jr > cr
    forced = (jr == cr) | (jr == cr - 1)
    cf["keep"] = (~(invalid | forced)).astype(np.float32)
    cf["ovr"] = np.where(invalid, -1.0, np.where(forced, 1000.0, 0.0)).astype(np.float32)
    cb = {}
    cb["identb"] = np.eye(128, dtype=np.float32)
    fq = np.floor((p - 31) / 16.0).astype(np.int64)
    step = np.zeros((128, 768), np.float32)
    for vi in range(9):
        step[vi, :] = np.where((np.arange(768) - 384) > (vi - 2), NEG, 0.0)
    cb["step"] = step
    oh = np.zeros((128, 128), np.float32)
    for vi in range(9):
        oh[vi, :] = (fq == vi - 2)
    cb["onehot"] = np.tile(oh, (1, 4))
    be = np.zeros((128, 4096), np.float32)
    for j in range(64):
        be[j, j * 64:(j + 1) * 64] = 1.0
    cb["bigexp"] = be
    k = p[:, None]
    ql = p[None, :]
    cb["caus"] = np.tile(np.where(k > ql, NEG, 0.0), (1, 4))
    cb["winlo"] = np.tile(np.where(k <= ql, NEG, 0.0), (1, 4))
    cfo, cbo = {}, {}
    o = 0
    for kk, vv in cf.items():
        cfo[kk] = (o, vv.shape[1])
        o += vv.shape[1]
    ncf = o
    o = 0
    for kk, vv in cb.items():
        cbo[kk] = (o, vv.shape[1])
        o += vv.shape[1]
    ncb = o
    cfa = np.concatenate(list(cf.values()), 1).astype(np.float32)
    cba = _bf(np.concatenate(list(cb.values()), 1))
    return cfa, cba, cfo, cbo, ncf, ncb


_HC = host_consts()


def bc3(v, n):
    return v.w(lambda a: a.unsqueeze(2).broadcast_to([a.shape[0], a.shape[1], n]))


def bcm(v, m):
    return v.w(lambda a: a.unsqueeze(1).broadcast_to([a.shape[0], m, a.shape[1]]))


import os
STOP = int(os.environ.get('K_STOP', '0'))
NSTOP = int(os.environ.get('K_NSTOP', '0'))


class Ctx:
    pass


def setup_common(P, nseq, dbg):
    g = Ctx()
    ntok = nseq * S
    g.nseq = nseq
    g.ntok = ntok
    g.x = P.dram("x", [ntok, D], F32, kind="ExternalInput")
    g.w_in = P.dram("w_in", [D, INW], F32, kind="ExternalInput")
    g.w_gu = P.dram("w_gu", [D, 2 * FH], F32, kind="ExternalInput")
    g.w_down = P.dram("w_down", [FH, D], F32, kind="ExternalInput")
    g.w_a = P.dram("w_a", [512, D], F32, kind="ExternalInput")
    g.w_b = P.dram("w_b", [512, D], F32, kind="ExternalInput")
    g.w_out = P.dram("w_out", [D, D], F32, kind="ExternalInput")
    g.w1k = P.dram("w1k", [2048, 128], F32, kind="ExternalInput")
    g.w1v = P.dram("w1v", [2048, 128], F32, kind="ExternalInput")
    g.w2k = P.dram("w2k", [128, 64], F32, kind="ExternalInput")
    g.w2v = P.dram("w2v", [128, 64], F32, kind="ExternalInput")
    g.posk = P.dram("posk", [64, 32], F32, kind="ExternalInput")
    g.posv = P.dram("posv", [64, 32], F32, kind="ExternalInput")
    g.gain1 = P.dram("gain1", [1, D], F32, kind="ExternalInput")
    g.gain2 = P.dram("gain2", [1, D], F32, kind="ExternalInput")
    g.gain3 = P.dram("gain3", [1, D], F32, kind="ExternalInput")
    g.onorm = P.dram("onorm", [1, 128], F32, kind="ExternalInput")
    g.alog = P.dram("alog", [1, 4], F32, kind="ExternalInput")
    g.dtb = P.dram("dtb", [1, 4], F32, kind="ExternalInput")
    g.convw = P.dram("convw", [128, 48], F32, kind="ExternalInput")
    g.cf = P.dram("cf", [128, _HC[4]], F32, kind="ExternalInput")
    g.cb = P.dram("cb", [128, _HC[5]], BF16, kind="ExternalInput")
    g.out = P.dram("out", [ntok, D], F32, kind="ExternalOutput")
    ntile = ntok // 128
    g.xnT_d = P.dram("xnT_d", [ntile * 128, D], BF16)
    g.oaT_d = P.dram("oaT_d", [ntile * 128, 512], BF16)
    g.obT_d = P.dram("obT_d", [ntile * 128, 512], BF16)
    g.x1_d = P.dram("x1_d", [ntok, D], F32)
    g.dbg = {}
    for name, shape in dbg.items():
        g.dbg[name] = P.dram("dbg_" + name, list(shape), F32, kind="ExternalOutput")
    return g


def load_consts(P, g):
    cfo, cbo = _HC[2], _HC[3]
    cft = P.sb([128, _HC[4]], F32)
    cbt = P.sb([128, _HC[5]], BF16)
    P.dma("sp", cft.all(), g.cf.all())
    P.dma("sp", cbt.all(), g.cb.all())
    g.cft, g.cbt = cft, cbt

    def CF(name, *shape):
        o, n = cfo[name]
        t = T("sb", cft.ap[:, o:o + n], [128, n], 4, cft.boff + o * 4)
        if shape:
            names = " ".join("d%d" % i for i in range(len(shape)))
            t = T("sb", t.ap.rearrange("p (%s) -> p %s" % (names, names), **{"d%d" % i: s for i, s in enumerate(shape)}),
                  [128] + list(shape), 4, cft.boff + o * 4)
        return t

    def CB(name):
        o, n = cbo[name]
        return T("sb", cbt.ap[:, o:o + n], [128, n], 2, cbt.boff + o * 2)

    g.CF, g.CB = CF, CB


def cast_load(P, dst, src_t, r0, c0, ncols, kchunks, step=1024):
    for k in range(kchunks):
        for cc in range(0, ncols, step):
            n = min(step, ncols - cc)
            P.dma("pool", dst[:, k, cc:cc + n], src_t[r0 + k * 128:r0 + (k + 1) * 128, c0 + cc:c0 + cc + n])


def rstd_of(P, out, ssq, n, tmp):
    P.ts("dve", tmp, ssq, 1.0 / n, ALU.mult, EPS, ALU.add)
    P.act(tmp, tmp, AF.Ln)
    P.act(out, tmp, AF.Exp, scale=-0.5)


def norm_transpose(P, g, xt, gainB, xn, xnT, junk, sm, identb, bank):
    P.act(xn.all(), xt.all(), AF.Square, accum=sm[:, 0:1])
    rstd_of(P, sm[:, 1:2], sm[:, 0:1], D, sm[:, 2:3])
    P.stt(xn.all(), xt.all(), sm[:, 1:2], gainB.all(), ALU.mult, ALU.mult)
    pT = P.ps(bank, [128, 8, 128], BF16)
    for k in range(8):
        P.tr(pT[:, k, :], xn[:, k * 128:(k + 1) * 128], identb.all())
    P.cp("act", xnT.all(), pT.all())


def phase1(P, g, ntiles, do_nsa=True):
    P.mark()
    CF, CB = g.CF, g.CB
    identb = CB("identb")
    identfold = CF("identfold")
    blockones = CF("blockones")
    ublk = CF("ublk")
    chunkmask = CF("chunkmask", 2, 128)
    maskD = CF("maskD", 3, 4, 64)
    win = P.sb([128, 8, NW1], BF16)
    cast_load(P, win, g.w_in, 0, 0, NW1, 8, step=840)
    gainB = P.sb([128, D], F32)
    P.dma("sp", gainB.all(), g.gain1[0:1, :].bc([128, D]))
    ognB = P.sb([128, 128], F32)
    P.dma("sp", ognB.all(), g.onorm[0:1, :].bc([128, 128]))
    sc0 = P.sb([128, 16], F32)
    P.dma("sp", sc0[:, 0:4], g.alog[0:1, :].bc([128, 4]))
    P.dma("sp", sc0[:, 4:8], g.dtb[0:1, :].bc([128, 4]))
    P.act(sc0[:, 8:12], sc0[:, 0:4], AF.Exp)
    P.ts("dve", sc0[:, 8:12], sc0[:, 8:12], -1.0, ALU.mult)
    nAB = sc0[:, 8:12]
    dtbB = sc0[:, 4:8]
    convw = P.sb([128, 48], F32)
    P.dma("sp", convw.all(), g.convw.all())
    cdiag = P.sb([128, 48, 128], BF16)
    for q in range(48):
        P.ts("dve", cdiag[:, q, :], identb.all(), convw[:, q:q + 1], ALU.mult)
    xt = P.sb([128, D], F32)
    xn = P.sb([128, D], BF16)
    junk = P.sb([128, 128], BF16)
    xnT = P.sb([128, 8, 128], BF16)
    sm = P.sb([128, 8], F32)
    cbuf = P.sb([128, 12, 131], BF16)
    cs = P.sb([128, 12, 128], BF16)
    ssq = P.sb([128, 8], F32)
    rn = P.sb([128, 8], F32)
    sc = P.sb([128, 64], F32)
    k_n = P.sb([128, 4, 128], BF16)
    kbg = P.sb([128, 4, 128], BF16)
    kd = P.sb([128, 4, 128], BF16)
    vb = P.sb([128, 4, 128], BF16)
    q_n = P.sb([128, 4, 128], BF16)
    kqT = P.sb([128, 8, 128], BF16)
    dgt = P.sb([128, 2, 4, 64], F32)
    DD = P.sb([128, 3, 4, 64], F32)
    EE = DD
    ATf = P.sb([128, 4, 64], F32)
    TTf = P.sb([128, 4, 64], F32)
    TTb = P.sb([128, 4, 64], BF16)
    Pb = [P.sb([128, 2, 4, 64], BF16) for _ in range(2)]
    QKm = P.sb([128, 4, 64], BF16)
    uS = P.sb([128, 4, 128], F32)
    wT = P.sb([128, 2, 4, 64], BF16)
    Sf = P.sb([128, 4, 128], F32)
    Sb = P.sb([128, 4, 128], BF16)
    vn = P.sb([128, 4, 128], BF16)
    tmpo = P.sb([128, 4, 128], F32)
    o_raw = P.sb([128, 4, 128], F32)
    zs = P.sb([128, 512], F32)
    oa = P.sb([128, 4, 128], BF16)
    oaT = P.sb([128, 4, 128], BF16)
    g1 = Ctx()
    if do_nsa:
        nsa_setup(P, g, g1)

    bk = [2]

    def nb():
        b = bk[0]
        bk[0] = 2 + (bk[0] - 2 + 1) % 6
        return b

    for gt in range(ntiles):
        b, i = divmod(gt, NT)
        r0 = gt * 128
        if i == 0:
            P.memset("pool", cbuf[:, :, 0:3], 0.0)
            P.memset("pool", Sf.all(), 0.0)
            P.memset("pool", Sb.all(), 0.0)
        P.dma("sp", xt.all(), g.x[r0:r0 + 128, :])
        norm_transpose(P, g, xt, gainB, xn, xnT, junk, sm, identb, nb())
        P.dma("sp", g.xnT_d[r0:r0 + 128, :], xnT.all().w(lambda a: a.rearrange("p a b -> p (a b)")))
        if STOP == 1:
            continue
        for fb in range(3):
            pq = P.ps(nb(), [128, 4, 128])
            for ff in range(4):
                f = fb * 4 + ff
                for k in range(8):
                    P.mm(pq[:, ff, :], win[:, k, f * 128:(f + 1) * 128], xnT[:, k, :], start=(k == 0), stop=(k == 7))
            P.cp("act", cbuf[:, fb * 4:(fb + 1) * 4, 3:131], pq.all())
        for fb in range(3):
            pc = P.ps(nb(), [128, 4, 128])
            for ff in range(4):
                f = fb * 4 + ff
                for j in range(4):
                    P.mm(pc[:, ff, :], cdiag[:, f * 4 + j, :], cbuf[:, f, j:j + 128], start=(j == 0), stop=(j == 3))
            P.act(cs[:, fb * 4:(fb + 1) * 4, :], pc.all(), AF.Silu)
        P.cp("pool", cbuf[:, :, 0:3], cbuf[:, :, 128:131])
        if STOP == 2:
            continue
        psA = P.ps(nb(), [128, 8, 128], BF16)
        psB = P.ps(nb(), [128, 4, 128], BF16)
        for f in range(8):
            P.tr(psA[:, f, :], cs[:, f, :], identb.all())
        for f in range(4):
            P.tr(psB[:, f, :], cs[:, 8 + f, :], identb.all())
        for f in range(8):
            P.act(junk[:, 0:128], psA[:, f, :], AF.Square, accum=ssq[:, f:f + 1])
        P.ts("dve", rn.all(), ssq.all(), EPS, ALU.add)
        P.act(rn.all(), rn.all(), AF.Ln)
        P.act(rn.all(), rn.all(), AF.Exp, scale=-0.5)
        if STOP == 3:
            continue
        pba = P.ps(nb(), [128, 8])
        for k in range(8):
            P.mm(pba.all(), xnT[:, k, :], win[:, k, C_BA:C_BA + 8], start=(k == 0), stop=(k == 7))
        P.act(sc[:, 52:56], pba[:, 0:4], AF.Exp, scale=-1.0)
        P.act(sc[:, 0:4], sc[:, 52:56], AF.Ln, bias=1.0)
        P.act(sc[:, 4:8], sc[:, 0:4], AF.Exp, scale=-1.0)
        P.tt("dve", sc[:, 8:12], pba[:, 4:8], dtbB, ALU.add)
        P.act(sc[:, 8:12], sc[:, 8:12], AF.Exp)
        P.act(sc[:, 8:12], sc[:, 8:12], AF.Ln, bias=1.0)
        P.tt("dve", sc[:, 8:12], sc[:, 8:12], nAB, ALU.mult)
        pg = P.ps(nb(), [128, 16])
        P.mm(pg[:, 0:4], ublk.all(), sc[:, 8:12])
        P.mm(pg[:, 4:8], blockones.all(), sc[:, 8:12])
        for c in range(2):
            P.mm(pg[:, 8 + 4 * c:12 + 4 * c], chunkmask[:, c, :], sc[:, 8:12])
        P.cp("act", sc[:, 12:20], pg[:, 0:8])
        P.act(sc[:, 28:36], pg[:, 8:16], AF.Exp)
        P.act(sc[:, 20:24], sc[:, 12:16], AF.Exp)
        P.tt("dve", sc[:, 24:28], sc[:, 16:20], sc[:, 12:16], ALU.subtract)
        P.act(sc[:, 24:28], sc[:, 24:28], AF.Exp)
        P.tt("dve", sc[:, 36:40], sc[:, 12:16], sc[:, 0:4], ALU.subtract)
        P.tt("dve", sc[:, 40:44], rn[:, 4:8], sc[:, 4:8], ALU.mult)
        P.tt("dve", sc[:, 40:44], sc[:, 40:44], sc[:, 20:24], ALU.mult)
        P.tt("dve", sc[:, 44:48], rn[:, 4:8], sc[:, 24:28], ALU.mult)
        P.ts("dve", sc[:, 48:52], rn[:, 0:4], 128.0 ** -0.5, ALU.mult)
        if STOP == 4:
            continue
        P.tt("dve", k_n.all(), psA[:, 4:8, :], bc3(rn[:, 4:8], 128), ALU.mult)
        P.tt("dve", kbg.all(), psA[:, 4:8, :], bc3(sc[:, 40:44], 128), ALU.mult)
        P.tt("dve", kd.all(), psA[:, 4:8, :], bc3(sc[:, 44:48], 128), ALU.mult)
        P.tt("dve", q_n.all(), psA[:, 0:4, :], bc3(sc[:, 48:52], 128), ALU.mult)
        P.tt("dve", vb.all(), psB.all(), bc3(sc[:, 4:8], 128), ALU.mult)
        pkq = P.ps(nb(), [128, 8, 128], BF16)
        for h in range(4):
            P.tr(pkq[:, h, :], k_n[:, h, :], identb.all())
            P.tr(pkq[:, 4 + h, :], q_n[:, h, :], identb.all())
        P.cp("act", kqT.all(), pkq.all())
        if STOP == 5:
            continue
        pKQ = P.ps(nb(), [128, 2, 4, 64])
        for c in range(2):
            cs_ = slice(c * 64, (c + 1) * 64)
            for h in range(4):
                P.mm(pKQ[cs_, 0, h, :], kqT[:, h, cs_], kqT[:, h, cs_])
                P.mm(pKQ[cs_, 1, h, :], kqT[:, h, cs_], kqT[:, 4 + h, cs_])
        P.tt("pool", dgt[:, 0], bcm(identfold.all(), 4), bc3(sc[:, 12:16], 64), ALU.mult)
        P.tt("pool", dgt[:, 1], bcm(identfold.all(), 4), bc3(sc[:, 36:40], 64), ALU.mult)
        pBf = P.ps(nb(), [128, 2, 4, 64])
        P.mm(pBf.all().w(lambda a: a.rearrange("p a b c -> p (a b c)")), blockones.all(),
             dgt.all().w(lambda a: a.rearrange("p a b c -> p (a b c)")))
        P.tt("dve", DD[:, 0], bc3(sc[:, 36:40], 64), pBf[:, 0], ALU.subtract)
        P.tt("dve", DD[:, 1], pBf[:, 1], bc3(sc[:, 12:16], 64), ALU.subtract)
        P.tt("dve", DD[:, 2], pBf[:, 0], bc3(sc[:, 12:16], 64), ALU.subtract)
        P.tt("pool", DD.all(), DD.all(), maskD.all(), ALU.add)
        P.act(EE.all(), DD.all(), AF.Exp)
        if STOP == 6:
            continue
        P.tt("dve", Pb[0][:, 0], pKQ[:, 0], EE[:, 0], ALU.mult)
        P.tt("dve", ATf.all(), pKQ[:, 0], EE[:, 1], ALU.mult)
        P.tt("dve", QKm.all(), pKQ[:, 1], EE[:, 2], ALU.mult)
        P.cp("pool", Pb[0][:, 1], ATf.all())
        P.tt("pool", TTf.all(), bcm(identfold.all(), 4), ATf.all(), ALU.subtract)
        P.cp("pool", TTb.all(), TTf.all())
        cur = 0
        if STOP == 7:
            continue
        NL = 5
        for lvl in range(NL):
            last = lvl == NL - 1
            pN = P.ps(nb(), [128, 2, 4, 64])
            Pc, Pn = Pb[cur], Pb[1 - cur]
            for c in range(2):
                cs_ = slice(c * 64, (c + 1) * 64)
                for h in range(4):
                    P.mm(pN[cs_, 0, h, :], Pc[cs_, 1, h, :], Pc[cs_, 0, h, :])
                    if not last:
                        P.mm(pN[cs_, 1, h, :], Pc[cs_, 0, h, :], Pc[cs_, 1, h, :])
            if last:
                P.cp("act", Pn[:, 0], pN[:, 0])
            else:
                P.cp("act", Pn.all(), pN.all())
            pT_ = P.ps(nb(), [128, 4, 64])
            for c in range(2):
                cs_ = slice(c * 64, (c + 1) * 64)
                for h in range(4):
                    P.mm(pT_[cs_, h, :], Pn[cs_, 0, h, :], TTb[cs_, h, :])
            P.tt("dve", TTf.all(), TTf.all(), pT_.all(), ALU.add)
            P.cp("pool", TTb.all(), TTf.all())
            cur = 1 - cur
        pU = P.ps(nb(), [128, 4, 128])
        pW = P.ps(nb(), [128, 2, 4, 64])
        for c in range(2):
            cs_ = slice(c * 64, (c + 1) * 64)
            for h in range(4):
                P.mm(pU[cs_, h, :], TTb[cs_, h, :], vb[cs_, h, :])
                P.mm(pW[:, c, h, :], kbg[cs_, h, :], TTb[cs_, h, :])
        P.cp("act", uS.all(), pU.all())
        P.cp("act", wT.all(), pW.all())
        if STOP == 8:
            continue
        for c in range(2):
            cs_ = slice(c * 64, (c + 1) * 64)
            pWS = P.ps(nb(), [128, 4, 128])
            pQS = P.ps(nb(), [128, 4, 128])
            for h in range(4):
                P.mm(pWS[cs_, h, :], wT[:, c, h, :], Sb[:, h, :])
                P.mm(pQS[cs_, h, :], kqT[:, 4 + h, cs_], Sb[:, h, :])
            P.tt("dve", vn[cs_], uS[cs_], pWS[cs_], ALU.subtract)
            pO = P.ps(nb(), [128, 4, 128])
            pS = P.ps(nb(), [128, 4, 128])
            for h in range(4):
                P.mm(pO[cs_, h, :], QKm[cs_, h, :], vn[cs_, h, :])
                P.mm(pS[:, h, :], kd[cs_, h, :], vn[cs_, h, :])
            P.tt("dve", tmpo[cs_], pQS[cs_], bc3(sc[cs_, 20:24], 128), ALU.mult)
            P.tt("dve", o_raw[cs_], tmpo[cs_], pO[cs_], ALU.add)
            P.tt("pool", Sf.all(), Sf.all(), bc3(sc[:, 28 + 4 * c:32 + 4 * c], 128), ALU.mult)
            P.tt("dve", Sf.all(), Sf.all(), pS.all(), ALU.add)
            P.cp("pool", Sb.all(), Sf.all())
        pz = P.ps(nb(), [128, 512])
        for k in range(8):
            P.mm(pz.all(), xnT[:, k, :], win[:, k, C_Z:C_Z + 512], start=(k == 0), stop=(k == 7))
        P.act(zs.all(), pz.all(), AF.Silu)
        for h in range(4):
            P.act(junk[:, 0:128], o_raw[:, h, :], AF.Square, accum=sc[:, 52 + h:53 + h])
        rstd_of(P, sc[:, 56:60], sc[:, 52:56], 128, sc[:, 60:64])
        P.tt("dve", tmpo.all(), o_raw.all(), bc3(sc[:, 56:60], 128), ALU.mult)
        P.tt("pool", tmpo.all(), tmpo.all(), bcm(ognB.all(), 4), ALU.mult)
        P.tt("dve", oa.all().w(lambda a: a.rearrange("p a b -> p (a b)")), tmpo.all().w(lambda a: a.rearrange("p a b -> p (a b)")), zs.all(), ALU.mult)
        if "o_a" in g.dbg:
            P.tt("pool", tmpo.all().w(lambda a: a.rearrange("p a b -> p (a b)")), tmpo.all().w(lambda a: a.rearrange("p a b -> p (a b)")), zs.all(), ALU.mult)
            P.dma("sp", g.dbg["o_a"][r0:r0 + 128, :], tmpo.all().w(lambda a: a.rearrange("p a b -> p (a b)")))
        if "o_raw" in g.dbg:
            P.dma("sp", g.dbg["o_raw"][r0:r0 + 128, :], o_raw.all().w(lambda a: a.rearrange("p a b -> p (a b)")))
        pOT = P.ps(nb(), [128, 4, 128], BF16)
        for h in range(4):
            P.tr(pOT[:, h, :], oa[:, h, :], identb.all())
        P.cp("act", oaT.all(), pOT.all())
        P.dma("sp", g.oaT_d[r0:r0 + 128, :], oaT.all().w(lambda a: a.rearrange("p a b -> p (a b)")))
        if do_nsa:
            nsa_tile(P, g, g1, gt, xnT, win, nb)
    P.release()


def shared_inputs(inp):
    f = lambda a: np.ascontiguousarray(np.asarray(a, np.float32))
    cw = np.asarray(inp["gdn_conv_w"], np.float32)[0]
    convw = cw.reshape(4, 12, 128).transpose(2, 1, 0).reshape(128, 48)
    d = {
        "w_in": f(inp["w_in"][0]), "w_gu": f(inp["w_gate_up"][0]), "w_down": f(inp["w_down"][0]),
        "w_a": f(inp["w_branch_gdn"][0]), "w_b": f(inp["w_branch_nsa"][0]), "w_out": f(inp["w_out"][0]),
        "w1k": f(np.asarray(inp["cmp_w1_k"])[0].reshape(2048, 128)), "w1v": f(np.asarray(inp["cmp_w1_v"])[0].reshape(2048, 128)),
        "w2k": f(inp["cmp_w2_k"][0]), "w2v": f(inp["cmp_w2_v"][0]),
        "posk": f(np.asarray(inp["cmp_pos_k"])[0].T), "posv": f(np.asarray(inp["cmp_pos_v"])[0].T),
        "gain1": f(inp["mix_norm_gain"]).reshape(1, D), "gain2": f(inp["ffn_norm_gain"]).reshape(1, D),
        "gain3": f(inp["final_norm_gain"]).reshape(1, D), "onorm": f(inp["gdn_out_norm_gain"]).reshape(1, 128),
        "alog": f(inp["gdn_a_log"]).reshape(1, 4), "dtb": f(inp["gdn_dt_bias"]).reshape(1, 4),
        "convw": f(convw), "cf": _HC[0], "cb": _HC[1],
    }
    return d


def nsa_setup(P, g, n):
    CF, CB = g.CF, g.CB
    n.identb = CB("identb")
    n.step = CB("step")
    n.onehot = CB("onehot")
    n.bigexp = CB("bigexp")
    n.caus = CB("caus")
    n.winlo = CB("winlo")
    n.keep = CF("keep")
    n.ovr = CF("ovr")
    n.cosk = CF("cosk", NT, 8)
    n.sink = CF("sink", NT, 8)
    n.w1 = []
    n.w2 = []
    n.bias = []
    pb = P.ps(7, [128, 2])
    for kv, (w1d, w2d, posd, nm) in enumerate(((g.w1k, g.w2k, g.posk, "w1k"), (g.w1v, g.w2v, g.posv, "w1v"))):
        w1 = P.sb([128, 32, 128], BF16)
        P.dma("pool", w1[0:64], V(w1d.ap.rearrange("(l d) h -> d l h", d=64), (nm, 0, 2048, 0, 128)))
        w2 = P.sb([128, 64], BF16)
        P.dma("pool", w2.all(), w2d.all())
        pt = P.sb([128, 32], BF16)
        P.dma("pool", pt[0:64], posd.all())
        for l in range(32):
            P.mm(pb[:, kv:kv + 1], w1[0:64, l, :], pt[0:64, l:l + 1], start=(l == 0), stop=(l == 31))
        n.w1.append(w1)
        n.w2.append(w2)
    bias = P.sb([128, 2], F32)
    P.cp("act", bias.all(), pb.all())
    n.biasv = bias
    n.ksT = P.sb([128, S], BF16)
    n.qTz = [P.sb([128, 4, 128], BF16) for _ in range(2)]
    P.memset("pool", n.qTz[0].all(), 0.0)
    P.memset("pool", n.qTz[1].all(), 0.0)
    n.vsE = P.sb([128, NT, 2, 65], BF16)
    n.kwT = P.sb([128, 2, 5, 128], BF16)
    n.vwE = P.sb([128, 5, 2, 65], BF16)
    n.kcr = P.sb([128, 2, 144], BF16)
    n.vcr = P.sb([128, 2, 144], BF16)
    n.kcT = P.sb([128, 2, 256], BF16)
    n.vcT = P.sb([128, 2, 256], BF16)
    n.vcE = P.sb([128, 2, 2, 65], BF16)
    P.memset("pool", n.vsE.all(), 1.0)
    P.memset("pool", n.vwE.all(), 1.0)
    P.memset("pool", n.vcE.all(), 1.0)
    n.q_r = P.sb([128, 8, 64], BF16)
    n.kr1 = P.sb([128, 8, 64], BF16)
    n.kr2 = P.sb([128, 2, 64], BF16)
    n.vraw = P.sb([128, 2, 64], BF16)
    n.rt = P.sb([128, 4, 8, 8], F32)
    n.gsig = P.sb([128, 2, 4, 3], F32)
    n.qT = P.sb([128, 8, 128], BF16)
    n.hid = P.sb([128, 2, 2, 8], BF16)
    n.esc = P.sb([128, 4, 256], F32)
    n.den = P.sb([128, 16], F32)
    n.psp = P.sb([128, 260], F32)
    n.imp = P.sb([128, 64], F32)
    n.score = P.sb([128, 64], F32)
    n.score2 = P.sb([128, 64], F32)
    n.m8 = P.sb([128, 16], F32)
    n.negsel = P.sb([128, 64], BF16)
    n.nselT4 = P.sb([128, 4, 128], BF16)
    n.eT = [P.sb([128, 512], BF16) for _ in range(2)]
    n.ob_f = P.sb([128, 8, 64], F32)
    n.obtmp = P.sb([128, 4, 64], F32)
    n.ob = P.sb([128, 512], BF16)
    n.obT = P.sb([128, 4, 128], BF16)
    n.ei = 0


def rope(P, n, src, dst, H, cos, sin, rest_scale):
    rt = n.rt
    c = bcm(cos, H)
    s = bcm(sin, H)
    P.tt("dve", rt[:, 0, 0:H, :], src[:, :, 0:8], c, ALU.mult)
    P.tt("dve", rt[:, 1, 0:H, :], src[:, :, 8:16], s, ALU.mult)
    P.tt("dve", rt[:, 2, 0:H, :], src[:, :, 8:16], c, ALU.mult)
    P.tt("dve", rt[:, 3, 0:H, :], src[:, :, 0:8], s, ALU.mult)
    P.tt("pool", dst[:, :, 0:8], rt[:, 0, 0:H, :], rt[:, 1, 0:H, :], ALU.subtract)
    P.tt("pool", dst[:, :, 8:16], rt[:, 2, 0:H, :], rt[:, 3, 0:H, :], ALU.add)
    P.act(dst[:, :, 16:64], src[:, :, 16:64], AF.Copy, scale=rest_scale)


def nsa_tile(P, g, n, gt, xnT, win, nb):
    b, i = divmod(gt, NT)
    r0 = gt * 128
    identb = n.identb
    slot = i % 5
    if i == 0:
        P.memset("pool", n.kcT.all(), 0.0)
        P.memset("pool", n.vcT.all(), 0.0)
        P.memset("pool", n.vcE[:, :, :, 0:64], 0.0)
        P.memset("pool", n.kcr.all(), 0.0)
        P.memset("pool", n.vcr.all(), 0.0)
    pn = []
    for (c0, w) in ((C_Q, 512), (C_KC, 512), (C_KC + 512, 280)):
        pt = P.ps(nb(), [128, 512])
        for k in range(8):
            P.mm(pt[:, 0:w], xnT[:, k, :], win[:, k, c0:c0 + w], start=(k == 0), stop=(k == 7))
        pn.append(pt)

    def v3(t, c0, H):
        return T("ps", t.ap[:, c0:c0 + H * 64].rearrange("p (h d) -> p h d", d=64), [128, H, 64], 4, t.boff + c0 * 4)

    rope(P, n, v3(pn[0], 0, 8), n.q_r, 8, n.cosk[:, i, :], n.sink[:, i, :], 1.0)
    rope(P, n, v3(pn[1], 0, 8), n.kr1, 8, n.cosk[:, i, :], n.sink[:, i, :], 1.0)
    rope(P, n, v3(pn[2], 0, 2), n.kr2, 2, n.cosk[:, i, :], n.sink[:, i, :], 1.0)
    P.cp("act", n.vraw.all(), v3(pn[1], 128, 2).all())
    P.cp("act", n.vsE[:, i, :, 0:64], v3(pn[1], 384, 2).all())
    P.cp("act", n.vwE[:, slot, :, 0:64], v3(pn[2], 128, 2).all())
    gs = n.gsig.all().w(lambda a: a.rearrange("p a b c -> p (a b c)"))
    P.act(gs, pn[2][:, 256:280], AF.Exp, scale=-1.0)
    P.ts("dve", gs, gs, 1.0, ALU.add)
    P.recip(gs, gs)
    if NSTOP == 1:
        return
    pqT = P.ps(nb(), [128, 8, 128], BF16)
    for h in range(8):
        P.tr(pqT[0:64, h, :], n.q_r[:, h, :], identb.all())
    P.act(n.qT[0:64], pqT[0:64], AF.Copy, scale=0.125)
    pq2 = P.ps(nb(), [128, 4, 128], BF16)
    for r in range(4):
        P.tr(pq2[0:64, r, :], n.q_r[:, r, :], identb.all())
        P.tr(pq2[64:128, r, :], n.q_r[:, 4 + r, :], identb.all())
    P.act(n.qTz[0][0:64], pq2[0:64], AF.Copy, scale=0.125)
    P.act(n.qTz[1][64:128], pq2[64:128], AF.Copy, scale=0.125)
    pks = P.ps(nb(), [128, 128], BF16)
    P.tr(pks[0:64, :], n.kr1[:, 4, :], identb.all())
    P.tr(pks[64:128, :], n.kr1[:, 5, :], identb.all())
    P.cp("dve", n.ksT[:, i * 128:(i + 1) * 128], pks.all())
    pkT = P.ps(nb(), [128, 8, 128], BF16)
    srcs = [n.kr1[:, 0, :], n.kr1[:, 1, :], n.vraw[:, 0, :], n.vraw[:, 1, :], n.kr1[:, 4, :], n.kr1[:, 5, :], n.kr2[:, 0, :], n.kr2[:, 1, :]]
    for h in range(8):
        P.tr(pkT[0:64, h, :], srcs[h], identb.all())
    P.cp("dve", n.kcr[0:64, :, 16:144], pkT[0:64, 0:2, :])
    P.cp("dve", n.vcr[0:64, :, 16:144], pkT[0:64, 2:4, :])
    P.cp("act", n.kwT[0:64, :, slot, :], pkT[0:64, 6:8, :])
    if NSTOP == 2:
        return
    n0 = 8 * i - 1 if i > 0 else 0
    nn = 8 if i > 0 else 7
    m0 = 0 if i > 0 else 1
    ph = P.ps(nb(), [128, 2, 2, 8])
    for kv, cr in enumerate((n.kcr, n.vcr)):
        for l in range(32):
            P.mm(ph[:, kv, :, :], n.w1[kv][0:64, l, :], cr[0:64, :, l:l + 113:16], start=(l == 0), stop=(l == 31))
        P.act(n.hid[:, kv, :, :], ph[:, kv, :, :], AF.Silu, bias=n.biasv[:, kv:kv + 1])
    pk2 = P.ps(nb(), [128, 2, 2, 8])
    for kv in range(2):
        P.mm(pk2[0:64, kv, :, :], n.w2[kv].all(), n.hid[:, kv, :, :])
    P.cp("act", n.kcT[0:64, :, n0:n0 + nn], pk2[0:64, 0, :, m0:8])
    P.cp("act", n.vcT[0:64, :, n0:n0 + nn], pk2[0:64, 1, :, m0:8])
    P.cp("pool", n.kcr[0:64, :, 0:16], n.kcr[0:64, :, 128:144])
    P.cp("pool", n.vcr[0:64, :, 0:16], n.vcr[0:64, :, 128:144])
    for ch in sorted(set((n0 // 128, (n0 + nn - 1) // 128))):
        pvc = P.ps(nb(), [128, 2, 64], BF16)
        for gq in range(2):
            P.tr(pvc[:, gq, :], n.vcT[0:64, gq, ch * 128:(ch + 1) * 128], identb[0:64, 0:64])
        P.cp("act", n.vcE[:, ch, :, 0:64], pvc.all())
        if NSTOP == 3:
            pass
    if NSTOP == 3:
        return
    nch = 1 if 8 * i + 7 <= 128 else 2
    NC = nch * 128
    sh = 384 - 8 * i
    for gq in range(2):
        qT4 = n.qT[0:64, gq * 4:(gq + 1) * 4, :].w(lambda a: a.rearrange("p a b -> p (a b)"))
        if nch == 1:
            bankA = nb()
            regs = [P.ps(bankA, [128, 4, 128])[:, r, :] for r in range(4)]
        else:
            bankA, bankB = nb(), nb()
            regs = [P.ps(bankA if r < 2 else bankB, [128, 2, 256])[:, r % 2, :] for r in range(4)]
        for r in range(4):
            P.mm(regs[r], n.qT[0:64, gq * 4 + r, :], n.kcT[0:64, gq, 0:NC], start=True, stop=False)
            P.mm(regs[r], n.onehot[0:9, 0:128], n.step[0:9, sh:sh + NC], start=False, stop=True)
        for r in range(4):
            P.act(n.esc[:, r, 0:NC], regs[r], AF.Exp, accum=n.den[:, r:r + 1])
            if NSTOP == 4:
                pass
        if NSTOP == 4:
            return
        P.ts("dve", n.den[:, 4:8], n.den[:, 0:4], 1e-30, ALU.max)
        P.recip(n.den[:, 4:8], n.den[:, 4:8])
        P.memset("pool", n.psp.all(), 0.0)
        P.ts("dve", n.psp[:, 1:1 + NC], n.esc[:, 0, 0:NC], n.den[:, 4:5], ALU.mult)
        for r in range(1, 4):
            P.stt(n.psp[:, 1:1 + NC], n.esc[:, r, 0:NC], n.den[:, 4 + r:5 + r], n.psp[:, 1:1 + NC], ALU.mult, ALU.add)
        P.tt("dve", n.imp.all(), n.psp[:, 0:256:4], n.psp[:, 1:257:4], ALU.add)
        for m in range(2, 5):
            P.tt("dve", n.imp.all(), n.imp.all(), n.psp[:, m:m + 256:4], ALU.add)
        P.tt("dve", n.score.all(), n.imp.all(), n.keep[:, 64 - 2 * i:128 - 2 * i], ALU.mult)
        P.tt("dve", n.score.all(), n.score.all(), n.ovr[:, 64 - 2 * i:128 - 2 * i], ALU.add)
        P.memset("dve", n.score[:, 0:1], 1000.0)
        if NSTOP == 5:
            return
        sc_, sc2, m8 = n.score.all(), n.score2.all(), n.m8
        m8a, m8b = m8[:, 0:8], m8[:, 8:16]
        P.op("dve", lambda e, o=m8a, s=sc_: e.max(out=o.ap, in_=s.ap), [sc_], [m8a])
        P.op("dve", lambda e, o=sc2, a=m8a, s=sc_: e.match_replace(out=o.ap, in_to_replace=a.ap, in_values=s.ap, imm_value=-2.0), [m8a, sc_], [sc2])
        P.op("dve", lambda e, o=m8b, s=sc2: e.max(out=o.ap, in_=s.ap), [sc2], [m8b])
        P.ts("dve", n.negsel.all(), n.score.all(), m8[:, 15:16], ALU.is_lt, NEG, ALU.mult)
        if NSTOP == 6:
            return
        pns = P.ps(nb(), [128, 128], BF16)
        P.tr(pns[0:64, :], n.negsel.all(), identb.all())
        P.cp("act", n.nselT4[0:64], bcm(pns[0:64, :], 4))
        nsel = n.nselT4[0:64].w(lambda a: a.rearrange("p a b -> p (a b)"))
        if NSTOP == 7:
            return

        def branch(chunks, kfn, vfn, maskfn, br, qop=None):
            qop = qT4 if qop is None else qop
            po = P.ps(gq, [128, 4, 65])
            first = True
            for ci, kc_ in enumerate(chunks):
                lastc = ci == len(chunks) - 1
                pst = P.ps(nb(), [128, 512])
                masks = maskfn(kc_)
                P.mm(pst.all(), kfn(kc_), qop, start=True, stop=(len(masks) == 0))
                for mi, (ml, mr) in enumerate(masks):
                    P.mm(pst.all(), ml, mr, start=False, stop=(mi == len(masks) - 1))
                eT = n.eT[n.ei % 2]
                n.ei += 1
                P.act(eT.all(), pst.all(), AF.Exp)
                for r in range(4):
                    P.mm(po[:, r, :], eT[:, r * 128:(r + 1) * 128], vfn(kc_), start=first, stop=(lastc and r == 3), skip_group_check=True)
                    first = False
            P.ts("dve", n.den[:, 8:12], po[:, :, 64], 1e-30, ALU.max)
            P.recip(n.den[:, 8:12], n.den[:, 8:12])
            P.tt("dve", n.den[:, 8:12], n.den[:, 8:12], n.gsig[:, gq, :, br], ALU.mult)
            dst = n.ob_f[:, gq * 4:(gq + 1) * 4, :]
            if br == 0:
                P.tt("dve", dst, po[:, :, 0:64], bc3(n.den[:, 8:12], 64), ALU.mult)
            else:
                P.tt("dve", n.obtmp.all(), po[:, :, 0:64], bc3(n.den[:, 8:12], 64), ALU.mult)
                P.tt("pool", dst, dst, n.obtmp.all(), ALU.add)

        branch(list(range(nch)),
               lambda c: n.kcT[0:64, gq, c * 128:(c + 1) * 128],
               lambda c: n.vcE[:, c, gq, :],
               lambda c: [(n.step[0:9, sh + c * 128:sh + (c + 1) * 128], n.onehot[0:9, :])], 0)
        if NSTOP == 8:
            return
        branch(list(range(i + 1)),
               lambda c: n.ksT[:, c * 128:(c + 1) * 128],
               lambda c: n.vsE[:, c, gq, :],
               lambda c: [(n.bigexp[0:64, c * 128:(c + 1) * 128], nsel)] + ([(identb.all(), n.caus.all())] if c == i else []), 1,
               qop=n.qTz[gq].all().w(lambda a: a.rearrange("p a b -> p (a b)")))
        if NSTOP == 9:
            return
        branch(list(range(max(0, i - 4), i + 1)),
               lambda c: n.kwT[0:64, gq, c % 5, :],
               lambda c: n.vwE[:, c % 5, gq, :],
               lambda c: ([(identb.all(), n.caus.all())] if c == i else []) + ([(identb.all(), n.winlo.all())] if c == i - 4 else []), 2)
        if NSTOP == 10:
            return
    P.cp("act", n.ob.all(), n.ob_f.all().w(lambda a: a.rearrange("p a b -> p (a b)")))
    if "o_b" in g.dbg:
        P.dma("sp", g.dbg["o_b"][r0:r0 + 128, :], n.ob_f.all().w(lambda a: a.rearrange("p a b -> p (a b)")))
    pOT = P.ps(nb(), [128, 4, 128], BF16)
    for k in range(4):
        P.tr(pOT[:, k, :], n.ob[:, k * 128:(k + 1) * 128], identb.all())
    P.cp("act", n.obT.all(), pOT.all())
    P.dma("sp", g.obT_d[r0:r0 + 128, :], n.obT.all().w(lambda a: a.rearrange("p a b -> p (a b)")))


def phase2(P, g, ntiles):
    P.mark()
    identb = g.CB("identb")
    wm = P.sb([128, 8, 2048], BF16)
    cast_load(P, wm, g.w_in, 0, C_MA, 2048, 8, step=1024)
    wa = P.sb([128, 4, D], BF16)
    cast_load(P, wa, g.w_a, 0, 0, D, 4)
    wb = P.sb([128, 4, D], BF16)
    cast_load(P, wb, g.w_b, 0, 0, D, 4)
    wo = P.sb([128, 8, D], BF16)
    cast_load(P, wo, g.w_out, 0, 0, D, 8)
    xnT = P.sb([128, 8, 128], BF16)
    oaT = P.sb([128, 4, 128], BF16)
    obT = P.sb([128, 4, 128], BF16)
    xt = P.sb([128, D], F32)
    sg = P.sb([128, 2048], F32)
    m1 = P.sb([128, 512], F32)
    m2 = P.sb([128, 512], F32)
    mg = P.sb([128, D], BF16)
    mT = P.sb([128, 8, 128], BF16)
    x1t = P.sb([128, D], F32)
    bk = [0]

    def nb():
        b = bk[0]
        bk[0] = (bk[0] + 1) % 8
        return b

    fl = lambda a: a.rearrange("p a b -> p (a b)")
    for gt in range(ntiles):
        r0 = gt * 128
        P.dma("sp", xnT.all().w(fl), g.xnT_d[r0:r0 + 128, :])
        P.dma("sp", oaT.all().w(fl), g.oaT_d[r0:r0 + 128, :])
        P.dma("sp", obT.all().w(fl), g.obT_d[r0:r0 + 128, :])
        P.dma("sp", xt.all(), g.x[r0:r0 + 128, :])
        for j in range(4):
            pg = P.ps(nb(), [128, 512])
            for k in range(8):
                P.mm(pg.all(), xnT[:, k, :], wm[:, k, j * 512:(j + 1) * 512], start=(k == 0), stop=(k == 7))
            P.act(sg[:, j * 512:(j + 1) * 512], pg.all(), AF.Exp, scale=-1.0)
        P.ts("pool", sg.all(), sg.all(), 1.0, ALU.add)
        P.recip(sg.all(), sg.all())
        for j in range(2):
            pa = P.ps(nb(), [128, 512])
            pb = P.ps(nb(), [128, 512])
            for k in range(4):
                P.mm(pa.all(), oaT[:, k, :], wa[:, k, j * 512:(j + 1) * 512], start=(k == 0), stop=(k == 3))
            for k in range(4):
                P.mm(pb.all(), obT[:, k, :], wb[:, k, j * 512:(j + 1) * 512], start=(k == 0), stop=(k == 3))
            P.tt("dve", m1.all(), pa.all(), sg[:, j * 512:(j + 1) * 512], ALU.mult)
            P.tt("dve", m2.all(), pb.all(), sg[:, 1024 + j * 512:1024 + (j + 1) * 512], ALU.mult)
            P.tt("pool", mg[:, j * 512:(j + 1) * 512], m1.all(), m2.all(), ALU.add)
        if "merged" in g.dbg:
            P.tt("pool", m1.all(), m1.all(), m2.all(), ALU.add)
            P.dma("sp", g.dbg["merged"][r0:r0 + 128, 512:1024], m1.all())
        pmT = P.ps(nb(), [128, 8, 128], BF16)
        for k in range(8):
            P.tr(pmT[:, k, :], mg[:, k * 128:(k + 1) * 128], identb.all())
        P.cp("act", mT.all(), pmT.all())
        for j in range(2):
            po = P.ps(nb(), [128, 512])
            for k in range(8):
                P.mm(po.all(), mT[:, k, :], wo[:, k, j * 512:(j + 1) * 512], start=(k == 0), stop=(k == 7))
            P.tt("dve", x1t[:, j * 512:(j + 1) * 512], xt[:, j * 512:(j + 1) * 512], po.all(), ALU.add)
        P.dma("sp", g.x1_d[r0:r0 + 128, :], x1t.all())
        if "x1" in g.dbg:
            P.dma("sp", g.dbg["x1"][r0:r0 + 128, :], x1t.all())
    P.release()


def phase3(P, g, ntiles, ST=256):
    P.sb_off = 0
    identb = P.sb([128, 128], BF16)
    P.dma("sp", identb.all(), g.cb[:, 0:128])
    wgu = P.sb([128, 8, 2 * FH], BF16)
    cast_load(P, wgu, g.w_gu, 0, 0, 2 * FH, 8, step=1408)
    wd = P.sb([128, 22, D], BF16)
    cast_load(P, wd, g.w_down, 0, 0, D, 22)
    gain2B = P.sb([128, D], F32)
    P.dma("sp", gain2B.all(), g.gain2[0:1, :].bc([128, D]))
    gain3B = P.sb([128, D], F32)
    P.dma("sp", gain3B.all(), g.gain3[0:1, :].bc([128, D]))
    nsub = ST // 128
    x1s = [P.sb([128, D], F32) for _ in range(nsub)]
    xn = P.sb([128, D], BF16)
    xnT = P.sb([128, 8, 128], BF16)
    xn2T = P.sb([128, 8, ST], BF16)
    hT = P.sb([128, 22, ST], BF16)
    sgt = P.sb([128, ST], F32)
    x2 = P.sb([128, D], F32)
    outt = P.sb([128, D], F32)
    sm = P.sb([128, 8], F32)
    bk = [0]

    def nb():
        b = bk[0]
        bk[0] = (bk[0] + 1) % 8
        return b

    for st in range(ntiles * 128 // ST):
        for sub in range(nsub):
            r0 = st * ST + sub * 128
            P.dma("sp", x1s[sub].all(), g.x1_d[r0:r0 + 128, :])
            norm_transpose(P, g, x1s[sub], gain2B, xn, xnT, None, sm, identb, nb())
            P.cp("pool", xn2T[:, :, sub * 128:(sub + 1) * 128], xnT.all())
        for f in range(22):
            pgt = P.ps(nb(), [128, ST])
            pup = P.ps(nb(), [128, ST])
            for k in range(8):
                P.mm(pgt.all(), wgu[:, k, f * 128:(f + 1) * 128], xn2T[:, k, :], start=(k == 0), stop=(k == 7))
            for k in range(8):
                P.mm(pup.all(), wgu[:, k, FH + f * 128:FH + (f + 1) * 128], xn2T[:, k, :], start=(k == 0), stop=(k == 7))
            P.act(sgt.all(), pgt.all(), AF.Silu)
            P.tt("dve", hT[:, f, :], sgt.all(), pup.all(), ALU.mult)
        for sub in range(nsub):
            r0 = st * ST + sub * 128
            for j in range(2):
                pd = P.ps(nb(), [128, 512])
                for f in range(22):
                    P.mm(pd.all(), hT[:, f, sub * 128:(sub + 1) * 128], wd[:, f, j * 512:(j + 1) * 512], start=(f == 0), stop=(f == 21))
                P.tt("dve", x2[:, j * 512:(j + 1) * 512], x1s[sub][:, j * 512:(j + 1) * 512], pd.all(), ALU.add)
            P.act(outt.all(), x2.all(), AF.Square, accum=sm[:, 4:5])
            rstd_of(P, sm[:, 5:6], sm[:, 4:5], D, sm[:, 6:7])
            P.stt(outt.all(), x2.all(), sm[:, 5:6], gain3B.all(), ALU.mult, ALU.mult)
            P.dma("sp", g.out[r0:r0 + 128, :], outt.all())


def build_program(nseq=2, ntiles=None, dbg=None):
    nc = bass.Bass("TRN2", target_bir_lowering=False)
    P = Prog(nc)
    g = setup_common(P, nseq, dbg or {})
    load_consts(P, g)
    nt = nseq * NT if ntiles is None else ntiles
    ph = os.environ.get("K_PH", "123")
    phase1(P, g, nt, do_nsa=("n" not in ph))
    if "2" in ph:
        phase2(P, g, nt)
    if "3" in ph:
        phase3(P, g, nt)
    P.finalize()
    return nc, P


def kernel(**inputs):
    x = np.asarray(inputs["x"], np.float32)
    B = x.shape[0]
    ncore = 8
    nseq = B // ncore
    nc, P = build_program(nseq=nseq)
    sh = shared_inputs(inputs)
    in_maps = []
    for c in range(ncore):
        m = dict(sh)
        m["x"] = np.ascontiguousarray(x[c * nseq:(c + 1) * nseq].reshape(nseq * S, D))
        in_maps.append(m)
    res = run_bass_kernel_spmd(nc, in_maps, core_ids=list(range(ncore)))
    out = np.stack([np.asarray(r["out"], np.float32).reshape(nseq, S, D) for r in res.results], 0)
    return out.reshape(B, S, D)
```

```python
import numpy as np
import ml_dtypes
import concourse.bass as bass
import concourse.mybir as mybir
from concourse.bass_utils import run_bass_kernel_spmd

F32 = mybir.dt.float32
BF16 = mybir.dt.bfloat16
U8 = mybir.dt.uint8
ALU = mybir.AluOpType
AF = mybir.ActivationFunctionType
DSZ = {F32: 4, BF16: 2, U8: 1}

ENGS = ("pe", "act", "dve", "pool", "sp")
DMA_POOL = 6
NEG = -30000.0


class V:
    __slots__ = ("ap", "reg")

    def __init__(self, ap, reg):
        self.ap = ap
        self.reg = reg

    def w(self, fn):
        return V(fn(self.ap), self.reg)

    def bc(self, shape):
        return V(self.ap.broadcast_to(list(shape)), self.reg)


class T:
    def __init__(self, space, ap, shape, esz, boff):
        self.space = space
        self.ap = ap
        self.shape = tuple(shape)
        self.esz = esz
        self.boff = boff
        fs = [1] * len(shape)
        for i in range(len(shape) - 2, 0, -1):
            fs[i] = fs[i + 1] * shape[i + 1]
        self.fstr = fs

    def __getitem__(self, idx):
        if not isinstance(idx, tuple):
            idx = (idx,)
        idx = idx + (slice(None),) * (len(self.shape) - len(idx))
        lo, hi = [], []
        for ix, n in zip(idx, self.shape):
            if isinstance(ix, slice):
                a, b, st = ix.indices(n)
                assert st > 0 and b > a, (self.space, idx, self.shape)
                lo.append(a)
                hi.append(a + ((b - a - 1) // st) * st)
            else:
                assert 0 <= ix < n, (self.space, idx, self.shape)
                lo.append(ix)
                hi.append(ix)
        f0 = sum(l * s for l, s in zip(lo[1:], self.fstr[1:]))
        f1 = sum(h * s for h, s in zip(hi[1:], self.fstr[1:])) + 1
        p0, p1 = lo[0], hi[0] + 1
        b0, b1 = self.boff + f0 * self.esz, self.boff + f1 * self.esz
        if self.space == "ps":
            p0, p1 = p0 // 32 * 32, (p1 + 31) // 32 * 32
            b0, b1 = b0 // 2048 * 2048, (b1 + 2047) // 2048 * 2048
        return V(self.ap[idx], (self.space, p0, p1, b0, b1))

    def all(self):
        return self[tuple(slice(None) for _ in self.shape)]


def _ovl(a, b):
    return a[0] == b[0] and a[1] < b[2] and b[1] < a[2] and a[3] < b[4] and b[3] < a[4]


def _contains(a, b):
    return a[0] == b[0] and a[1] <= b[1] and b[2] <= a[2] and a[3] <= b[3] and b[4] <= a[4]


class Op:
    __slots__ = ("eng", "emit", "idx", "deps", "dma", "tok", "sig", "eidx")


class Prog:
    def __init__(self, nc, sb_bytes=204288):
        self.nc = nc
        self.ops = []
        self.acc = {}
        self.sb_h = nc.alloc_sbuf_tensor("arena", [128, sb_bytes], U8)
        self.ps_h = nc.alloc_psum_tensor("psarena", [128, 4096], F32)
        self.sb_bytes = sb_bytes
        self.sb_off = 0
        self.marks = []

    def sb(self, shape, dtype, align=32):
        esz = DSZ[dtype]
        n = int(np.prod(shape[1:]))
        off = (self.sb_off + align - 1) // align * align
        nb = n * esz
        assert off + nb <= self.sb_bytes, ("SBUF arena overflow", off, nb)
        self.sb_off = off + nb
        ap = self.sb_h[:, off:off + nb].bitcast(dtype)
        if len(shape) > 2:
            names = " ".join("d%d" % i for i in range(1, len(shape)))
            ap = ap.rearrange("p (%s) -> p %s" % (names, names), **{"d%d" % i: shape[i] for i in range(1, len(shape))})
        return T("sb", ap, [128] + list(shape[1:]), esz, off)

    def mark(self):
        self.marks.append(self.sb_off)

    def release(self):
        self.sb_off = self.marks.pop()

    def ps(self, bank, shape, dtype=F32, boff=0):
        esz = DSZ[dtype]
        n = int(np.prod(shape[1:]))
        nb = n * esz
        assert boff + nb <= 2048
        e0 = (bank * 2048 + boff) // 4
        ap = self.ps_h[:, e0:e0 + (nb + 3) // 4]
        if dtype != F32:
            ap = ap.bitcast(dtype)
        if len(shape) > 2:
            names = " ".join("d%d" % i for i in range(1, len(shape)))
            ap = ap.rearrange("p (%s) -> p %s" % (names, names), **{"d%d" % i: shape[i] for i in range(1, len(shape))})
        return T("ps", ap, [128] + list(shape[1:]), esz, bank * 2048 + boff)

    def dram(self, name, shape, dtype, kind="Internal"):
        h = self.nc.dram_tensor(name, list(shape), dtype, kind=kind)
        return T(name, h, shape, 1, 0)

    def op(self, eng, emit, reads, writes, dma=False):
        o = Op()
        o.eng, o.emit, o.idx, o.dma, o.sig = eng, emit, len(self.ops), dma, False
        deps = set()
        for v in reads:
            r = v.reg
            for (reg, oi, w) in self.acc.setdefault(r[0], []):
                if w and _ovl(reg, r):
                    deps.add(oi)
        for v in writes:
            r = v.reg
            for (reg, oi, w) in self.acc.setdefault(r[0], []):
                if _ovl(reg, r):
                    deps.add(oi)
        deps.discard(o.idx)
        o.deps = deps
        self.ops.append(o)
        for v in writes:
            r = v.reg
            lst = self.acc[r[0]]
            lst[:] = [e for e in lst if not _contains(r, e[0])]
            lst.append((r, o.idx, True))
        for v in reads:
            r = v.reg
            lst = self.acc[r[0]]
            if not dma:
                lst[:] = [e for e in lst if not ((not e[2]) and _contains(r, e[0])
                                                 and (not self.ops[e[1]].dma) and self.ops[e[1]].eng == eng)]
            lst.append((r, o.idx, False))
        return o

    def dma(self, q, out, in_):
        return self.op(q, lambda e: e.dma_start(out=out.ap, in_=in_.ap), [in_], [out], dma=True)

    def mm(self, out, lhsT, rhs, start=True, stop=True, **kw):
        return self.op("pe", lambda e: e.matmul(out.ap, lhsT=lhsT.ap, rhs=rhs.ap, start=start, stop=stop, **kw), [lhsT, rhs], [out])

    def tr(self, out, in_, ident):
        return self.op("pe", lambda e: e.transpose(out.ap, in_.ap, ident.ap), [in_, ident], [out])

    def act(self, out, in_, func, bias=None, scale=None, accum=None, eng="act"):
        rd = [in_]
        kw = {}
        if bias is not None:
            if isinstance(bias, V):
                rd.append(bias)
                kw["bias"] = bias.ap
            else:
                kw["bias"] = float(bias)
        if scale is not None:
            if isinstance(scale, V):
                rd.append(scale)
                kw["scale"] = scale.ap
            else:
                kw["scale"] = float(scale)
        wr = [out]
        if accum is not None:
            wr.append(accum)
            kw["accum_out"] = accum.ap
        return self.op(eng, lambda e: e.activation(out=out.ap, in_=in_.ap, func=func, **kw), rd, wr)

    def tt(self, eng, out, in0, in1, op):
        return self.op(eng, lambda e: e.tensor_tensor(out=out.ap, in0=in0.ap, in1=in1.ap, op=op), [in0, in1], [out])

    def ts(self, eng, out, in0, s1, op0, s2=None, op1=None):
        rd = [in0]
        a1 = s1.ap if isinstance(s1, V) else float(s1)
        if isinstance(s1, V):
            rd.append(s1)
        kw = {}
        if op1 is not None:
            kw["op1"] = op1
            kw["scalar2"] = s2.ap if isinstance(s2, V) else float(s2)
            if isinstance(s2, V):
                rd.append(s2)
        else:
            kw["scalar2"] = None
        return self.op(eng, lambda e: e.tensor_scalar(out=out.ap, in0=in0.ap, scalar1=a1, op0=op0, **kw), rd, [out])

    def stt(self, out, in0, scalar, in1, op0, op1):
        rd = [in0, in1]
        a = scalar.ap if isinstance(scalar, V) else float(scalar)
        if isinstance(scalar, V):
            rd.append(scalar)
        return self.op("dve", lambda e: e.scalar_tensor_tensor(out=out.ap, in0=in0.ap, scalar=a, in1=in1.ap, op0=op0, op1=op1), rd, [out])

    def cp(self, eng, out, in_):
        if eng == "act":
            return self.op("act", lambda e: e.copy(out=out.ap, in_=in_.ap), [in_], [out])
        return self.op(eng, lambda e: e.tensor_copy(out=out.ap, in_=in_.ap), [in_], [out])

    def memset(self, eng, out, val):
        return self.op(eng, lambda e: e.memset(out.ap, val), [], [out])

    def recip(self, out, in_):
        return self.op("dve", lambda e: e.reciprocal(out=out.ap, in_=in_.ap), [in_], [out])

    def finalize(self):
        nc = self.nc
        ops = self.ops

        def pepe(a, b):
            return a.eng == "pe" and b.eng == "pe" and not a.dma and not b.dma

        for o in ops:
            for d in o.deps:
                if not pepe(ops[d], o):
                    ops[d].sig = True
            if o.dma:
                o.sig = True
        eng_sem = {e: nc.alloc_semaphore("s_" + e) for e in ENGS}
        dma_sems = {e: [nc.alloc_semaphore("d_%s_%d" % (e, i)) for i in range(DMA_POOL)] for e in ("sp", "act", "pool")}
        cnt = {e: 0 for e in ENGS}
        dcnt = {e: 0 for e in ENGS}
        for o in ops:
            if o.dma:
                n = dcnt[o.eng]
                dcnt[o.eng] += 1
                o.tok = (("d", o.eng, n % DMA_POOL), 16 * (n // DMA_POOL + 1))
                o.eidx = n
            elif o.sig:
                cnt[o.eng] += 1
                o.tok = (("e", o.eng), cnt[o.eng])
            else:
                o.tok = None

        def semof(k):
            return eng_sem[k[1]] if k[0] == "e" else dma_sems[k[1]][k[2]]

        streams = {e: [] for e in ENGS}
        known = {e: {} for e in ENGS}
        for o in ops:
            need = {}
            for d in o.deps:
                od = ops[d]
                if pepe(od, o):
                    continue
                k, v = od.tok
                if need.get(k, 0) < v:
                    need[k] = v
            if o.dma and o.eidx >= DMA_POOL:
                k = ("d", o.eng, o.eidx % DMA_POOL)
                v = 16 * (o.eidx // DMA_POOL)
                if need.get(k, 0) < v:
                    need[k] = v
            kn = known[o.eng]
            waits = []
            for k, v in need.items():
                if kn.get(k, 0) < v:
                    kn[k] = v
                    waits.append((semof(k), v))
            streams[o.eng].append((waits, o))
        final = []
        for e in ENGS:
            if cnt[e]:
                final.append((eng_sem[e], cnt[e]))
            n = dcnt[e]
            for i in range(min(n, DMA_POOL)):
                last = ((n - 1 - i) // DMA_POOL) * DMA_POOL + i
                final.append((dma_sems[e][i], 16 * (last // DMA_POOL + 1)))
        self.stats = {e: len(streams[e]) for e in ENGS}
        self.stats["sig"] = dict(cnt)
        self.stats["dma"] = dict(dcnt)

        def run(eng_obj, name):
            for waits, o in streams[name]:
                for s, v in waits:
                    eng_obj.wait_ge(s, v)
                ins = o.emit(eng_obj)
                if o.tok is not None:
                    ins.then_inc(semof(o.tok[0]), 16 if o.dma else 1)
            if name == "sp":
                for s, v in final:
                    eng_obj.wait_ge(s, v)

        with nc.Block() as block:
            @block.tensor
            def _(e):
                run(e, "pe")

            @block.scalar
            def _(e):
                run(e, "act")

            @block.vector
            def _(e):
                run(e, "dve")

            @block.gpsimd
            def _(e):
                run(e, "pool")

            @block.sync
            def _(e):
                run(e, "sp")


D = 1024
S = 4096
NT = 32
INW = 5408
FH = 2816
EPS = 1e-6
C_Z = 1536
C_BA = 2048
C_Q = 2056
C_KC = 2568
C_MA = 3360
NW1 = 3360


def _bf(a):
    return np.asarray(a, np.float32).astype(ml_dtypes.bfloat16)


def host_consts():
    p = np.arange(128)
    cf = {}
    cf["identf"] = np.eye(128, dtype=np.float32)
    cf["identfold"] = (p[:, None] % 64 == np.arange(64)[None, :]).astype(np.float32)
    cf["blockones"] = (p[:, None] // 64 == p[None, :] // 64).astype(np.float32)
    cf["ublk"] = ((p[:, None] // 64 == p[None, :] // 64) & (p[:, None] <= p[None, :])).astype(np.float32)
    cm = np.zeros((128, 2, 128), np.float32)
    for c in range(2):
        cm[c * 64:(c + 1) * 64, c, :] = 1.0
    cf["chunkmask"] = cm.reshape(128, 256)
    pl = (p % 64)[:, None]
    xx = np.arange(64)[None, :]
    mA = np.where(xx < pl, 0.0, NEG)
    mAT = np.where(xx > pl, 0.0, NEG)
    mQT = np.where(xx >= pl, 0.0, NEG)
    md = np.stack([np.repeat(m[:, None, :], 4, 1) for m in (mA, mAT, mQT)], 1)
    cf["maskD"] = md.reshape(128, 768).astype(np.float32)
    half = 8
    inv_freq = 500000.0 ** (-np.arange(half, dtype=np.float32) / half)
    pos = np.arange(S, dtype=np.float32)
    ang = pos[:, None] * inv_freq[None, :]
    cos = np.cos(ang).astype(np.float32).reshape(NT, 128, 8).transpose(1, 0, 2).reshape(128, NT * 8)
    sin = np.sin(ang).astype(np.float32).reshape(NT, 128, 8).transpose(1, 0, 2).reshape(128, NT * 8)
    cf["cosk"] = cos
    cf["sink"] = sin
    c = np.arange(128)[None, :]
    jr = c - 64
    cr = (p[:, None] >= 64).astype(np.int64)
    invalid = jr > cr
    forced = (jr == cr) | (jr == cr - 1)
    cf["keep"] = (~(invalid | forced)).astype(np.float32)
    cf["ovr"] = np.where(invalid, -1.0, np.where(forced, 1000.0, 0.0)).astype(np.float32)
    cb = {}
    cb["identb"] = np.eye(128, dtype=np.float32)
    fq = np.floor((p - 31) / 16.0).astype(np.int64)
    step = np.zeros((128, 768), np.float32)
    for vi in range(9):
        step[vi, :] = np.where((np.arange(768) - 384) > (vi - 2), NEG, 0.0)
    cb["step"] = step
    oh = np.zeros((128, 128), np.float32)
    for vi in range(9):
        oh[vi, :] = (fq == vi - 2)
    cb["onehot"] = np.tile(oh, (1, 4))
    be = np.zeros((128, 4096), np.float32)
    for j in range(64):
        be[j, j * 64:(j + 1) * 64] = 1.0
    cb["bigexp"] = be
    k = p[:, None]
    ql = p[None, :]
    cb["caus"] = np.tile(np.where(k > ql, NEG, 0.0), (1, 4))
    cb["winlo"] = np.tile(np.where(k <= ql, NEG, 0.0), (1, 4))
    cfo, cbo = {}, {}
    o = 0
    for kk, vv in cf.items():
        cfo[kk] = (o, vv.shape[1])
        o += vv.shape[1]
    ncf = o
    o = 0
    for kk, vv in cb.items():
        cbo[kk] = (o, vv.shape[1])
        o += vv.shape[1]
    ncb = o
    cfa = np.concatenate(list(cf.values()), 1).astype(np.float32)
    cba = _bf(np.concatenate(list(cb.values()), 1))
    return cfa, cba, cfo, cbo, ncf, ncb


_HC = host_consts()


def bc3(v, n):
    return v.w(lambda a: a.unsqueeze(2).broadcast_to([a.shape[0], a.shape[1], n]))


def bcm(v, m):
    return v.w(lambda a: a.unsqueeze(1).broadcast_to([a.shape[0], m, a.shape[1]]))


import os
STOP = int(os.environ.get('K_STOP', '0'))
NSTOP = int(os.environ.get('K_NSTOP', '0'))


class Ctx:
    pass


def setup_common(P, nseq, dbg):
    g = Ctx()
    ntok = nseq * S
    g.nseq = nseq
    g.ntok = ntok
    g.x = P.dram("x", [ntok, D], F32, kind="ExternalInput")
    g.w_in = P.dram("w_in", [D, INW], F32, kind="ExternalInput")
    g.w_gu = P.dram("w_gu", [D, 2 * FH], F32, kind="ExternalInput")
    g.w_down = P.dram("w_down", [FH, D], F32, kind="ExternalInput")
    g.w_a = P.dram("w_a", [512, D], F32, kind="ExternalInput")
    g.w_b = P.dram("w_b", [512, D], F32, kind="ExternalInput")
    g.w_out = P.dram("w_out", [D, D], F32, kind="ExternalInput")
    g.w1k = P.dram("w1k", [2048, 128], F32, kind="ExternalInput")
    g.w1v = P.dram("w1v", [2048, 128], F32, kind="ExternalInput")
    g.w2k = P.dram("w2k", [128, 64], F32, kind="ExternalInput")
    g.w2v = P.dram("w2v", [128, 64], F32, kind="ExternalInput")
    g.posk = P.dram("posk", [64, 32], F32, kind="ExternalInput")
    g.posv = P.dram("posv", [64, 32], F32, kind="ExternalInput")
    g.gain1 = P.dram("gain1", [1, D], F32, kind="ExternalInput")
    g.gain2 = P.dram("gain2", [1, D], F32, kind="ExternalInput")
    g.gain3 = P.dram("gain3", [1, D], F32, kind="ExternalInput")
    g.onorm = P.dram("onorm", [1, 128], F32, kind="ExternalInput")
    g.alog = P.dram("alog", [1, 4], F32, kind="ExternalInput")
    g.dtb = P.dram("dtb", [1, 4], F32, kind="ExternalInput")
    g.convw = P.dram("convw", [128, 48], F32, kind="ExternalInput")
    g.cf = P.dram("cf", [128, _HC[4]], F32, kind="ExternalInput")
    g.cb = P.dram("cb", [128, _HC[5]], BF16, kind="ExternalInput")
    g.out = P.dram("out", [ntok, D], F32, kind="ExternalOutput")
    ntile = ntok // 128
    g.xnT_d = P.dram("xnT_d", [ntile * 128, D], BF16)
    g.oaT_d = P.dram("oaT_d", [ntile * 128, 512], BF16)
    g.obT_d = P.dram("obT_d", [ntile * 128, 512], BF16)
    g.x1_d = P.dram("x1_d", [ntok, D], F32)
    g.dbg = {}
    for name, shape in dbg.items():
        g.dbg[name] = P.dram("dbg_" + name, list(shape), F32, kind="ExternalOutput")
    return g


def load_consts(P, g):
    cfo, cbo = _HC[2], _HC[3]
    cft = P.sb([128, _HC[4]], F32)
    cbt = P.sb([128, _HC[5]], BF16)
    P.dma("sp", cft.all(), g.cf.all())
    P.dma("sp", cbt.all(), g.cb.all())
    g.cft, g.cbt = cft, cbt

    def CF(name, *shape):
        o, n = cfo[name]
        t = T("sb", cft.ap[:, o:o + n], [128, n], 4, cft.boff + o * 4)
        if shape:
            names = " ".join("d%d" % i for i in range(len(shape)))
            t = T("sb", t.ap.rearrange("p (%s) -> p %s" % (names, names), **{"d%d" % i: s for i, s in enumerate(shape)}),
                  [128] + list(shape), 4, cft.boff + o * 4)
        return t

    def CB(name):
        o, n = cbo[name]
        return T("sb", cbt.ap[:, o:o + n], [128, n], 2, cbt.boff + o * 2)

    g.CF, g.CB = CF, CB


def cast_load(P, dst, src_t, r0, c0, ncols, kchunks, step=1024):
    for k in range(kchunks):
        for cc in range(0, ncols, step):
            n = min(step, ncols - cc)
            P.dma("pool", dst[:, k, cc:cc + n], src_t[r0 + k * 128:r0 + (k + 1) * 128, c0 + cc:c0 + cc + n])


def rstd_of(P, out, ssq, n, tmp):
    P.ts("dve", tmp, ssq, 1.0 / n, ALU.mult, EPS, ALU.add)
    P.act(tmp, tmp, AF.Ln)
    P.act(out, tmp, AF.Exp, scale=-0.5)


def norm_transpose(P, g, xt, gainB, xn, xnT, junk, sm, identb, bank):
    P.act(xn.all(), xt.all(), AF.Square, accum=sm[:, 0:1])
    rstd_of(P, sm[:, 1:2], sm[:, 0:1], D, sm[:, 2:3])
    P.stt(xn.all(), xt.all(), sm[:, 1:2], gainB.all(), ALU.mult, ALU.mult)
    pT = P.ps(bank, [128, 8, 128], BF16)
    for k in range(8):
        P.tr(pT[:, k, :], xn[:, k * 128:(k + 1) * 128], identb.all())
    P.cp("act", xnT.all(), pT.all())


def phase1(P, g, ntiles, do_nsa=True):
    P.mark()
    CF, CB = g.CF, g.CB
    identb = CB("identb")
    identfold = CF("identfold")
    blockones = CF("blockones")
    ublk = CF("ublk")
    chunkmask = CF("chunkmask", 2, 128)
    maskD = CF("maskD", 3, 4, 64)
    win = P.sb([128, 8, NW1], BF16)
    cast_load(P, win, g.w_in, 0, 0, NW1, 8, step=840)
    gainB = P.sb([128, D], F32)
    P.dma("sp", gainB.all(), g.gain1[0:1, :].bc([128, D]))
    ognB = P.sb([128, 128], F32)
    P.dma("sp", ognB.all(), g.onorm[0:1, :].bc([128, 128]))
    sc0 = P.sb([128, 16], F32)
    P.dma("sp", sc0[:, 0:4], g.alog[0:1, :].bc([128, 4]))
    P.dma("sp", sc0[:, 4:8], g.dtb[0:1, :].bc([128, 4]))
    P.act(sc0[:, 8:12], sc0[:, 0:4], AF.Exp)
    P.ts("dve", sc0[:, 8:12], sc0[:, 8:12], -1.0, ALU.mult)
    nAB = sc0[:, 8:12]
    dtbB = sc0[:, 4:8]
    convw = P.sb([128, 48], F32)
    P.dma("sp", convw.all(), g.convw.all())
    cdiag = P.sb([128, 48, 128], BF16)
    for q in range(48):
        P.ts("dve", cdiag[:, q, :], identb.all(), convw[:, q:q + 1], ALU.mult)
    xt = P.sb([128, D], F32)
    xn = P.sb([128, D], BF16)
    junk = P.sb([128, 128], BF16)
    xnT = P.sb([128, 8, 128], BF16)
    sm = P.sb([128, 8], F32)
    cbuf = P.sb([128, 12, 131], BF16)
    cs = P.sb([128, 12, 128], BF16)
    ssq = P.sb([128, 8], F32)
    rn = P.sb([128, 8], F32)
    sc = P.sb([128, 64], F32)
    k_n = P.sb([128, 4, 128], BF16)
    kbg = P.sb([128, 4, 128], BF16)
    kd = P.sb([128, 4, 128], BF16)
    vb = P.sb([128, 4, 128], BF16)
    q_n = P.sb([128, 4, 128], BF16)
    kqT = P.sb([128, 8, 128], BF16)
    dgt = P.sb([128, 2, 4, 64], F32)
    DD = P.sb([128, 3, 4, 64], F32)
    EE = DD
    ATf = P.sb([128, 4, 64], F32)
    TTf = P.sb([128, 4, 64], F32)
    TTb = P.sb([128, 4, 64], BF16)
    Pb = [P.sb([128, 2, 4, 64], BF16) for _ in range(2)]
    QKm = P.sb([128, 4, 64], BF16)
    uS = P.sb([128, 4, 128], F32)
    wT = P.sb([128, 2, 4, 64], BF16)
    Sf = P.sb([128, 4, 128], F32)
    Sb = P.sb([128, 4, 128], BF16)
    vn = P.sb([128, 4, 128], BF16)
    tmpo = P.sb([128, 4, 128], F32)
    o_raw = P.sb([128, 4, 128], F32)
    zs = P.sb([128, 512], F32)
    oa = P.sb([128, 4, 128], BF16)
    oaT = P.sb([128, 4, 128], BF16)
    g1 = Ctx()
    if do_nsa:
        nsa_setup(P, g, g1)

    bk = [2, 6]

    def nb():
        b = bk[0]
        bk[0] = 2 + (bk[0] - 2 + 1) % 4
        return b

    def nbt():
        b = bk[1]
        bk[1] = 6 + (bk[1] - 6 + 1) % 2
        return b

    for gt in range(ntiles):
        b, i = divmod(gt, NT)
        r0 = gt * 128
        if i == 0:
            P.memset("pool", cbuf[:, :, 0:3], 0.0)
            P.memset("pool", Sf.all(), 0.0)
            P.memset("pool", Sb.all(), 0.0)
        P.dma("sp", xt.all(), g.x[r0:r0 + 128, :])
        norm_transpose(P, g, xt, gainB, xn, xnT, junk, sm, identb, nb())
        P.dma("sp", g.xnT_d[r0:r0 + 128, :], xnT.all().w(lambda a: a.rearrange("p a b -> p (a b)")))
        if STOP == 1:
            continue
        for fb in range(3):
            pq = P.ps(nb(), [128, 4, 128])
            for ff in range(4):
                f = fb * 4 + ff
                for k in range(8):
                    P.mm(pq[:, ff, :], win[:, k, f * 128:(f + 1) * 128], xnT[:, k, :], start=(k == 0), stop=(k == 7))
            P.cp("act", cbuf[:, fb * 4:(fb + 1) * 4, 3:131], pq.all())
        for fb in range(3):
            pc = P.ps(nb(), [128, 4, 128])
            for ff in range(4):
                f = fb * 4 + ff
                for j in range(4):
                    P.mm(pc[:, ff, :], cdiag[:, f * 4 + j, :], cbuf[:, f, j:j + 128], start=(j == 0), stop=(j == 3))
            P.act(cs[:, fb * 4:(fb + 1) * 4, :], pc.all(), AF.Silu)
        P.cp("pool", cbuf[:, :, 0:3], cbuf[:, :, 128:131])
        if STOP == 2:
            continue
        psA = P.ps(nb(), [128, 8, 128], BF16)
        psB = P.ps(nb(), [128, 4, 128], BF16)
        for f in range(8):
            P.tr(psA[:, f, :], cs[:, f, :], identb.all())
        for f in range(4):
            P.tr(psB[:, f, :], cs[:, 8 + f, :], identb.all())
        for f in range(8):
            P.act(junk[:, 0:128], psA[:, f, :], AF.Square, accum=ssq[:, f:f + 1])
        P.ts("dve", rn.all(), ssq.all(), EPS, ALU.add)
        P.act(rn.all(), rn.all(), AF.Ln)
        P.act(rn.all(), rn.all(), AF.Exp, scale=-0.5)
        if STOP == 3:
            continue
        pba = P.ps(nb(), [128, 8])
        for k in range(8):
            P.mm(pba.all(), xnT[:, k, :], win[:, k, C_BA:C_BA + 8], start=(k == 0), stop=(k == 7))
        P.act(sc[:, 52:56], pba[:, 0:4], AF.Exp, scale=-1.0)
        P.act(sc[:, 0:4], sc[:, 52:56], AF.Ln, bias=1.0)
        P.act(sc[:, 4:8], sc[:, 0:4], AF.Exp, scale=-1.0)
        P.tt("dve", sc[:, 8:12], pba[:, 4:8], dtbB, ALU.add)
        P.act(sc[:, 8:12], sc[:, 8:12], AF.Exp)
        P.act(sc[:, 8:12], sc[:, 8:12], AF.Ln, bias=1.0)
        P.tt("dve", sc[:, 8:12], sc[:, 8:12], nAB, ALU.mult)
        pg = P.ps(nb(), [128, 16])
        P.mm(pg[:, 0:4], ublk.all(), sc[:, 8:12])
        P.mm(pg[:, 4:8], blockones.all(), sc[:, 8:12])
        for c in range(2):
            P.mm(pg[:, 8 + 4 * c:12 + 4 * c], chunkmask[:, c, :], sc[:, 8:12])
        P.cp("act", sc[:, 12:20], pg[:, 0:8])
        P.act(sc[:, 28:36], pg[:, 8:16], AF.Exp)
        P.act(sc[:, 20:24], sc[:, 12:16], AF.Exp)
        P.tt("dve", sc[:, 24:28], sc[:, 16:20], sc[:, 12:16], ALU.subtract)
        P.act(sc[:, 24:28], sc[:, 24:28], AF.Exp)
        P.tt("dve", sc[:, 36:40], sc[:, 12:16], sc[:, 0:4], ALU.subtract)
        P.tt("dve", sc[:, 40:44], rn[:, 4:8], sc[:, 4:8], ALU.mult)
        P.tt("dve", sc[:, 40:44], sc[:, 40:44], sc[:, 20:24], ALU.mult)
        P.tt("dve", sc[:, 44:48], rn[:, 4:8], sc[:, 24:28], ALU.mult)
        P.ts("dve", sc[:, 48:52], rn[:, 0:4], 128.0 ** -0.5, ALU.mult)
        if STOP == 4:
            continue
        P.tt("dve", k_n.all(), psA[:, 4:8, :], bc3(rn[:, 4:8], 128), ALU.mult)
        P.tt("dve", kbg.all(), psA[:, 4:8, :], bc3(sc[:, 40:44], 128), ALU.mult)
        P.tt("dve", kd.all(), psA[:, 4:8, :], bc3(sc[:, 44:48], 128), ALU.mult)
        P.tt("dve", q_n.all(), psA[:, 0:4, :], bc3(sc[:, 48:52], 128), ALU.mult)
        P.tt("dve", vb.all(), psB.all(), bc3(sc[:, 4:8], 128), ALU.mult)
        pkq = P.ps(nb(), [128, 8, 128], BF16)
        for h in range(4):
            P.tr(pkq[:, h, :], k_n[:, h, :], identb.all())
            P.tr(pkq[:, 4 + h, :], q_n[:, h, :], identb.all())
        P.cp("act", kqT.all(), pkq.all())
        if STOP == 5:
            continue
        pKQ = P.ps(nb(), [128, 2, 4, 64])
        for c in range(2):
            cs_ = slice(c * 64, (c + 1) * 64)
            for h in range(4):
                P.mm(pKQ[cs_, 0, h, :], kqT[:, h, cs_], kqT[:, h, cs_])
                P.mm(pKQ[cs_, 1, h, :], kqT[:, h, cs_], kqT[:, 4 + h, cs_])
        P.tt("pool", dgt[:, 0], bcm(identfold.all(), 4), bc3(sc[:, 12:16], 64), ALU.mult)
        P.tt("pool", dgt[:, 1], bcm(identfold.all(), 4), bc3(sc[:, 36:40], 64), ALU.mult)
        pBf = P.ps(nb(), [128, 2, 4, 64])
        P.mm(pBf.all().w(lambda a: a.rearrange("p a b c -> p (a b c)")), blockones.all(),
             dgt.all().w(lambda a: a.rearrange("p a b c -> p (a b c)")))
        P.tt("dve", DD[:, 0], bc3(sc[:, 36:40], 64), pBf[:, 0], ALU.subtract)
        P.tt("dve", DD[:, 1], pBf[:, 1], bc3(sc[:, 12:16], 64), ALU.subtract)
        P.tt("dve", DD[:, 2], pBf[:, 0], bc3(sc[:, 12:16], 64), ALU.subtract)
        P.tt("pool", DD.all(), DD.all(), maskD.all(), ALU.add)
        P.act(EE.all(), DD.all(), AF.Exp)
        if STOP == 6:
            continue
        P.tt("dve", Pb[0][:, 0], pKQ[:, 0], EE[:, 0], ALU.mult)
        P.tt("dve", ATf.all(), pKQ[:, 0], EE[:, 1], ALU.mult)
        P.tt("dve", QKm.all(), pKQ[:, 1], EE[:, 2], ALU.mult)
        P.cp("pool", Pb[0][:, 1], ATf.all())
        P.tt("pool", TTf.all(), bcm(identfold.all(), 4), ATf.all(), ALU.subtract)
        P.cp("pool", TTb.all(), TTf.all())
        if STOP == 7:
            continue

        def tail():
            cur = 0
            NL = 5
            for lvl in range(NL):
                last = lvl == NL - 1
                pN = P.ps(nbt(), [128, 2, 4, 64])
                Pc, Pn = Pb[cur], Pb[1 - cur]
                for c in range(2):
                    cs_ = slice(c * 64, (c + 1) * 64)
                    for h in range(4):
                        P.mm(pN[cs_, 0, h, :], Pc[cs_, 1, h, :], Pc[cs_, 0, h, :])
                        if not last:
                            P.mm(pN[cs_, 1, h, :], Pc[cs_, 0, h, :], Pc[cs_, 1, h, :])
                if last:
                    P.cp("act", Pn[:, 0], pN[:, 0])
                else:
                    P.cp("act", Pn.all(), pN.all())
                    yield
                pT_ = P.ps(nbt(), [128, 4, 64])
                for c in range(2):
                    cs_ = slice(c * 64, (c + 1) * 64)
                    for h in range(4):
                        P.mm(pT_[cs_, h, :], Pn[cs_, 0, h, :], TTb[cs_, h, :])
                P.tt("dve", TTf.all(), TTf.all(), pT_.all(), ALU.add)
                P.cp("pool", TTb.all(), TTf.all())
                yield
                cur = 1 - cur
            pU = P.ps(nbt(), [128, 4, 128])
            pW = P.ps(nbt(), [128, 2, 4, 64])
            for c in range(2):
                cs_ = slice(c * 64, (c + 1) * 64)
                for h in range(4):
                    P.mm(pU[cs_, h, :], TTb[cs_, h, :], vb[cs_, h, :])
                    P.mm(pW[:, c, h, :], kbg[cs_, h, :], TTb[cs_, h, :])
            P.cp("act", uS.all(), pU.all())
            P.cp("act", wT.all(), pW.all())
            yield
            for c in range(2):
                cs_ = slice(c * 64, (c + 1) * 64)
                pWS = P.ps(nbt(), [128, 4, 128])
                pQS = P.ps(nbt(), [128, 4, 128])
                for h in range(4):
                    P.mm(pWS[cs_, h, :], wT[:, c, h, :], Sb[:, h, :])
                    P.mm(pQS[cs_, h, :], kqT[:, 4 + h, cs_], Sb[:, h, :])
                P.tt("dve", vn[cs_], uS[cs_], pWS[cs_], ALU.subtract)
                P.tt("dve", tmpo[cs_], pQS[cs_], bc3(sc[cs_, 20:24], 128), ALU.mult)
                yield
                pO = P.ps(nbt(), [128, 4, 128])
                pS = P.ps(nbt(), [128, 4, 128])
                for h in range(4):
                    P.mm(pO[cs_, h, :], QKm[cs_, h, :], vn[cs_, h, :])
                    P.mm(pS[:, h, :], kd[cs_, h, :], vn[cs_, h, :])
                P.tt("dve", o_raw[cs_], tmpo[cs_], pO[cs_], ALU.add)
                yield
                P.tt("pool", Sf.all(), Sf.all(), bc3(sc[:, 28 + 4 * c:32 + 4 * c], 128), ALU.mult)
                P.tt("dve", Sf.all(), Sf.all(), pS.all(), ALU.add)
                P.cp("pool", Sb.all(), Sf.all())
                yield
            pz = P.ps(nbt(), [128, 512])
            for k in range(8):
                P.mm(pz.all(), xnT[:, k, :], win[:, k, C_Z:C_Z + 512], start=(k == 0), stop=(k == 7))
            P.act(zs.all(), pz.all(), AF.Silu)
            yield
            for h in range(4):
                P.act(junk[:, 0:128], o_raw[:, h, :], AF.Square, accum=sc[:, 52 + h:53 + h])
            rstd_of(P, sc[:, 56:60], sc[:, 52:56], 128, sc[:, 60:64])
            yield
            P.tt("dve", tmpo.all(), o_raw.all(), bc3(sc[:, 56:60], 128), ALU.mult)
            P.tt("pool", tmpo.all(), tmpo.all(), bcm(ognB.all(), 4), ALU.mult)
            P.tt("dve", oa.all().w(lambda a: a.rearrange("p a b -> p (a b)")), tmpo.all().w(lambda a: a.rearrange("p a b -> p (a b)")), zs.all(), ALU.mult)
            if "o_a" in g.dbg:
                P.tt("pool", tmpo.all().w(lambda a: a.rearrange("p a b -> p (a b)")), tmpo.all().w(lambda a: a.rearrange("p a b -> p (a b)")), zs.all(), ALU.mult)
                P.dma("sp", g.dbg["o_a"][r0:r0 + 128, :], tmpo.all().w(lambda a: a.rearrange("p a b -> p (a b)")))
            if "o_raw" in g.dbg:
                P.dma("sp", g.dbg["o_raw"][r0:r0 + 128, :], o_raw.all().w(lambda a: a.rearrange("p a b -> p (a b)")))
            pOT = P.ps(nbt(), [128, 4, 128], BF16)
            for h in range(4):
                P.tr(pOT[:, h, :], oa[:, h, :], identb.all())
            P.cp("act", oaT.all(), pOT.all())
            P.dma("sp", g.oaT_d[r0:r0 + 128, :], oaT.all().w(lambda a: a.rearrange("p a b -> p (a b)")))
            yield

        tg = tail()
        if do_nsa:
            nsa_tile(P, g, g1, gt, xnT, win, nb, bg=tg)
        for _ in tg:
            pass
    P.release()


def shared_inputs(inp):
    f = lambda a: np.ascontiguousarray(np.asarray(a, np.float32))
    cw = np.asarray(inp["gdn_conv_w"], np.float32)[0]
    convw = cw.reshape(4, 12, 128).transpose(2, 1, 0).reshape(128, 48)
    d = {
        "w_in": f(inp["w_in"][0]), "w_gu": f(inp["w_gate_up"][0]), "w_down": f(inp["w_down"][0]),
        "w_a": f(inp["w_branch_gdn"][0]), "w_b": f(inp["w_branch_nsa"][0]), "w_out": f(inp["w_out"][0]),
        "w1k": f(np.asarray(inp["cmp_w1_k"])[0].reshape(2048, 128)), "w1v": f(np.asarray(inp["cmp_w1_v"])[0].reshape(2048, 128)),
        "w2k": f(inp["cmp_w2_k"][0]), "w2v": f(inp["cmp_w2_v"][0]),
        "posk": f(np.asarray(inp["cmp_pos_k"])[0].T), "posv": f(np.asarray(inp["cmp_pos_v"])[0].T),
        "gain1": f(inp["mix_norm_gain"]).reshape(1, D), "gain2": f(inp["ffn_norm_gain"]).reshape(1, D),
        "gain3": f(inp["final_norm_gain"]).reshape(1, D), "onorm": f(inp["gdn_out_norm_gain"]).reshape(1, 128),
        "alog": f(inp["gdn_a_log"]).reshape(1, 4), "dtb": f(inp["gdn_dt_bias"]).reshape(1, 4),
        "convw": f(convw), "cf": _HC[0], "cb": _HC[1],
    }
    return d


def nsa_setup(P, g, n):
    CF, CB = g.CF, g.CB
    n.identb = CB("identb")
    n.step = CB("step")
    n.onehot = CB("onehot")
    n.bigexp = CB("bigexp")
    n.caus = CB("caus")
    n.winlo = CB("winlo")
    n.keep = CF("keep")
    n.ovr = CF("ovr")
    n.cosk = CF("cosk", NT, 8)
    n.sink = CF("sink", NT, 8)
    n.w1 = []
    n.w2 = []
    n.bias = []
    pb = P.ps(7, [128, 2])
    for kv, (w1d, w2d, posd, nm) in enumerate(((g.w1k, g.w2k, g.posk, "w1k"), (g.w1v, g.w2v, g.posv, "w1v"))):
        w1 = P.sb([128, 32, 128], BF16)
        P.dma("pool", w1[0:64], V(w1d.ap.rearrange("(l d) h -> d l h", d=64), (nm, 0, 2048, 0, 128)))
        w2 = P.sb([128, 64], BF16)
        P.dma("pool", w2.all(), w2d.all())
        pt = P.sb([128, 32], BF16)
        P.dma("pool", pt[0:64], posd.all())
        for l in range(32):
            P.mm(pb[:, kv:kv + 1], w1[0:64, l, :], pt[0:64, l:l + 1], start=(l == 0), stop=(l == 31))
        n.w1.append(w1)
        n.w2.append(w2)
    bias = P.sb([128, 2], F32)
    P.cp("act", bias.all(), pb.all())
    n.biasv = bias
    n.ksT = P.sb([128, S], BF16)
    n.qTz = [P.sb([128, 4, 128], BF16) for _ in range(2)]
    P.memset("pool", n.qTz[0].all(), 0.0)
    P.memset("pool", n.qTz[1].all(), 0.0)
    n.vsE = P.sb([128, NT, 2, 65], BF16)
    n.kwT = P.sb([128, 2, 5, 128], BF16)
    n.vwE = P.sb([128, 5, 2, 65], BF16)
    n.kcr = P.sb([128, 2, 144], BF16)
    n.vcr = P.sb([128, 2, 144], BF16)
    n.kcT = P.sb([128, 2, 256], BF16)
    n.vcT = P.sb([128, 2, 256], BF16)
    n.vcE = P.sb([128, 2, 2, 65], BF16)
    P.memset("pool", n.vsE.all(), 1.0)
    P.memset("pool", n.vwE.all(), 1.0)
    P.memset("pool", n.vcE.all(), 1.0)
    n.q_r = P.sb([128, 8, 64], BF16)
    n.kr1 = P.sb([128, 8, 64], BF16)
    n.kr2 = P.sb([128, 2, 64], BF16)
    n.vraw = P.sb([128, 2, 64], BF16)
    n.rt = P.sb([128, 4, 8, 8], F32)
    n.gsig = P.sb([128, 2, 4, 3], F32)
    n.qT = P.sb([128, 8, 128], BF16)
    n.hid = P.sb([128, 2, 2, 8], BF16)
    n.esc = P.sb([128, 4, 256], F32)
    n.den = P.sb([128, 16], F32)
    n.psp = P.sb([128, 260], F32)
    n.imp = P.sb([128, 64], F32)
    n.score = P.sb([128, 64], F32)
    n.score2 = P.sb([128, 64], F32)
    n.m8 = P.sb([128, 16], F32)
    n.negsel = P.sb([128, 64], BF16)
    n.nselT4 = P.sb([128, 4, 128], BF16)
    n.eT = [P.sb([128, 512], BF16) for _ in range(2)]
    n.ob_f = P.sb([128, 8, 64], F32)
    n.obtmp = P.sb([128, 4, 64], F32)
    n.ob = P.sb([128, 512], BF16)
    n.obT = P.sb([128, 4, 128], BF16)
    n.ei = 0


def rope(P, n, src, dst, H, cos, sin, rest_scale):
    rt = n.rt
    c = bcm(cos, H)
    s = bcm(sin, H)
    P.tt("dve", rt[:, 0, 0:H, :], src[:, :, 0:8], c, ALU.mult)
    P.tt("dve", rt[:, 1, 0:H, :], src[:, :, 8:16], s, ALU.mult)
    P.tt("dve", rt[:, 2, 0:H, :], src[:, :, 8:16], c, ALU.mult)
    P.tt("dve", rt[:, 3, 0:H, :], src[:, :, 0:8], s, ALU.mult)
    P.tt("pool", dst[:, :, 0:8], rt[:, 0, 0:H, :], rt[:, 1, 0:H, :], ALU.subtract)
    P.tt("pool", dst[:, :, 8:16], rt[:, 2, 0:H, :], rt[:, 3, 0:H, :], ALU.add)
    P.act(dst[:, :, 16:64], src[:, :, 16:64], AF.Copy, scale=rest_scale)


def nsa_tile(P, g, n, gt, xnT, win, nb, bg=None):
    b, i = divmod(gt, NT)
    r0 = gt * 128
    identb = n.identb
    slot = i % 5
    if i == 0:
        P.memset("pool", n.kcT.all(), 0.0)
        P.memset("pool", n.vcT.all(), 0.0)
        P.memset("pool", n.vcE[:, :, :, 0:64], 0.0)
        P.memset("pool", n.kcr.all(), 0.0)
        P.memset("pool", n.vcr.all(), 0.0)
    pn = []
    for (c0, w) in ((C_Q, 512), (C_KC, 512), (C_KC + 512, 280)):
        pt = P.ps(nb(), [128, 512])
        for k in range(8):
            P.mm(pt[:, 0:w], xnT[:, k, :], win[:, k, c0:c0 + w], start=(k == 0), stop=(k == 7))
        pn.append(pt)

    def v3(t, c0, H):
        return T("ps", t.ap[:, c0:c0 + H * 64].rearrange("p (h d) -> p h d", d=64), [128, H, 64], 4, t.boff + c0 * 4)

    rope(P, n, v3(pn[0], 0, 8), n.q_r, 8, n.cosk[:, i, :], n.sink[:, i, :], 1.0)
    rope(P, n, v3(pn[1], 0, 8), n.kr1, 8, n.cosk[:, i, :], n.sink[:, i, :], 1.0)
    rope(P, n, v3(pn[2], 0, 2), n.kr2, 2, n.cosk[:, i, :], n.sink[:, i, :], 1.0)
    P.cp("act", n.vraw.all(), v3(pn[1], 128, 2).all())
    P.cp("act", n.vsE[:, i, :, 0:64], v3(pn[1], 384, 2).all())
    P.cp("act", n.vwE[:, slot, :, 0:64], v3(pn[2], 128, 2).all())
    gs = n.gsig.all().w(lambda a: a.rearrange("p a b c -> p (a b c)"))
    P.act(gs, pn[2][:, 256:280], AF.Exp, scale=-1.0)
    P.ts("dve", gs, gs, 1.0, ALU.add)
    P.recip(gs, gs)
    if NSTOP == 1:
        return
    pqT = P.ps(nb(), [128, 8, 128], BF16)
    for h in range(8):
        P.tr(pqT[0:64, h, :], n.q_r[:, h, :], identb.all())
    P.act(n.qT[0:64], pqT[0:64], AF.Copy, scale=0.125)
    pq2 = P.ps(nb(), [128, 4, 128], BF16)
    for r in range(4):
        P.tr(pq2[0:64, r, :], n.q_r[:, r, :], identb.all())
        P.tr(pq2[64:128, r, :], n.q_r[:, 4 + r, :], identb.all())
    P.act(n.qTz[0][0:64], pq2[0:64], AF.Copy, scale=0.125)
    P.act(n.qTz[1][64:128], pq2[64:128], AF.Copy, scale=0.125)
    pks = P.ps(nb(), [128, 128], BF16)
    P.tr(pks[0:64, :], n.kr1[:, 4, :], identb.all())
    P.tr(pks[64:128, :], n.kr1[:, 5, :], identb.all())
    P.cp("dve", n.ksT[:, i * 128:(i + 1) * 128], pks.all())
    pkT = P.ps(nb(), [128, 8, 128], BF16)
    srcs = [n.kr1[:, 0, :], n.kr1[:, 1, :], n.vraw[:, 0, :], n.vraw[:, 1, :], n.kr1[:, 4, :], n.kr1[:, 5, :], n.kr2[:, 0, :], n.kr2[:, 1, :]]
    for h in range(8):
        P.tr(pkT[0:64, h, :], srcs[h], identb.all())
    P.cp("dve", n.kcr[0:64, :, 16:144], pkT[0:64, 0:2, :])
    P.cp("dve", n.vcr[0:64, :, 16:144], pkT[0:64, 2:4, :])
    P.cp("act", n.kwT[0:64, :, slot, :], pkT[0:64, 6:8, :])
    if NSTOP == 2:
        return
    n0 = 8 * i - 1 if i > 0 else 0
    nn = 8 if i > 0 else 7
    m0 = 0 if i > 0 else 1
    ph = P.ps(nb(), [128, 2, 2, 8])
    for kv, cr in enumerate((n.kcr, n.vcr)):
        for l in range(32):
            P.mm(ph[:, kv, :, :], n.w1[kv][0:64, l, :], cr[0:64, :, l:l + 113:16], start=(l == 0), stop=(l == 31))
        P.act(n.hid[:, kv, :, :], ph[:, kv, :, :], AF.Silu, bias=n.biasv[:, kv:kv + 1])
    pk2 = P.ps(nb(), [128, 2, 2, 8])
    for kv in range(2):
        P.mm(pk2[0:64, kv, :, :], n.w2[kv].all(), n.hid[:, kv, :, :])
    P.cp("act", n.kcT[0:64, :, n0:n0 + nn], pk2[0:64, 0, :, m0:8])
    P.cp("act", n.vcT[0:64, :, n0:n0 + nn], pk2[0:64, 1, :, m0:8])
    P.cp("pool", n.kcr[0:64, :, 0:16], n.kcr[0:64, :, 128:144])
    P.cp("pool", n.vcr[0:64, :, 0:16], n.vcr[0:64, :, 128:144])
    for ch in sorted(set((n0 // 128, (n0 + nn - 1) // 128))):
        pvc = P.ps(nb(), [128, 2, 64], BF16)
        for gq in range(2):
            P.tr(pvc[:, gq, :], n.vcT[0:64, gq, ch * 128:(ch + 1) * 128], identb[0:64, 0:64])
        P.cp("act", n.vcE[:, ch, :, 0:64], pvc.all())
        if NSTOP == 3:
            pass
    if NSTOP == 3:
        return
    nch = 1 if 8 * i + 7 <= 128 else 2
    NC = nch * 128
    sh = 384 - 8 * i
    for gq in range(2):
        qT4 = n.qT[0:64, gq * 4:(gq + 1) * 4, :].w(lambda a: a.rearrange("p a b -> p (a b)"))
        if nch == 1:
            bankA = nb()
            regs = [P.ps(bankA, [128, 4, 128])[:, r, :] for r in range(4)]
        else:
            bankA, bankB = nb(), nb()
            regs = [P.ps(bankA if r < 2 else bankB, [128, 2, 256])[:, r % 2, :] for r in range(4)]
        for r in range(4):
            P.mm(regs[r], n.qT[0:64, gq * 4 + r, :], n.kcT[0:64, gq, 0:NC], start=True, stop=False)
            P.mm(regs[r], n.onehot[0:9, 0:128], n.step[0:9, sh:sh + NC], start=False, stop=True)
        for r in range(4):
            P.act(n.esc[:, r, 0:NC], regs[r], AF.Exp, accum=n.den[:, r:r + 1])
            if NSTOP == 4:
                pass
        if NSTOP == 4:
            return
        P.ts("dve", n.den[:, 4:8], n.den[:, 0:4], 1e-30, ALU.max)
        P.recip(n.den[:, 4:8], n.den[:, 4:8])
        P.memset("pool", n.psp.all(), 0.0)
        P.ts("dve", n.psp[:, 1:1 + NC], n.esc[:, 0, 0:NC], n.den[:, 4:5], ALU.mult)
        for r in range(1, 4):
            P.stt(n.psp[:, 1:1 + NC], n.esc[:, r, 0:NC], n.den[:, 4 + r:5 + r], n.psp[:, 1:1 + NC], ALU.mult, ALU.add)
        P.tt("dve", n.imp.all(), n.psp[:, 0:256:4], n.psp[:, 1:257:4], ALU.add)
        for m in range(2, 5):
            P.tt("dve", n.imp.all(), n.imp.all(), n.psp[:, m:m + 256:4], ALU.add)
        P.tt("dve", n.score.all(), n.imp.all(), n.keep[:, 64 - 2 * i:128 - 2 * i], ALU.mult)
        P.tt("dve", n.score.all(), n.score.all(), n.ovr[:, 64 - 2 * i:128 - 2 * i], ALU.add)
        P.memset("dve", n.score[:, 0:1], 1000.0)
        if NSTOP == 5:
            return
        sc_, sc2, m8 = n.score.all(), n.score2.all(), n.m8
        m8a, m8b = m8[:, 0:8], m8[:, 8:16]
        P.op("dve", lambda e, o=m8a, s=sc_: e.max(out=o.ap, in_=s.ap), [sc_], [m8a])
        P.op("dve", lambda e, o=sc2, a=m8a, s=sc_: e.match_replace(out=o.ap, in_to_replace=a.ap, in_values=s.ap, imm_value=-2.0), [m8a, sc_], [sc2])
        P.op("dve", lambda e, o=m8b, s=sc2: e.max(out=o.ap, in_=s.ap), [sc2], [m8b])
        P.ts("dve", n.negsel.all(), n.score.all(), m8[:, 15:16], ALU.is_lt, NEG, ALU.mult)
        if NSTOP == 6:
            return
        pns = P.ps(nb(), [128, 128], BF16)
        P.tr(pns[0:64, :], n.negsel.all(), identb.all())
        P.cp("act", n.nselT4[0:64], bcm(pns[0:64, :], 4))
        nsel = n.nselT4[0:64].w(lambda a: a.rearrange("p a b -> p (a b)"))
        if NSTOP == 7:
            return

        def branch(chunks, kfn, vfn, maskfn, br, qop=None):
            qop = qT4 if qop is None else qop
            po = P.ps(gq, [128, 4, 65])
            first = True
            for ci, kc_ in enumerate(chunks):
                lastc = ci == len(chunks) - 1
                pst = P.ps(nb(), [128, 512])
                masks = maskfn(kc_)
                P.mm(pst.all(), kfn(kc_), qop, start=True, stop=(len(masks) == 0))
                for mi, (ml, mr) in enumerate(masks):
                    P.mm(pst.all(), ml, mr, start=False, stop=(mi == len(masks) - 1))
                if bg is not None:
                    next(bg, None)
                eT = n.eT[n.ei % 2]
                n.ei += 1
                P.act(eT.all(), pst.all(), AF.Exp)
                for r in range(4):
                    P.mm(po[:, r, :], eT[:, r * 128:(r + 1) * 128], vfn(kc_), start=first, stop=(lastc and r == 3), skip_group_check=True)
                    first = False
            P.ts("dve", n.den[:, 8:12], po[:, :, 64], 1e-30, ALU.max)
            P.recip(n.den[:, 8:12], n.den[:, 8:12])
            P.tt("dve", n.den[:, 8:12], n.den[:, 8:12], n.gsig[:, gq, :, br], ALU.mult)
            dst = n.ob_f[:, gq * 4:(gq + 1) * 4, :]
            if br == 0:
                P.tt("dve", dst, po[:, :, 0:64], bc3(n.den[:, 8:12], 64), ALU.mult)
            else:
                P.tt("dve", n.obtmp.all(), po[:, :, 0:64], bc3(n.den[:, 8:12], 64), ALU.mult)
                P.tt("pool", dst, dst, n.obtmp.all(), ALU.add)

        branch(list(range(nch)),
               lambda c: n.kcT[0:64, gq, c * 128:(c + 1) * 128],
               lambda c: n.vcE[:, c, gq, :],
               lambda c: [(n.step[0:9, sh + c * 128:sh + (c + 1) * 128], n.onehot[0:9, :])], 0)
        if NSTOP == 8:
            return
        branch(list(range(i + 1)),
               lambda c: n.ksT[:, c * 128:(c + 1) * 128],
               lambda c: n.vsE[:, c, gq, :],
               lambda c: [(n.bigexp[0:64, c * 128:(c + 1) * 128], nsel)] + ([(identb.all(), n.caus.all())] if c == i else []), 1,
               qop=n.qTz[gq].all().w(lambda a: a.rearrange("p a b -> p (a b)")))
        if NSTOP == 9:
            return
        branch(list(range(max(0, i - 4), i + 1)),
               lambda c: n.kwT[0:64, gq, c % 5, :],
               lambda c: n.vwE[:, c % 5, gq, :],
               lambda c: ([(identb.all(), n.caus.all())] if c == i else []) + ([(identb.all(), n.winlo.all())] if c == i - 4 else []), 2)
        if NSTOP == 10:
            return
    P.cp("act", n.ob.all(), n.ob_f.all().w(lambda a: a.rearrange("p a b -> p (a b)")))
    if "o_b" in g.dbg:
        P.dma("sp", g.dbg["o_b"][r0:r0 + 128, :], n.ob_f.all().w(lambda a: a.rearrange("p a b -> p (a b)")))
    pOT = P.ps(nb(), [128, 4, 128], BF16)
    for k in range(4):
        P.tr(pOT[:, k, :], n.ob[:, k * 128:(k + 1) * 128], identb.all())
    P.cp("act", n.obT.all(), pOT.all())
    P.dma("sp", g.obT_d[r0:r0 + 128, :], n.obT.all().w(lambda a: a.rearrange("p a b -> p (a b)")))


def phase2(P, g, ntiles):
    P.mark()
    identb = g.CB("identb")
    wm = P.sb([128, 8, 2048], BF16)
    cast_load(P, wm, g.w_in, 0, C_MA, 2048, 8, step=1024)
    wa = P.sb([128, 4, D], BF16)
    cast_load(P, wa, g.w_a, 0, 0, D, 4)
    wb = P.sb([128, 4, D], BF16)
    cast_load(P, wb, g.w_b, 0, 0, D, 4)
    wo = P.sb([128, 8, D], BF16)
    cast_load(P, wo, g.w_out, 0, 0, D, 8)
    xnT = P.sb([128, 8, 128], BF16)
    oaT = P.sb([128, 4, 128], BF16)
    obT = P.sb([128, 4, 128], BF16)
    xt = P.sb([128, D], F32)
    sg = P.sb([128, 2048], F32)
    m1 = P.sb([128, 512], F32)
    m2 = P.sb([128, 512], F32)
    mg = P.sb([128, D], BF16)
    mT = P.sb([128, 8, 128], BF16)
    x1t = P.sb([128, D], F32)
    bk = [0]

    def nb():
        b = bk[0]
        bk[0] = (bk[0] + 1) % 8
        return b

    fl = lambda a: a.rearrange("p a b -> p (a b)")
    for gt in range(ntiles):
        r0 = gt * 128
        P.dma("sp", xnT.all().w(fl), g.xnT_d[r0:r0 + 128, :])
        P.dma("sp", oaT.all().w(fl), g.oaT_d[r0:r0 + 128, :])
        P.dma("sp", obT.all().w(fl), g.obT_d[r0:r0 + 128, :])
        P.dma("sp", xt.all(), g.x[r0:r0 + 128, :])
        for j in range(4):
            pg = P.ps(nb(), [128, 512])
            for k in range(8):
                P.mm(pg.all(), xnT[:, k, :], wm[:, k, j * 512:(j + 1) * 512], start=(k == 0), stop=(k == 7))
            P.act(sg[:, j * 512:(j + 1) * 512], pg.all(), AF.Exp, scale=-1.0)
        P.ts("pool", sg.all(), sg.all(), 1.0, ALU.add)
        P.recip(sg.all(), sg.all())
        for j in range(2):
            pa = P.ps(nb(), [128, 512])
            pb = P.ps(nb(), [128, 512])
            for k in range(4):
                P.mm(pa.all(), oaT[:, k, :], wa[:, k, j * 512:(j + 1) * 512], start=(k == 0), stop=(k == 3))
            for k in range(4):
                P.mm(pb.all(), obT[:, k, :], wb[:, k, j * 512:(j + 1) * 512], start=(k == 0), stop=(k == 3))
            P.tt("dve", m1.all(), pa.all(), sg[:, j * 512:(j + 1) * 512], ALU.mult)
            P.tt("dve", m2.all(), pb.all(), sg[:, 1024 + j * 512:1024 + (j + 1) * 512], ALU.mult)
            P.tt("pool", mg[:, j * 512:(j + 1) * 512], m1.all(), m2.all(), ALU.add)
        if "merged" in g.dbg:
            P.tt("pool", m1.all(), m1.all(), m2.all(), ALU.add)
            P.dma("sp", g.dbg["merged"][r0:r0 + 128, 512:1024], m1.all())
        pmT = P.ps(nb(), [128, 8, 128], BF16)
        for k in range(8):
            P.tr(pmT[:, k, :], mg[:, k * 128:(k + 1) * 128], identb.all())
        P.cp("act", mT.all(), pmT.all())
        for j in range(2):
            po = P.ps(nb(), [128, 512])
            for k in range(8):
                P.mm(po.all(), mT[:, k, :], wo[:, k, j * 512:(j + 1) * 512], start=(k == 0), stop=(k == 7))
            P.tt("dve", x1t[:, j * 512:(j + 1) * 512], xt[:, j * 512:(j + 1) * 512], po.all(), ALU.add)
        P.dma("sp", g.x1_d[r0:r0 + 128, :], x1t.all())
        if "x1" in g.dbg:
            P.dma("sp", g.dbg["x1"][r0:r0 + 128, :], x1t.all())
    P.release()


def phase3(P, g, ntiles, ST=256):
    P.sb_off = 0
    identb = P.sb([128, 128], BF16)
    P.dma("sp", identb.all(), g.cb[:, 0:128])
    wgu = P.sb([128, 8, 2 * FH], BF16)
    cast_load(P, wgu, g.w_gu, 0, 0, 2 * FH, 8, step=1408)
    wd = P.sb([128, 22, D], BF16)
    cast_load(P, wd, g.w_down, 0, 0, D, 22)
    gain2B = P.sb([128, D], F32)
    P.dma("sp", gain2B.all(), g.gain2[0:1, :].bc([128, D]))
    gain3B = P.sb([128, D], F32)
    P.dma("sp", gain3B.all(), g.gain3[0:1, :].bc([128, D]))
    nsub = ST // 128
    x1s = [P.sb([128, D], F32) for _ in range(nsub)]
    xn = P.sb([128, D], BF16)
    xnT = P.sb([128, 8, 128], BF16)
    xn2T = P.sb([128, 8, ST], BF16)
    hT = P.sb([128, 22, ST], BF16)
    sgt = P.sb([128, ST], F32)
    x2 = P.sb([128, D], F32)
    outt = P.sb([128, D], F32)
    sm = P.sb([128, 8], F32)
    bk = [0]

    def nb():
        b = bk[0]
        bk[0] = (bk[0] + 1) % 8
        return b

    for st in range(ntiles * 128 // ST):
        for sub in range(nsub):
            r0 = st * ST + sub * 128
            P.dma("sp", x1s[sub].all(), g.x1_d[r0:r0 + 128, :])
            norm_transpose(P, g, x1s[sub], gain2B, xn, xnT, None, sm, identb, nb())
            P.cp("pool", xn2T[:, :, sub * 128:(sub + 1) * 128], xnT.all())
        for f in range(22):
            pgt = P.ps(nb(), [128, ST])
            pup = P.ps(nb(), [128, ST])
            for k in range(8):
                P.mm(pgt.all(), wgu[:, k, f * 128:(f + 1) * 128], xn2T[:, k, :], start=(k == 0), stop=(k == 7))
            for k in range(8):
                P.mm(pup.all(), wgu[:, k, FH + f * 128:FH + (f + 1) * 128], xn2T[:, k, :], start=(k == 0), stop=(k == 7))
            P.act(sgt.all(), pgt.all(), AF.Silu)
            P.tt("dve", hT[:, f, :], sgt.all(), pup.all(), ALU.mult)
        for sub in range(nsub):
            r0 = st * ST + sub * 128
            for j in range(2):
                pd = P.ps(nb(), [128, 512])
                for f in range(22):
                    P.mm(pd.all(), hT[:, f, sub * 128:(sub + 1) * 128], wd[:, f, j * 512:(j + 1) * 512], start=(f == 0), stop=(f == 21))
                P.tt("dve", x2[:, j * 512:(j + 1) * 512], x1s[sub][:, j * 512:(j + 1) * 512], pd.all(), ALU.add)
            P.act(outt.all(), x2.all(), AF.Square, accum=sm[:, 4:5])
            rstd_of(P, sm[:, 5:6], sm[:, 4:5], D, sm[:, 6:7])
            P.stt(outt.all(), x2.all(), sm[:, 5:6], gain3B.all(), ALU.mult, ALU.mult)
            P.dma("sp", g.out[r0:r0 + 128, :], outt.all())


def build_program(nseq=2, ntiles=None, dbg=None):
    nc = bass.Bass("TRN2", target_bir_lowering=False)
    P = Prog(nc)
    g = setup_common(P, nseq, dbg or {})
    load_consts(P, g)
    nt = nseq * NT if ntiles is None else ntiles
    ph = os.environ.get("K_PH", "123")
    phase1(P, g, nt, do_nsa=("n" not in ph))
    if "2" in ph:
        phase2(P, g, nt)
    if "3" in ph:
        phase3(P, g, nt)
    P.finalize()
    return nc, P


def kernel(**inputs):
    x = np.asarray(inputs["x"], np.float32)
    B = x.shape[0]
    ncore = 8
    nseq = B // ncore
    nc, P = build_program(nseq=nseq)
    sh = shared_inputs(inputs)
    in_maps = []
    for c in range(ncore):
        m = dict(sh)
        m["x"] = np.ascontiguousarray(x[c * nseq:(c + 1) * nseq].reshape(nseq * S, D))
        in_maps.append(m)
    res = run_bass_kernel_spmd(nc, in_maps, core_ids=list(range(ncore)))
    out = np.stack([np.asarray(r["out"], np.float32).reshape(nseq, S, D) for r in res.results], 0)
    return out.reshape(B, S, D)
```

```python
import numpy as np
import ml_dtypes
import concourse.bass as bass
import concourse.mybir as mybir
from concourse.bass_utils import run_bass_kernel_spmd

F32 = mybir.dt.float32
BF16 = mybir.dt.bfloat16
U8 = mybir.dt.uint8
ALU = mybir.AluOpType
AF = mybir.ActivationFunctionType
DSZ = {F32: 4, BF16: 2, U8: 1}

ENGS = ("pe", "act", "dve", "pool", "sp")
DMA_POOL = 6
NEG = -30000.0


class V:
    __slots__ = ("ap", "reg")

    def __init__(self, ap, reg):
        self.ap = ap
        self.reg = reg

    def w(self, fn):
        return V(fn(self.ap), self.reg)

    def bc(self, shape):
        return V(self.ap.broadcast_to(list(shape)), self.reg)


class T:
    def __init__(self, space, ap, shape, esz, boff):
        self.space = space
        self.ap = ap
        self.shape = tuple(shape)
        self.esz = esz
        self.boff = boff
        fs = [1] * len(shape)
        for i in range(len(shape) - 2, 0, -1):
            fs[i] = fs[i + 1] * shape[i + 1]
        self.fstr = fs

    def __getitem__(self, idx):
        if not isinstance(idx, tuple):
            idx = (idx,)
        idx = idx + (slice(None),) * (len(self.shape) - len(idx))
        lo, hi = [], []
        for ix, n in zip(idx, self.shape):
            if isinstance(ix, slice):
                a, b, st = ix.indices(n)
                assert st > 0 and b > a, (self.space, idx, self.shape)
                lo.append(a)
                hi.append(a + ((b - a - 1) // st) * st)
            else:
                assert 0 <= ix < n, (self.space, idx, self.shape)
                lo.append(ix)
                hi.append(ix)
        f0 = sum(l * s for l, s in zip(lo[1:], self.fstr[1:]))
        f1 = sum(h * s for h, s in zip(hi[1:], self.fstr[1:])) + 1
        p0, p1 = lo[0], hi[0] + 1
        b0, b1 = self.boff + f0 * self.esz, self.boff + f1 * self.esz
        if self.space == "ps":
            p0, p1 = p0 // 32 * 32, (p1 + 31) // 32 * 32
            b0, b1 = b0 // 2048 * 2048, (b1 + 2047) // 2048 * 2048
        return V(self.ap[idx], (self.space, p0, p1, b0, b1))

    def all(self):
        return self[tuple(slice(None) for _ in self.shape)]


def _ovl(a, b):
    return a[0] == b[0] and a[1] < b[2] and b[1] < a[2] and a[3] < b[4] and b[3] < a[4]


def _contains(a, b):
    return a[0] == b[0] and a[1] <= b[1] and b[2] <= a[2] and a[3] <= b[3] and b[4] <= a[4]


class Op:
    __slots__ = ("eng", "emit", "idx", "deps", "dma", "tok", "sig", "eidx")


class Prog:
    def __init__(self, nc, sb_bytes=204288):
        self.nc = nc
        self.ops = []
        self.acc = {}
        self.sb_h = nc.alloc_sbuf_tensor("arena", [128, sb_bytes], U8)
        self.ps_h = nc.alloc_psum_tensor("psarena", [128, 4096], F32)
        self.sb_bytes = sb_bytes
        self.sb_off = 0
        self.marks = []

    def sb(self, shape, dtype, align=32):
        esz = DSZ[dtype]
        n = int(np.prod(shape[1:]))
        off = (self.sb_off + align - 1) // align * align
        nb = n * esz
        assert off + nb <= self.sb_bytes, ("SBUF arena overflow", off, nb)
        self.sb_off = off + nb
        ap = self.sb_h[:, off:off + nb].bitcast(dtype)
        if len(shape) > 2:
            names = " ".join("d%d" % i for i in range(1, len(shape)))
            ap = ap.rearrange("p (%s) -> p %s" % (names, names), **{"d%d" % i: shape[i] for i in range(1, len(shape))})
        return T("sb", ap, [128] + list(shape[1:]), esz, off)

    def mark(self):
        self.marks.append(self.sb_off)

    def release(self):
        self.sb_off = self.marks.pop()

    def ps(self, bank, shape, dtype=F32, boff=0):
        esz = DSZ[dtype]
        n = int(np.prod(shape[1:]))
        nb = n * esz
        assert boff + nb <= 2048
        e0 = (bank * 2048 + boff) // 4
        ap = self.ps_h[:, e0:e0 + (nb + 3) // 4]
        if dtype != F32:
            ap = ap.bitcast(dtype)
        if len(shape) > 2:
            names = " ".join("d%d" % i for i in range(1, len(shape)))
            ap = ap.rearrange("p (%s) -> p %s" % (names, names), **{"d%d" % i: shape[i] for i in range(1, len(shape))})
        return T("ps", ap, [128] + list(shape[1:]), esz, bank * 2048 + boff)

    def dram(self, name, shape, dtype, kind="Internal"):
        h = self.nc.dram_tensor(name, list(shape), dtype, kind=kind)
        return T(name, h, shape, 1, 0)

    def op(self, eng, emit, reads, writes, dma=False):
        o = Op()
        o.eng, o.emit, o.idx, o.dma, o.sig = eng, emit, len(self.ops), dma, False
        deps = set()
        for v in reads:
            r = v.reg
            for (reg, oi, w) in self.acc.setdefault(r[0], []):
                if w and _ovl(reg, r):
                    deps.add(oi)
        for v in writes:
            r = v.reg
            for (reg, oi, w) in self.acc.setdefault(r[0], []):
                if _ovl(reg, r):
                    deps.add(oi)
        deps.discard(o.idx)
        o.deps = deps
        self.ops.append(o)
        for v in writes:
            r = v.reg
            lst = self.acc[r[0]]
            lst[:] = [e for e in lst if not _contains(r, e[0])]
            lst.append((r, o.idx, True))
        for v in reads:
            r = v.reg
            lst = self.acc[r[0]]
            if not dma:
                lst[:] = [e for e in lst if not ((not e[2]) and _contains(r, e[0])
                                                 and (not self.ops[e[1]].dma) and self.ops[e[1]].eng == eng)]
            lst.append((r, o.idx, False))
        return o

    def dma(self, q, out, in_):
        return self.op(q, lambda e: e.dma_start(out=out.ap, in_=in_.ap), [in_], [out], dma=True)

    def mm(self, out, lhsT, rhs, start=True, stop=True, **kw):
        return self.op("pe", lambda e: e.matmul(out.ap, lhsT=lhsT.ap, rhs=rhs.ap, start=start, stop=stop, **kw), [lhsT, rhs], [out])

    def tr(self, out, in_, ident):
        return self.op("pe", lambda e: e.transpose(out.ap, in_.ap, ident.ap), [in_, ident], [out])

    def act(self, out, in_, func, bias=None, scale=None, accum=None, eng="act"):
        rd = [in_]
        kw = {}
        if bias is not None:
            if isinstance(bias, V):
                rd.append(bias)
                kw["bias"] = bias.ap
            else:
                kw["bias"] = float(bias)
        if scale is not None:
            if isinstance(scale, V):
                rd.append(scale)
                kw["scale"] = scale.ap
            else:
                kw["scale"] = float(scale)
        wr = [out]
        if accum is not None:
            wr.append(accum)
            kw["accum_out"] = accum.ap
        return self.op(eng, lambda e: e.activation(out=out.ap, in_=in_.ap, func=func, **kw), rd, wr)

    def tt(self, eng, out, in0, in1, op):
        return self.op(eng, lambda e: e.tensor_tensor(out=out.ap, in0=in0.ap, in1=in1.ap, op=op), [in0, in1], [out])

    def ts(self, eng, out, in0, s1, op0, s2=None, op1=None):
        rd = [in0]
        a1 = s1.ap if isinstance(s1, V) else float(s1)
        if isinstance(s1, V):
            rd.append(s1)
        kw = {}
        if op1 is not None:
            kw["op1"] = op1
            kw["scalar2"] = s2.ap if isinstance(s2, V) else float(s2)
            if isinstance(s2, V):
                rd.append(s2)
        else:
            kw["scalar2"] = None
        return self.op(eng, lambda e: e.tensor_scalar(out=out.ap, in0=in0.ap, scalar1=a1, op0=op0, **kw), rd, [out])

    def stt(self, out, in0, scalar, in1, op0, op1):
        rd = [in0, in1]
        a = scalar.ap if isinstance(scalar, V) else float(scalar)
        if isinstance(scalar, V):
            rd.append(scalar)
        return self.op("dve", lambda e: e.scalar_tensor_tensor(out=out.ap, in0=in0.ap, scalar=a, in1=in1.ap, op0=op0, op1=op1), rd, [out])

    def cp(self, eng, out, in_):
        if eng == "act":
            return self.op("act", lambda e: e.copy(out=out.ap, in_=in_.ap), [in_], [out])
        return self.op(eng, lambda e: e.tensor_copy(out=out.ap, in_=in_.ap), [in_], [out])

    def memset(self, eng, out, val):
        return self.op(eng, lambda e: e.memset(out.ap, val), [], [out])

    def recip(self, out, in_):
        return self.op("dve", lambda e: e.reciprocal(out=out.ap, in_=in_.ap), [in_], [out])

    def finalize(self):
        nc = self.nc
        ops = self.ops

        def pepe(a, b):
            return a.eng == "pe" and b.eng == "pe" and not a.dma and not b.dma

        for o in ops:
            for d in o.deps:
                if not pepe(ops[d], o):
                    ops[d].sig = True
            if o.dma:
                o.sig = True
        eng_sem = {e: nc.alloc_semaphore("s_" + e) for e in ENGS}
        dma_sems = {e: [nc.alloc_semaphore("d_%s_%d" % (e, i)) for i in range(DMA_POOL)] for e in ("sp", "act", "pool")}
        cnt = {e: 0 for e in ENGS}
        dcnt = {e: 0 for e in ENGS}
        for o in ops:
            if o.dma:
                n = dcnt[o.eng]
                dcnt[o.eng] += 1
                o.tok = (("d", o.eng, n % DMA_POOL), 16 * (n // DMA_POOL + 1))
                o.eidx = n
            elif o.sig:
                cnt[o.eng] += 1
                o.tok = (("e", o.eng), cnt[o.eng])
            else:
                o.tok = None

        def semof(k):
            return eng_sem[k[1]] if k[0] == "e" else dma_sems[k[1]][k[2]]

        streams = {e: [] for e in ENGS}
        known = {e: {} for e in ENGS}
        for o in ops:
            need = {}
            for d in o.deps:
                od = ops[d]
                if pepe(od, o):
                    continue
                k, v = od.tok
                if need.get(k, 0) < v:
                    need[k] = v
            if o.dma and o.eidx >= DMA_POOL:
                k = ("d", o.eng, o.eidx % DMA_POOL)
                v = 16 * (o.eidx // DMA_POOL)
                if need.get(k, 0) < v:
                    need[k] = v
            kn = known[o.eng]
            waits = []
            for k, v in need.items():
                if kn.get(k, 0) < v:
                    kn[k] = v
                    waits.append((semof(k), v))
            streams[o.eng].append((waits, o))
        final = []
        for e in ENGS:
            if cnt[e]:
                final.append((eng_sem[e], cnt[e]))
            n = dcnt[e]
            for i in range(min(n, DMA_POOL)):
                last = ((n - 1 - i) // DMA_POOL) * DMA_POOL + i
                final.append((dma_sems[e][i], 16 * (last // DMA_POOL + 1)))
        self.stats = {e: len(streams[e]) for e in ENGS}
        self.stats["sig"] = dict(cnt)
        self.stats["dma"] = dict(dcnt)

        def run(eng_obj, name):
            for waits, o in streams[name]:
                for s, v in waits:
                    eng_obj.wait_ge(s, v)
                ins = o.emit(eng_obj)
                if o.tok is not None:
                    ins.then_inc(semof(o.tok[0]), 16 if o.dma else 1)
            if name == "sp":
                for s, v in final:
                    eng_obj.wait_ge(s, v)

        with nc.Block() as block:
            @block.tensor
            def _(e):
                run(e, "pe")

            @block.scalar
            def _(e):
                run(e, "act")

            @block.vector
            def _(e):
                run(e, "dve")

            @block.gpsimd
            def _(e):
                run(e, "pool")

            @block.sync
            def _(e):
                run(e, "sp")


D = 1024
S = 4096
NT = 32
INW = 5408
FH = 2816
EPS = 1e-6
C_Z = 1536
C_BA = 2048
C_Q = 2056
C_KC = 2568
C_MA = 3360
NW1 = 3360


def _bf(a):
    return np.asarray(a, np.float32).astype(ml_dtypes.bfloat16)


def host_consts():
    p = np.arange(128)
    cf = {}
    cf["identf"] = np.eye(128, dtype=np.float32)
    cf["identfold"] = (p[:, None] % 64 == np.arange(64)[None, :]).astype(np.float32)
    cf["blockones"] = (p[:, None] // 64 == p[None, :] // 64).astype(np.float32)
    cf["ublk"] = ((p[:, None] // 64 == p[None, :] // 64) & (p[:, None] <= p[None, :])).astype(np.float32)
    cm = np.zeros((128, 2, 128), np.float32)
    for c in range(2):
        cm[c * 64:(c + 1) * 64, c, :] = 1.0
    cf["chunkmask"] = cm.reshape(128, 256)
    pl = (p % 64)[:, None]
    xx = np.arange(64)[None, :]
    mA = np.where(xx < pl, 0.0, NEG)
    mAT = np.where(xx > pl, 0.0, NEG)
    mQT = np.where(xx >= pl, 0.0, NEG)
    md = np.stack([np.repeat(m[:, None, :], 4, 1) for m in (mA, mAT, mQT)], 1)
    cf["maskD"] = md.reshape(128, 768).astype(np.float32)
    half = 8
    inv_freq = 500000.0 ** (-np.arange(half, dtype=np.float32) / half)
    pos = np.arange(S, dtype=np.float32)
    ang = pos[:, None] * inv_freq[None, :]
    cos = np.cos(ang).astype(np.float32).reshape(NT, 128, 8).transpose(1, 0, 2).reshape(128, NT * 8)
    sin = np.sin(ang).astype(np.float32).reshape(NT, 128, 8).transpose(1, 0, 2).reshape(128, NT * 8)
    cf["cosk"] = cos
    cf["sink"] = sin
    c = np.arange(128)[None, :]
    jr = c - 64
    cr = (p[:, None] >= 64).astype(np.int64)
    invalid = jr > cr
    forced = (jr == cr) | (jr == cr - 1)
    cf["keep"] = (~(invalid | forced)).astype(np.float32)
    cf["ovr"] = np.where(invalid, -1.0, np.where(forced, 1000.0, 0.0)).astype(np.float32)
    cb = {}
    cb["identb"] = np.eye(128, dtype=np.float32)
    fq = np.floor((p - 31) / 16.0).astype(np.int64)
    step = np.zeros((128, 768), np.float32)
    for vi in range(9):
        step[vi, :] = np.where((np.arange(768) - 384) > (vi - 2), NEG, 0.0)
    cb["step"] = step
    oh = np.zeros((128, 128), np.float32)
    for vi in range(9):
        oh[vi, :] = (fq == vi - 2)
    cb["onehot"] = np.tile(oh, (1, 4))
    be = np.zeros((128, 4096), np.float32)
    for j in range(64):
        be[j, j * 64:(j + 1) * 64] = 1.0
    cb["bigexp"] = be
    k = p[:, None]
    ql = p[None, :]
    cb["caus"] = np.tile(np.where(k > ql, NEG, 0.0), (1, 4))
    cb["winlo"] = np.tile(np.where(k <= ql, NEG, 0.0), (1, 4))
    cfo, cbo = {}, {}
    o = 0
    for kk, vv in cf.items():
        cfo[kk] = (o, vv.shape[1])
        o += vv.shape[1]
    ncf = o
    o = 0
    for kk, vv in cb.items():
        cbo[kk] = (o, vv.shape[1])
        o += vv.shape[1]
    ncb = o
    cfa = np.concatenate(list(cf.values()), 1).astype(np.float32)
    cba = _bf(np.concatenate(list(cb.values()), 1))
    return cfa, cba, cfo, cbo, ncf, ncb


_HC = host_consts()


def bc3(v, n):
    return v.w(lambda a: a.unsqueeze(2).broadcast_to([a.shape[0], a.shape[1], n]))


def bcm(v, m):
    return v.w(lambda a: a.unsqueeze(1).broadcast_to([a.shape[0], m, a.shape[1]]))


import os
STOP = int(os.environ.get('K_STOP', '0'))
NSTOP = int(os.environ.get('K_NSTOP', '0'))


class Ctx:
    pass


def setup_common(P, nseq, dbg):
    g = Ctx()
    ntok = nseq * S
    g.nseq = nseq
    g.ntok = ntok
    g.x = P.dram("x", [ntok, D], F32, kind="ExternalInput")
    g.w_in = P.dram("w_in", [D, INW], F32, kind="ExternalInput")
    g.w_gu = P.dram("w_gu", [D, 2 * FH], F32, kind="ExternalInput")
    g.w_down = P.dram("w_down", [FH, D], F32, kind="ExternalInput")
    g.w_a = P.dram("w_a", [512, D], F32, kind="ExternalInput")
    g.w_b = P.dram("w_b", [512, D], F32, kind="ExternalInput")
    g.w_out = P.dram("w_out", [D, D], F32, kind="ExternalInput")
    g.w1k = P.dram("w1k", [2048, 128], F32, kind="ExternalInput")
    g.w1v = P.dram("w1v", [2048, 128], F32, kind="ExternalInput")
    g.w2k = P.dram("w2k", [128, 64], F32, kind="ExternalInput")
    g.w2v = P.dram("w2v", [128, 64], F32, kind="ExternalInput")
    g.posk = P.dram("posk", [64, 32], F32, kind="ExternalInput")
    g.posv = P.dram("posv", [64, 32], F32, kind="ExternalInput")
    g.gain1 = P.dram("gain1", [1, D], F32, kind="ExternalInput")
    g.gain2 = P.dram("gain2", [1, D], F32, kind="ExternalInput")
    g.gain3 = P.dram("gain3", [1, D], F32, kind="ExternalInput")
    g.onorm = P.dram("onorm", [1, 128], F32, kind="ExternalInput")
    g.alog = P.dram("alog", [1, 4], F32, kind="ExternalInput")
    g.dtb = P.dram("dtb", [1, 4], F32, kind="ExternalInput")
    g.convw = P.dram("convw", [128, 48], F32, kind="ExternalInput")
    g.cf = P.dram("cf", [128, _HC[4]], F32, kind="ExternalInput")
    g.cb = P.dram("cb", [128, _HC[5]], BF16, kind="ExternalInput")
    g.out = P.dram("out", [ntok, D], F32, kind="ExternalOutput")
    ntile = ntok // 128
    g.xnT_d = P.dram("xnT_d", [ntile * 128, D], BF16)
    g.oaT_d = P.dram("oaT_d", [ntile * 128, 512], BF16)
    g.obT_d = P.dram("obT_d", [ntile * 128, 512], BF16)
    g.x1_d = P.dram("x1_d", [ntok, D], F32)
    g.dbg = {}
    for name, shape in dbg.items():
        g.dbg[name] = P.dram("dbg_" + name, list(shape), F32, kind="ExternalOutput")
    return g


def load_consts(P, g):
    cfo, cbo = _HC[2], _HC[3]
    cft = P.sb([128, _HC[4]], F32)
    cbt = P.sb([128, _HC[5]], BF16)
    P.dma("sp", cft.all(), g.cf.all())
    P.dma("sp", cbt.all(), g.cb.all())
    g.cft, g.cbt = cft, cbt

    def CF(name, *shape):
        o, n = cfo[name]
        t = T("sb", cft.ap[:, o:o + n], [128, n], 4, cft.boff + o * 4)
        if shape:
            names = " ".join("d%d" % i for i in range(len(shape)))
            t = T("sb", t.ap.rearrange("p (%s) -> p %s" % (names, names), **{"d%d" % i: s for i, s in enumerate(shape)}),
                  [128] + list(shape), 4, cft.boff + o * 4)
        return t

    def CB(name):
        o, n = cbo[name]
        return T("sb", cbt.ap[:, o:o + n], [128, n], 2, cbt.boff + o * 2)

    g.CF, g.CB = CF, CB


def cast_load(P, dst, src_t, r0, c0, ncols, kchunks, step=1024):
    for k in range(kchunks):
        for cc in range(0, ncols, step):
            n = min(step, ncols - cc)
            P.dma("pool", dst[:, k, cc:cc + n], src_t[r0 + k * 128:r0 + (k + 1) * 128, c0 + cc:c0 + cc + n])


def rstd_of(P, out, ssq, n, tmp):
    P.ts("dve", tmp, ssq, 1.0 / n, ALU.mult, EPS, ALU.add)
    P.act(tmp, tmp, AF.Ln)
    P.act(out, tmp, AF.Exp, scale=-0.5)


def norm_transpose(P, g, xt, gainB, xn, xnT, junk, sm, identb, bank):
    P.act(xn.all(), xt.all(), AF.Square, accum=sm[:, 0:1])
    rstd_of(P, sm[:, 1:2], sm[:, 0:1], D, sm[:, 2:3])
    P.stt(xn.all(), xt.all(), sm[:, 1:2], gainB.all(), ALU.mult, ALU.mult)
    pT = P.ps(bank, [128, 8, 128], BF16)
    for k in range(8):
        P.tr(pT[:, k, :], xn[:, k * 128:(k + 1) * 128], identb.all())
    P.cp("act", xnT.all(), pT.all())


def phase1(P, g, ntiles, do_nsa=True):
    P.mark()
    CF, CB = g.CF, g.CB
    identb = CB("identb")
    identfold = CF("identfold")
    blockones = CF("blockones")
    ublk = CF("ublk")
    chunkmask = CF("chunkmask", 2, 128)
    maskD = CF("maskD", 3, 4, 64)
    win = P.sb([128, 8, NW1], BF16)
    cast_load(P, win, g.w_in, 0, 0, NW1, 8, step=840)
    gainB = P.sb([128, D], F32)
    P.dma("sp", gainB.all(), g.gain1[0:1, :].bc([128, D]))
    ognB = P.sb([128, 128], F32)
    P.dma("sp", ognB.all(), g.onorm[0:1, :].bc([128, 128]))
    sc0 = P.sb([128, 16], F32)
    P.dma("sp", sc0[:, 0:4], g.alog[0:1, :].bc([128, 4]))
    P.dma("sp", sc0[:, 4:8], g.dtb[0:1, :].bc([128, 4]))
    P.act(sc0[:, 8:12], sc0[:, 0:4], AF.Exp)
    P.ts("dve", sc0[:, 8:12], sc0[:, 8:12], -1.0, ALU.mult)
    nAB = sc0[:, 8:12]
    dtbB = sc0[:, 4:8]
    convw = P.sb([128, 48], F32)
    P.dma("sp", convw.all(), g.convw.all())
    cdiag = P.sb([128, 48, 128], BF16)
    for q in range(48):
        P.ts("dve", cdiag[:, q, :], identb.all(), convw[:, q:q + 1], ALU.mult)
    xt = P.sb([128, D], F32)
    xn = P.sb([128, D], BF16)
    junk = P.sb([128, 128], BF16)
    xnT = P.sb([128, 8, 128], BF16)
    sm = P.sb([128, 8], F32)
    cbuf = P.sb([128, 12, 131], BF16)
    cs = P.sb([128, 12, 128], BF16)
    ssq = P.sb([128, 8], F32)
    rn = P.sb([128, 8], F32)
    sc = P.sb([128, 64], F32)
    k_n = P.sb([128, 4, 128], BF16)
    kbg = P.sb([128, 4, 128], BF16)
    kd = P.sb([128, 4, 128], BF16)
    vb = P.sb([128, 4, 128], BF16)
    q_n = P.sb([128, 4, 128], BF16)
    kqT = P.sb([128, 8, 128], BF16)
    dgt = P.sb([128, 2, 4, 64], F32)
    DD = P.sb([128, 3, 4, 64], F32)
    EE = DD
    ATf = P.sb([128, 4, 64], F32)
    TTf = P.sb([128, 4, 64], F32)
    TTb = P.sb([128, 4, 64], BF16)
    Pb = [P.sb([128, 2, 4, 64], BF16) for _ in range(2)]
    QKm = P.sb([128, 4, 64], BF16)
    uS = P.sb([128, 4, 128], F32)
    wT = P.sb([128, 2, 4, 64], BF16)
    Sf = P.sb([128, 4, 128], F32)
    Sb = P.sb([128, 4, 128], BF16)
    vn = P.sb([128, 4, 128], BF16)
    tmpo = P.sb([128, 4, 128], F32)
    o_raw = P.sb([128, 4, 128], F32)
    zs = P.sb([128, 512], F32)
    oa = P.sb([128, 4, 128], BF16)
    oaT = P.sb([128, 4, 128], BF16)
    g1 = Ctx()
    if do_nsa:
        nsa_setup(P, g, g1)

    bk = [2, 6]

    def nb():
        b = bk[0]
        bk[0] = 2 + (bk[0] - 2 + 1) % 4
        return b

    def nbt():
        b = bk[1]
        bk[1] = 6 + (bk[1] - 6 + 1) % 2
        return b

    for gt in range(ntiles):
        b, i = divmod(gt, NT)
        r0 = gt * 128
        if i == 0:
            P.memset("pool", cbuf[:, :, 0:3], 0.0)
            P.memset("pool", Sf.all(), 0.0)
            P.memset("pool", Sb.all(), 0.0)
        P.dma("sp", xt.all(), g.x[r0:r0 + 128, :])
        norm_transpose(P, g, xt, gainB, xn, xnT, junk, sm, identb, nb())
        P.dma("sp", g.xnT_d[r0:r0 + 128, :], xnT.all().w(lambda a: a.rearrange("p a b -> p (a b)")))
        if STOP == 1:
            continue
        for fb in range(3):
            pq = P.ps(nb(), [128, 4, 128])
            for ff in range(4):
                f = fb * 4 + ff
                for k in range(8):
                    P.mm(pq[:, ff, :], win[:, k, f * 128:(f + 1) * 128], xnT[:, k, :], start=(k == 0), stop=(k == 7))
            P.cp("act", cbuf[:, fb * 4:(fb + 1) * 4, 3:131], pq.all())
        for fb in range(3):
            pc = P.ps(nb(), [128, 4, 128])
            for ff in range(4):
                f = fb * 4 + ff
                for j in range(4):
                    P.mm(pc[:, ff, :], cdiag[:, f * 4 + j, :], cbuf[:, f, j:j + 128], start=(j == 0), stop=(j == 3))
            P.act(cs[:, fb * 4:(fb + 1) * 4, :], pc.all(), AF.Silu)
        P.cp("pool", cbuf[:, :, 0:3], cbuf[:, :, 128:131])
        if STOP == 2:
            continue
        psA = P.ps(nb(), [128, 8, 128], BF16)
        psB = P.ps(nb(), [128, 4, 128], BF16)
        for f in range(8):
            P.tr(psA[:, f, :], cs[:, f, :], identb.all())
        for f in range(4):
            P.tr(psB[:, f, :], cs[:, 8 + f, :], identb.all())
        for f in range(8):
            P.act(junk[:, 0:128], psA[:, f, :], AF.Square, accum=ssq[:, f:f + 1])
        P.ts("dve", rn.all(), ssq.all(), EPS, ALU.add)
        P.act(rn.all(), rn.all(), AF.Ln)
        P.act(rn.all(), rn.all(), AF.Exp, scale=-0.5)
        if STOP == 3:
            continue
        pba = P.ps(nb(), [128, 8])
        for k in range(8):
            P.mm(pba.all(), xnT[:, k, :], win[:, k, C_BA:C_BA + 8], start=(k == 0), stop=(k == 7))
        P.act(sc[:, 52:56], pba[:, 0:4], AF.Exp, scale=-1.0)
        P.act(sc[:, 0:4], sc[:, 52:56], AF.Ln, bias=1.0)
        P.act(sc[:, 4:8], sc[:, 0:4], AF.Exp, scale=-1.0)
        P.tt("dve", sc[:, 8:12], pba[:, 4:8], dtbB, ALU.add)
        P.act(sc[:, 8:12], sc[:, 8:12], AF.Exp)
        P.act(sc[:, 8:12], sc[:, 8:12], AF.Ln, bias=1.0)
        P.tt("dve", sc[:, 8:12], sc[:, 8:12], nAB, ALU.mult)
        pg = P.ps(nb(), [128, 16])
        P.mm(pg[:, 0:4], ublk.all(), sc[:, 8:12])
        P.mm(pg[:, 4:8], blockones.all(), sc[:, 8:12])
        for c in range(2):
            P.mm(pg[:, 8 + 4 * c:12 + 4 * c], chunkmask[:, c, :], sc[:, 8:12])
        P.cp("act", sc[:, 12:20], pg[:, 0:8])
        P.act(sc[:, 28:36], pg[:, 8:16], AF.Exp)
        P.act(sc[:, 20:24], sc[:, 12:16], AF.Exp)
        P.tt("dve", sc[:, 24:28], sc[:, 16:20], sc[:, 12:16], ALU.subtract)
        P.act(sc[:, 24:28], sc[:, 24:28], AF.Exp)
        P.tt("dve", sc[:, 36:40], sc[:, 12:16], sc[:, 0:4], ALU.subtract)
        P.tt("dve", sc[:, 40:44], rn[:, 4:8], sc[:, 4:8], ALU.mult)
        P.tt("dve", sc[:, 40:44], sc[:, 40:44], sc[:, 20:24], ALU.mult)
        P.tt("dve", sc[:, 44:48], rn[:, 4:8], sc[:, 24:28], ALU.mult)
        P.ts("dve", sc[:, 48:52], rn[:, 0:4], 128.0 ** -0.5, ALU.mult)
        if STOP == 4:
            continue
        P.tt("dve", k_n.all(), psA[:, 4:8, :], bc3(rn[:, 4:8], 128), ALU.mult)
        P.tt("dve", kbg.all(), psA[:, 4:8, :], bc3(sc[:, 40:44], 128), ALU.mult)
        P.tt("dve", kd.all(), psA[:, 4:8, :], bc3(sc[:, 44:48], 128), ALU.mult)
        P.tt("dve", q_n.all(), psA[:, 0:4, :], bc3(sc[:, 48:52], 128), ALU.mult)
        P.tt("dve", vb.all(), psB.all(), bc3(sc[:, 4:8], 128), ALU.mult)
        pkq = P.ps(nb(), [128, 8, 128], BF16)
        for h in range(4):
            P.tr(pkq[:, h, :], k_n[:, h, :], identb.all())
            P.tr(pkq[:, 4 + h, :], q_n[:, h, :], identb.all())
        P.cp("act", kqT.all(), pkq.all())
        if STOP == 5:
            continue
        pKQ = P.ps(nb(), [128, 2, 4, 64])
        for c in range(2):
            cs_ = slice(c * 64, (c + 1) * 64)
            for h in range(4):
                P.mm(pKQ[cs_, 0, h, :], kqT[:, h, cs_], kqT[:, h, cs_])
                P.mm(pKQ[cs_, 1, h, :], kqT[:, h, cs_], kqT[:, 4 + h, cs_])
        P.tt("pool", dgt[:, 0], bcm(identfold.all(), 4), bc3(sc[:, 12:16], 64), ALU.mult)
        P.tt("pool", dgt[:, 1], bcm(identfold.all(), 4), bc3(sc[:, 36:40], 64), ALU.mult)
        pBf = P.ps(nb(), [128, 2, 4, 64])
        P.mm(pBf.all().w(lambda a: a.rearrange("p a b c -> p (a b c)")), blockones.all(),
             dgt.all().w(lambda a: a.rearrange("p a b c -> p (a b c)")))
        P.tt("dve", DD[:, 0], bc3(sc[:, 36:40], 64), pBf[:, 0], ALU.subtract)
        P.tt("dve", DD[:, 1], pBf[:, 1], bc3(sc[:, 12:16], 64), ALU.subtract)
        P.tt("dve", DD[:, 2], pBf[:, 0], bc3(sc[:, 12:16], 64), ALU.subtract)
        P.tt("pool", DD.all(), DD.all(), maskD.all(), ALU.add)
        P.act(EE.all(), DD.all(), AF.Exp)
        if STOP == 6:
            continue
        P.tt("dve", Pb[0][:, 0], pKQ[:, 0], EE[:, 0], ALU.mult)
        P.tt("dve", ATf.all(), pKQ[:, 0], EE[:, 1], ALU.mult)
        P.tt("dve", QKm.all(), pKQ[:, 1], EE[:, 2], ALU.mult)
        P.cp("pool", Pb[0][:, 1], ATf.all())
        P.tt("pool", TTf.all(), bcm(identfold.all(), 4), ATf.all(), ALU.subtract)
        P.cp("pool", TTb.all(), TTf.all())
        if STOP == 7:
            continue

        def tail():
            cur = 0
            NL = 5
            for lvl in range(NL):
                last = lvl == NL - 1
                pN = P.ps(nbt(), [128, 2, 4, 64])
                Pc, Pn = Pb[cur], Pb[1 - cur]
                for c in range(2):
                    cs_ = slice(c * 64, (c + 1) * 64)
                    for h in range(4):
                        P.mm(pN[cs_, 0, h, :], Pc[cs_, 1, h, :], Pc[cs_, 0, h, :])
                        if not last:
                            P.mm(pN[cs_, 1, h, :], Pc[cs_, 0, h, :], Pc[cs_, 1, h, :])
                if last:
                    P.cp("act", Pn[:, 0], pN[:, 0])
                else:
                    P.cp("act", Pn.all(), pN.all())
                    yield
                pT_ = P.ps(nbt(), [128, 4, 64])
                for c in range(2):
                    cs_ = slice(c * 64, (c + 1) * 64)
                    for h in range(4):
                        P.mm(pT_[cs_, h, :], Pn[cs_, 0, h, :], TTb[cs_, h, :])
                P.tt("dve", TTf.all(), TTf.all(), pT_.all(), ALU.add)
                P.cp("pool", TTb.all(), TTf.all())
                yield
                cur = 1 - cur
            pU = P.ps(nbt(), [128, 4, 128])
            pW = P.ps(nbt(), [128, 2, 4, 64])
            for c in range(2):
                cs_ = slice(c * 64, (c + 1) * 64)
                for h in range(4):
                    P.mm(pU[cs_, h, :], TTb[cs_, h, :], vb[cs_, h, :])
                    P.mm(pW[:, c, h, :], kbg[cs_, h, :], TTb[cs_, h, :])
            P.cp("act", uS.all(), pU.all())
            P.cp("act", wT.all(), pW.all())
            yield
            for c in range(2):
                cs_ = slice(c * 64, (c + 1) * 64)
                pWS = P.ps(nbt(), [128, 4, 128])
                pQS = P.ps(nbt(), [128, 4, 128])
                for h in range(4):
                    P.mm(pWS[cs_, h, :], wT[:, c, h, :], Sb[:, h, :])
                    P.mm(pQS[cs_, h, :], kqT[:, 4 + h, cs_], Sb[:, h, :])
                P.tt("dve", vn[cs_], uS[cs_], pWS[cs_], ALU.subtract)
                P.tt("dve", tmpo[cs_], pQS[cs_], bc3(sc[cs_, 20:24], 128), ALU.mult)
                yield
                pO = P.ps(nbt(), [128, 4, 128])
                pS = P.ps(nbt(), [128, 4, 128])
                for h in range(4):
                    P.mm(pO[cs_, h, :], QKm[cs_, h, :], vn[cs_, h, :])
                    P.mm(pS[:, h, :], kd[cs_, h, :], vn[cs_, h, :])
                P.tt("dve", o_raw[cs_], tmpo[cs_], pO[cs_], ALU.add)
                yield
                P.tt("pool", Sf.all(), Sf.all(), bc3(sc[:, 28 + 4 * c:32 + 4 * c], 128), ALU.mult)
                P.tt("dve", Sf.all(), Sf.all(), pS.all(), ALU.add)
                P.cp("pool", Sb.all(), Sf.all())
                yield
            pz = P.ps(nbt(), [128, 512])
            for k in range(8):
                P.mm(pz.all(), xnT[:, k, :], win[:, k, C_Z:C_Z + 512], start=(k == 0), stop=(k == 7))
            P.act(zs.all(), pz.all(), AF.Silu)
            yield
            for h in range(4):
                P.act(junk[:, 0:128], o_raw[:, h, :], AF.Square, accum=sc[:, 52 + h:53 + h])
            rstd_of(P, sc[:, 56:60], sc[:, 52:56], 128, sc[:, 60:64])
            yield
            P.tt("dve", tmpo.all(), o_raw.all(), bc3(sc[:, 56:60], 128), ALU.mult)
            P.tt("pool", tmpo.all(), tmpo.all(), bcm(ognB.all(), 4), ALU.mult)
            P.tt("dve", oa.all().w(lambda a: a.rearrange("p a b -> p (a b)")), tmpo.all().w(lambda a: a.rearrange("p a b -> p (a b)")), zs.all(), ALU.mult)
            if "o_a" in g.dbg:
                P.tt("pool", tmpo.all().w(lambda a: a.rearrange("p a b -> p (a b)")), tmpo.all().w(lambda a: a.rearrange("p a b -> p (a b)")), zs.all(), ALU.mult)
                P.dma("sp", g.dbg["o_a"][r0:r0 + 128, :], tmpo.all().w(lambda a: a.rearrange("p a b -> p (a b)")))
            if "o_raw" in g.dbg:
                P.dma("sp", g.dbg["o_raw"][r0:r0 + 128, :], o_raw.all().w(lambda a: a.rearrange("p a b -> p (a b)")))
            pOT = P.ps(nbt(), [128, 4, 128], BF16)
            for h in range(4):
                P.tr(pOT[:, h, :], oa[:, h, :], identb.all())
            P.cp("act", oaT.all(), pOT.all())
            P.dma("sp", g.oaT_d[r0:r0 + 128, :], oaT.all().w(lambda a: a.rearrange("p a b -> p (a b)")))
            yield

        tg = tail()
        if do_nsa:
            nsa_tile(P, g, g1, gt, xnT, win, nb, bg=tg)
        for _ in tg:
            pass
    P.release()


def shared_inputs(inp):
    f = lambda a: np.ascontiguousarray(np.asarray(a, np.float32))
    cw = np.asarray(inp["gdn_conv_w"], np.float32)[0]
    convw = cw.reshape(4, 12, 128).transpose(2, 1, 0).reshape(128, 48)
    d = {
        "w_in": f(inp["w_in"][0]), "w_gu": f(inp["w_gate_up"][0]), "w_down": f(inp["w_down"][0]),
        "w_a": f(inp["w_branch_gdn"][0]), "w_b": f(inp["w_branch_nsa"][0]), "w_out": f(inp["w_out"][0]),
        "w1k": f(np.asarray(inp["cmp_w1_k"])[0].reshape(2048, 128)), "w1v": f(np.asarray(inp["cmp_w1_v"])[0].reshape(2048, 128)),
        "w2k": f(inp["cmp_w2_k"][0]), "w2v": f(inp["cmp_w2_v"][0]),
        "posk": f(np.asarray(inp["cmp_pos_k"])[0].T), "posv": f(np.asarray(inp["cmp_pos_v"])[0].T),
        "gain1": f(inp["mix_norm_gain"]).reshape(1, D), "gain2": f(inp["ffn_norm_gain"]).reshape(1, D),
        "gain3": f(inp["final_norm_gain"]).reshape(1, D), "onorm": f(inp["gdn_out_norm_gain"]).reshape(1, 128),
        "alog": f(inp["gdn_a_log"]).reshape(1, 4), "dtb": f(inp["gdn_dt_bias"]).reshape(1, 4),
        "convw": f(convw), "cf": _HC[0], "cb": _HC[1],
    }
    return d


def nsa_setup(P, g, n):
    CF, CB = g.CF, g.CB
    n.identb = CB("identb")
    n.step = CB("step")
    n.onehot = CB("onehot")
    n.bigexp = CB("bigexp")
    n.caus = CB("caus")
    n.winlo = CB("winlo")
    n.keep = CF("keep")
    n.ovr = CF("ovr")
    n.cosk = CF("cosk", NT, 8)
    n.sink = CF("sink", NT, 8)
    n.w1 = []
    n.w2 = []
    n.bias = []
    pb = P.ps(7, [128, 2])
    for kv, (w1d, w2d, posd, nm) in enumerate(((g.w1k, g.w2k, g.posk, "w1k"), (g.w1v, g.w2v, g.posv, "w1v"))):
        w1 = P.sb([128, 32, 128], BF16)
        P.dma("pool", w1[0:64], V(w1d.ap.rearrange("(l d) h -> d l h", d=64), (nm, 0, 2048, 0, 128)))
        w2 = P.sb([128, 64], BF16)
        P.dma("pool", w2.all(), w2d.all())
        pt = P.sb([128, 32], BF16)
        P.dma("pool", pt[0:64], posd.all())
        for l in range(32):
            P.mm(pb[:, kv:kv + 1], w1[0:64, l, :], pt[0:64, l:l + 1], start=(l == 0), stop=(l == 31))
        n.w1.append(w1)
        n.w2.append(w2)
    bias = P.sb([128, 2], F32)
    P.cp("act", bias.all(), pb.all())
    n.biasv = bias
    n.ksT = P.sb([128, S], BF16)
    n.qTz = [P.sb([128, 4, 128], BF16) for _ in range(2)]
    P.memset("pool", n.qTz[0].all(), 0.0)
    P.memset("pool", n.qTz[1].all(), 0.0)
    n.vsE = P.sb([128, NT, 2, 65], BF16)
    n.kwT = P.sb([128, 2, 5, 128], BF16)
    n.vwE = P.sb([128, 5, 2, 65], BF16)
    n.kcr = P.sb([128, 2, 144], BF16)
    n.vcr = P.sb([128, 2, 144], BF16)
    n.kcT = P.sb([128, 2, 256], BF16)
    n.vcT = P.sb([128, 2, 256], BF16)
    n.vcE = P.sb([128, 2, 2, 65], BF16)
    P.memset("pool", n.vsE.all(), 1.0)
    P.memset("pool", n.vwE.all(), 1.0)
    P.memset("pool", n.vcE.all(), 1.0)
    n.q_r = P.sb([128, 8, 64], BF16)
    n.kr1 = P.sb([128, 8, 64], BF16)
    n.kr2 = P.sb([128, 2, 64], BF16)
    n.vraw = P.sb([128, 2, 64], BF16)
    n.rt = P.sb([128, 4, 8, 8], F32)
    n.gsig = P.sb([128, 2, 4, 3], F32)
    n.qT = P.sb([128, 8, 128], BF16)
    n.hid = P.sb([128, 2, 2, 8], BF16)
    n.esc = P.sb([128, 4, 256], F32)
    n.den = P.sb([128, 16], F32)
    n.psp = P.sb([128, 260], F32)
    n.imp = P.sb([128, 64], F32)
    n.score = P.sb([128, 64], F32)
    n.score2 = P.sb([128, 64], F32)
    n.m8 = P.sb([128, 16], F32)
    n.negsel = P.sb([128, 64], BF16)
    n.nselT4 = P.sb([128, 4, 128], BF16)
    n.eT = [P.sb([128, 512], BF16) for _ in range(2)]
    n.ob_f = P.sb([128, 8, 64], F32)
    n.obtmp = P.sb([128, 4, 64], F32)
    n.ob = P.sb([128, 512], BF16)
    n.obT = P.sb([128, 4, 128], BF16)
    n.ei = 0


def rope(P, n, src, dst, H, cos, sin, rest_scale):
    rt = n.rt
    c = bcm(cos, H)
    s = bcm(sin, H)
    P.tt("dve", rt[:, 0, 0:H, :], src[:, :, 0:8], c, ALU.mult)
    P.tt("dve", rt[:, 1, 0:H, :], src[:, :, 8:16], s, ALU.mult)
    P.tt("dve", rt[:, 2, 0:H, :], src[:, :, 8:16], c, ALU.mult)
    P.tt("dve", rt[:, 3, 0:H, :], src[:, :, 0:8], s, ALU.mult)
    P.tt("pool", dst[:, :, 0:8], rt[:, 0, 0:H, :], rt[:, 1, 0:H, :], ALU.subtract)
    P.tt("pool", dst[:, :, 8:16], rt[:, 2, 0:H, :], rt[:, 3, 0:H, :], ALU.add)
    P.act(dst[:, :, 16:64], src[:, :, 16:64], AF.Copy, scale=rest_scale)


def nsa_tile(P, g, n, gt, xnT, win, nb, bg=None):
    b, i = divmod(gt, NT)
    r0 = gt * 128
    identb = n.identb
    slot = i % 5
    if i == 0:
        P.memset("pool", n.kcT.all(), 0.0)
        P.memset("pool", n.vcT.all(), 0.0)
        P.memset("pool", n.vcE[:, :, :, 0:64], 0.0)
        P.memset("pool", n.kcr.all(), 0.0)
        P.memset("pool", n.vcr.all(), 0.0)
    pn = []
    for (c0, w) in ((C_Q, 512), (C_KC, 512), (C_KC + 512, 280)):
        pt = P.ps(nb(), [128, 512])
        for k in range(8):
            P.mm(pt[:, 0:w], xnT[:, k, :], win[:, k, c0:c0 + w], start=(k == 0), stop=(k == 7))
        pn.append(pt)

    def v3(t, c0, H):
        return T("ps", t.ap[:, c0:c0 + H * 64].rearrange("p (h d) -> p h d", d=64), [128, H, 64], 4, t.boff + c0 * 4)

    rope(P, n, v3(pn[0], 0, 8), n.q_r, 8, n.cosk[:, i, :], n.sink[:, i, :], 1.0)
    rope(P, n, v3(pn[1], 0, 8), n.kr1, 8, n.cosk[:, i, :], n.sink[:, i, :], 1.0)
    rope(P, n, v3(pn[2], 0, 2), n.kr2, 2, n.cosk[:, i, :], n.sink[:, i, :], 1.0)
    P.cp("act", n.vraw.all(), v3(pn[1], 128, 2).all())
    P.cp("act", n.vsE[:, i, :, 0:64], v3(pn[1], 384, 2).all())
    P.cp("act", n.vwE[:, slot, :, 0:64], v3(pn[2], 128, 2).all())
    gs = n.gsig.all().w(lambda a: a.rearrange("p a b c -> p (a b c)"))
    P.act(gs, pn[2][:, 256:280], AF.Exp, scale=-1.0)
    P.ts("dve", gs, gs, 1.0, ALU.add)
    P.recip(gs, gs)
    if NSTOP == 1:
        return
    pqT = P.ps(nb(), [128, 8, 128], BF16)
    for h in range(8):
        P.tr(pqT[0:64, h, :], n.q_r[:, h, :], identb.all())
    P.act(n.qT[0:64], pqT[0:64], AF.Copy, scale=0.125)
    pq2 = P.ps(nb(), [128, 4, 128], BF16)
    for r in range(4):
        P.tr(pq2[0:64, r, :], n.q_r[:, r, :], identb.all())
        P.tr(pq2[64:128, r, :], n.q_r[:, 4 + r, :], identb.all())
    P.act(n.qTz[0][0:64], pq2[0:64], AF.Copy, scale=0.125)
    P.act(n.qTz[1][64:128], pq2[64:128], AF.Copy, scale=0.125)
    pks = P.ps(nb(), [128, 128], BF16)
    P.tr(pks[0:64, :], n.kr1[:, 4, :], identb.all())
    P.tr(pks[64:128, :], n.kr1[:, 5, :], identb.all())
    P.cp("dve", n.ksT[:, i * 128:(i + 1) * 128], pks.all())
    pkT = P.ps(nb(), [128, 8, 128], BF16)
    srcs = [n.kr1[:, 0, :], n.kr1[:, 1, :], n.vraw[:, 0, :], n.vraw[:, 1, :], n.kr1[:, 4, :], n.kr1[:, 5, :], n.kr2[:, 0, :], n.kr2[:, 1, :]]
    for h in range(8):
        P.tr(pkT[0:64, h, :], srcs[h], identb.all())
    P.cp("dve", n.kcr[0:64, :, 16:144], pkT[0:64, 0:2, :])
    P.cp("dve", n.vcr[0:64, :, 16:144], pkT[0:64, 2:4, :])
    P.cp("act", n.kwT[0:64, :, slot, :], pkT[0:64, 6:8, :])
    if NSTOP == 2:
        return
    n0 = 8 * i - 1 if i > 0 else 0
    nn = 8 if i > 0 else 7
    m0 = 0 if i > 0 else 1
    ph = P.ps(nb(), [128, 2, 2, 8])
    for kv, cr in enumerate((n.kcr, n.vcr)):
        for l in range(32):
            P.mm(ph[:, kv, :, :], n.w1[kv][0:64, l, :], cr[0:64, :, l:l + 113:16], start=(l == 0), stop=(l == 31))
        P.act(n.hid[:, kv, :, :], ph[:, kv, :, :], AF.Silu, bias=n.biasv[:, kv:kv + 1])
    pk2 = P.ps(nb(), [128, 2, 2, 8])
    for kv in range(2):
        P.mm(pk2[0:64, kv, :, :], n.w2[kv].all(), n.hid[:, kv, :, :])
    P.cp("act", n.kcT[0:64, :, n0:n0 + nn], pk2[0:64, 0, :, m0:8])
    P.cp("act", n.vcT[0:64, :, n0:n0 + nn], pk2[0:64, 1, :, m0:8])
    P.cp("pool", n.kcr[0:64, :, 0:16], n.kcr[0:64, :, 128:144])
    P.cp("pool", n.vcr[0:64, :, 0:16], n.vcr[0:64, :, 128:144])
    for ch in sorted(set((n0 // 128, (n0 + nn - 1) // 128))):
        pvc = P.ps(nb(), [128, 2, 64], BF16)
        for gq in range(2):
            P.tr(pvc[:, gq, :], n.vcT[0:64, gq, ch * 128:(ch + 1) * 128], identb[0:64, 0:64])
        P.cp("act", n.vcE[:, ch, :, 0:64], pvc.all())
        if NSTOP == 3:
            pass
    if NSTOP == 3:
        return
    nch = 1 if 8 * i + 7 <= 128 else 2
    NC = nch * 128
    sh = 384 - 8 * i
    for gq in range(2):
        qT4 = n.qT[0:64, gq * 4:(gq + 1) * 4, :].w(lambda a: a.rearrange("p a b -> p (a b)"))
        if nch == 1:
            bankA = nb()
            regs = [P.ps(bankA, [128, 4, 128])[:, r, :] for r in range(4)]
        else:
            bankA, bankB = nb(), nb()
            regs = [P.ps(bankA if r < 2 else bankB, [128, 2, 256])[:, r % 2, :] for r in range(4)]
        for r in range(4):
            P.mm(regs[r], n.qT[0:64, gq * 4 + r, :], n.kcT[0:64, gq, 0:NC], start=True, stop=False)
            P.mm(regs[r], n.onehot[0:9, 0:128], n.step[0:9, sh:sh + NC], start=False, stop=True)
        for r in range(4):
            P.act(n.esc[:, r, 0:NC], regs[r], AF.Exp, accum=n.den[:, r:r + 1])
            if NSTOP == 4:
                pass
        if NSTOP == 4:
            return
        P.ts("dve", n.den[:, 4:8], n.den[:, 0:4], 1e-30, ALU.max)
        P.recip(n.den[:, 4:8], n.den[:, 4:8])
        P.memset("pool", n.psp.all(), 0.0)
        P.ts("dve", n.psp[:, 1:1 + NC], n.esc[:, 0, 0:NC], n.den[:, 4:5], ALU.mult)
        for r in range(1, 4):
            P.stt(n.psp[:, 1:1 + NC], n.esc[:, r, 0:NC], n.den[:, 4 + r:5 + r], n.psp[:, 1:1 + NC], ALU.mult, ALU.add)
        P.tt("dve", n.imp.all(), n.psp[:, 0:256:4], n.psp[:, 1:257:4], ALU.add)
        for m in range(2, 5):
            P.tt("dve", n.imp.all(), n.imp.all(), n.psp[:, m:m + 256:4], ALU.add)
        P.tt("dve", n.score.all(), n.imp.all(), n.keep[:, 64 - 2 * i:128 - 2 * i], ALU.mult)
        P.tt("dve", n.score.all(), n.score.all(), n.ovr[:, 64 - 2 * i:128 - 2 * i], ALU.add)
        P.memset("dve", n.score[:, 0:1], 1000.0)
        if NSTOP == 5:
            return
        sc_, sc2, m8 = n.score.all(), n.score2.all(), n.m8
        m8a, m8b = m8[:, 0:8], m8[:, 8:16]
        P.op("dve", lambda e, o=m8a, s=sc_: e.max(out=o.ap, in_=s.ap), [sc_], [m8a])
        P.op("dve", lambda e, o=sc2, a=m8a, s=sc_: e.match_replace(out=o.ap, in_to_replace=a.ap, in_values=s.ap, imm_value=-2.0), [m8a, sc_], [sc2])
        P.op("dve", lambda e, o=m8b, s=sc2: e.max(out=o.ap, in_=s.ap), [sc2], [m8b])
        P.ts("dve", n.negsel.all(), n.score.all(), m8[:, 15:16], ALU.is_lt, NEG, ALU.mult)
        if NSTOP == 6:
            return
        pns = P.ps(nb(), [128, 128], BF16)
        P.tr(pns[0:64, :], n.negsel.all(), identb.all())
        P.cp("act", n.nselT4[0:64], bcm(pns[0:64, :], 4))
        nsel = n.nselT4[0:64].w(lambda a: a.rearrange("p a b -> p (a b)"))
        if NSTOP == 7:
            return

        def branch(chunks, kfn, vfn, maskfn, br, qop=None):
            qop = qT4 if qop is None else qop
            po = P.ps(gq, [128, 4, 65])
            first = True

            def issue_scores(kc_):
                pst = P.ps(nb(), [128, 512])
                masks = maskfn(kc_)
                P.mm(pst.all(), kfn(kc_), qop, start=True, stop=(len(masks) == 0))
                for mi, (ml, mr) in enumerate(masks):
                    P.mm(pst.all(), ml, mr, start=False, stop=(mi == len(masks) - 1))
                return pst

            pst_next = issue_scores(chunks[0])
            for ci, kc_ in enumerate(chunks):
                lastc = ci == len(chunks) - 1
                pst = pst_next
                if bg is not None:
                    next(bg, None)
                eT = n.eT[n.ei % 2]
                n.ei += 1
                P.act(eT.all(), pst.all(), AF.Exp)
                if not lastc:
                    pst_next = issue_scores(chunks[ci + 1])
                for r in range(4):
                    P.mm(po[:, r, :], eT[:, r * 128:(r + 1) * 128], vfn(kc_), start=first, stop=(lastc and r == 3), skip_group_check=True)
                    first = False
            P.ts("dve", n.den[:, 8:12], po[:, :, 64], 1e-30, ALU.max)
            P.recip(n.den[:, 8:12], n.den[:, 8:12])
            P.tt("dve", n.den[:, 8:12], n.den[:, 8:12], n.gsig[:, gq, :, br], ALU.mult)
            dst = n.ob_f[:, gq * 4:(gq + 1) * 4, :]
            if br == 0:
                P.tt("dve", dst, po[:, :, 0:64], bc3(n.den[:, 8:12], 64), ALU.mult)
            else:
                P.tt("dve", n.obtmp.all(), po[:, :, 0:64], bc3(n.den[:, 8:12], 64), ALU.mult)
                P.tt("pool", dst, dst, n.obtmp.all(), ALU.add)

        branch(list(range(nch)),
               lambda c: n.kcT[0:64, gq, c * 128:(c + 1) * 128],
               lambda c: n.vcE[:, c, gq, :],
               lambda c: [(n.step[0:9, sh + c * 128:sh + (c + 1) * 128], n.onehot[0:9, :])], 0)
        if NSTOP == 8:
            return
        branch(list(range(i + 1)),
               lambda c: n.ksT[:, c * 128:(c + 1) * 128],
               lambda c: n.vsE[:, c, gq, :],
               lambda c: [(n.bigexp[0:64, c * 128:(c + 1) * 128], nsel)] + ([(identb.all(), n.caus.all())] if c == i else []), 1,
               qop=n.qTz[gq].all().w(lambda a: a.rearrange("p a b -> p (a b)")))
        if NSTOP == 9:
            return
        branch(list(range(max(0, i - 4), i + 1)),
               lambda c: n.kwT[0:64, gq, c % 5, :],
               lambda c: n.vwE[:, c % 5, gq, :],
               lambda c: ([(identb.all(), n.caus.all())] if c == i else []) + ([(identb.all(), n.winlo.all())] if c == i - 4 else []), 2)
        if NSTOP == 10:
            return
    P.cp("act", n.ob.all(), n.ob_f.all().w(lambda a: a.rearrange("p a b -> p (a b)")))
    if "o_b" in g.dbg:
        P.dma("sp", g.dbg["o_b"][r0:r0 + 128, :], n.ob_f.all().w(lambda a: a.rearrange("p a b -> p (a b)")))
    pOT = P.ps(nb(), [128, 4, 128], BF16)
    for k in range(4):
        P.tr(pOT[:, k, :], n.ob[:, k * 128:(k + 1) * 128], identb.all())
    P.cp("act", n.obT.all(), pOT.all())
    P.dma("sp", g.obT_d[r0:r0 + 128, :], n.obT.all().w(lambda a: a.rearrange("p a b -> p (a b)")))


def phase2(P, g, ntiles):
    P.mark()
    identb = g.CB("identb")
    wm = P.sb([128, 8, 2048], BF16)
    cast_load(P, wm, g.w_in, 0, C_MA, 2048, 8, step=1024)
    wa = P.sb([128, 4, D], BF16)
    cast_load(P, wa, g.w_a, 0, 0, D, 4)
    wb = P.sb([128, 4, D], BF16)
    cast_load(P, wb, g.w_b, 0, 0, D, 4)
    wo = P.sb([128, 8, D], BF16)
    cast_load(P, wo, g.w_out, 0, 0, D, 8)
    xnT = P.sb([128, 8, 128], BF16)
    oaT = P.sb([128, 4, 128], BF16)
    obT = P.sb([128, 4, 128], BF16)
    xt = P.sb([128, D], F32)
    sg = P.sb([128, 2048], F32)
    m1 = P.sb([128, 512], F32)
    m2 = P.sb([128, 512], F32)
    mg = P.sb([128, D], BF16)
    mT = P.sb([128, 8, 128], BF16)
    x1t = P.sb([128, D], F32)
    bk = [0]

    def nb():
        b = bk[0]
        bk[0] = (bk[0] + 1) % 8
        return b

    fl = lambda a: a.rearrange("p a b -> p (a b)")
    for gt in range(ntiles):
        r0 = gt * 128
        P.dma("sp", xnT.all().w(fl), g.xnT_d[r0:r0 + 128, :])
        P.dma("sp", oaT.all().w(fl), g.oaT_d[r0:r0 + 128, :])
        P.dma("sp", obT.all().w(fl), g.obT_d[r0:r0 + 128, :])
        P.dma("sp", xt.all(), g.x[r0:r0 + 128, :])
        for j in range(4):
            pg = P.ps(nb(), [128, 512])
            for k in range(8):
                P.mm(pg.all(), xnT[:, k, :], wm[:, k, j * 512:(j + 1) * 512], start=(k == 0), stop=(k == 7))
            P.act(sg[:, j * 512:(j + 1) * 512], pg.all(), AF.Exp, scale=-1.0)
        P.ts("pool", sg.all(), sg.all(), 1.0, ALU.add)
        P.recip(sg.all(), sg.all())
        for j in range(2):
            pa = P.ps(nb(), [128, 512])
            pb = P.ps(nb(), [128, 512])
            for k in range(4):
                P.mm(pa.all(), oaT[:, k, :], wa[:, k, j * 512:(j + 1) * 512], start=(k == 0), stop=(k == 3))
            for k in range(4):
                P.mm(pb.all(), obT[:, k, :], wb[:, k, j * 512:(j + 1) * 512], start=(k == 0), stop=(k == 3))
            P.tt("dve", m1.all(), pa.all(), sg[:, j * 512:(j + 1) * 512], ALU.mult)
            P.tt("dve", m2.all(), pb.all(), sg[:, 1024 + j * 512:1024 + (j + 1) * 512], ALU.mult)
            P.tt("pool", mg[:, j * 512:(j + 1) * 512], m1.all(), m2.all(), ALU.add)
        if "merged" in g.dbg:
            P.tt("pool", m1.all(), m1.all(), m2.all(), ALU.add)
            P.dma("sp", g.dbg["merged"][r0:r0 + 128, 512:1024], m1.all())
        pmT = P.ps(nb(), [128, 8, 128], BF16)
        for k in range(8):
            P.tr(pmT[:, k, :], mg[:, k * 128:(k + 1) * 128], identb.all())
        P.cp("act", mT.all(), pmT.all())
        for j in range(2):
            po = P.ps(nb(), [128, 512])
            for k in range(8):
                P.mm(po.all(), mT[:, k, :], wo[:, k, j * 512:(j + 1) * 512], start=(k == 0), stop=(k == 7))
            P.tt("dve", x1t[:, j * 512:(j + 1) * 512], xt[:, j * 512:(j + 1) * 512], po.all(), ALU.add)
        P.dma("sp", g.x1_d[r0:r0 + 128, :], x1t.all())
        if "x1" in g.dbg:
            P.dma("sp", g.dbg["x1"][r0:r0 + 128, :], x1t.all())
    P.release()


def phase3(P, g, ntiles, ST=256):
    P.sb_off = 0
    identb = P.sb([128, 128], BF16)
    P.dma("sp", identb.all(), g.cb[:, 0:128])
    wgu = P.sb([128, 8, 2 * FH], BF16)
    cast_load(P, wgu, g.w_gu, 0, 0, 2 * FH, 8, step=1408)
    wd = P.sb([128, 22, D], BF16)
    cast_load(P, wd, g.w_down, 0, 0, D, 22)
    gain2B = P.sb([128, D], F32)
    P.dma("sp", gain2B.all(), g.gain2[0:1, :].bc([128, D]))
    gain3B = P.sb([128, D], F32)
    P.dma("sp", gain3B.all(), g.gain3[0:1, :].bc([128, D]))
    nsub = ST // 128
    x1s = [P.sb([128, D], F32) for _ in range(nsub)]
    xn = P.sb([128, D], BF16)
    xnT = P.sb([128, 8, 128], BF16)
    xn2T = P.sb([128, 8, ST], BF16)
    hT = P.sb([128, 22, ST], BF16)
    sgt = P.sb([128, ST], F32)
    x2 = P.sb([128, D], F32)
    outt = P.sb([128, D], F32)
    sm = P.sb([128, 8], F32)
    bk = [0]

    def nb():
        b = bk[0]
        bk[0] = (bk[0] + 1) % 8
        return b

    for st in range(ntiles * 128 // ST):
        for sub in range(nsub):
            r0 = st * ST + sub * 128
            P.dma("sp", x1s[sub].all(), g.x1_d[r0:r0 + 128, :])
            norm_transpose(P, g, x1s[sub], gain2B, xn, xnT, None, sm, identb, nb())
            P.cp("pool", xn2T[:, :, sub * 128:(sub + 1) * 128], xnT.all())
        for f in range(22):
            pgt = P.ps(nb(), [128, ST])
            pup = P.ps(nb(), [128, ST])
            for k in range(8):
                P.mm(pgt.all(), wgu[:, k, f * 128:(f + 1) * 128], xn2T[:, k, :], start=(k == 0), stop=(k == 7))
            for k in range(8):
                P.mm(pup.all(), wgu[:, k, FH + f * 128:FH + (f + 1) * 128], xn2T[:, k, :], start=(k == 0), stop=(k == 7))
            P.act(sgt.all(), pgt.all(), AF.Silu)
            P.tt("dve", hT[:, f, :], sgt.all(), pup.all(), ALU.mult)
        for sub in range(nsub):
            r0 = st * ST + sub * 128
            for j in range(2):
                pd = P.ps(nb(), [128, 512])
                for f in range(22):
                    P.mm(pd.all(), hT[:, f, sub * 128:(sub + 1) * 128], wd[:, f, j * 512:(j + 1) * 512], start=(f == 0), stop=(f == 21))
                P.tt("dve", x2[:, j * 512:(j + 1) * 512], x1s[sub][:, j * 512:(j + 1) * 512], pd.all(), ALU.add)
            P.act(outt.all(), x2.all(), AF.Square, accum=sm[:, 4:5])
            rstd_of(P, sm[:, 5:6], sm[:, 4:5], D, sm[:, 6:7])
            P.stt(outt.all(), x2.all(), sm[:, 5:6], gain3B.all(), ALU.mult, ALU.mult)
            P.dma("sp", g.out[r0:r0 + 128, :], outt.all())


def build_program(nseq=2, ntiles=None, dbg=None):
    nc = bass.Bass("TRN2", target_bir_lowering=False)
    P = Prog(nc)
    g = setup_common(P, nseq, dbg or {})
    load_consts(P, g)
    nt = nseq * NT if ntiles is None else ntiles
    ph = os.environ.get("K_PH", "123")
    phase1(P, g, nt, do_nsa=("n" not in ph))
    if "2" in ph:
        phase2(P, g, nt)
    if "3" in ph:
        phase3(P, g, nt)
    P.finalize()
    return nc, P


def kernel(**inputs):
    x = np.asarray(inputs["x"], np.float32)
    B = x.shape[0]
    ncore = 8
    nseq = B // ncore
    nc, P = build_program(nseq=nseq)
    sh = shared_inputs(inputs)
    in_maps = []
    for c in range(ncore):
        m = dict(sh)
        m["x"] = np.ascontiguousarray(x[c * nseq:(c + 1) * nseq].reshape(nseq * S, D))
        in_maps.append(m)
    res = run_bass_kernel_spmd(nc, in_maps, core_ids=list(range(ncore)))
    out = np.stack([np.asarray(r["out"], np.float32).reshape(nseq, S, D) for r in res.results], 0)
    return out.reshape(B, S, D)
```

```python
import numpy as np
import ml_dtypes
import concourse.bass as bass
import concourse.mybir as mybir
from concourse.bass_utils import run_bass_kernel_spmd

F32 = mybir.dt.float32
BF16 = mybir.dt.bfloat16
U8 = mybir.dt.uint8
ALU = mybir.AluOpType
AF = mybir.ActivationFunctionType
DSZ = {F32: 4, BF16: 2, U8: 1}

ENGS = ("pe", "act", "dve", "pool", "sp")
DMA_POOL = 6
NEG = -30000.0


class V:
    __slots__ = ("ap", "reg")

    def __init__(self, ap, reg):
        self.ap = ap
        self.reg = reg

    def w(self, fn):
        return V(fn(self.ap), self.reg)

    def bc(self, shape):
        return V(self.ap.broadcast_to(list(shape)), self.reg)


class T:
    def __init__(self, space, ap, shape, esz, boff):
        self.space = space
        self.ap = ap
        self.shape = tuple(shape)
        self.esz = esz
        self.boff = boff
        fs = [1] * len(shape)
        for i in range(len(shape) - 2, 0, -1):
            fs[i] = fs[i + 1] * shape[i + 1]
        self.fstr = fs

    def __getitem__(self, idx):
        if not isinstance(idx, tuple):
            idx = (idx,)
        idx = idx + (slice(None),) * (len(self.shape) - len(idx))
        lo, hi = [], []
        for ix, n in zip(idx, self.shape):
            if isinstance(ix, slice):
                a, b, st = ix.indices(n)
                assert st > 0 and b > a, (self.space, idx, self.shape)
                lo.append(a)
                hi.append(a + ((b - a - 1) // st) * st)
            else:
                assert 0 <= ix < n, (self.space, idx, self.shape)
                lo.append(ix)
                hi.append(ix)
        f0 = sum(l * s for l, s in zip(lo[1:], self.fstr[1:]))
        f1 = sum(h * s for h, s in zip(hi[1:], self.fstr[1:])) + 1
        p0, p1 = lo[0], hi[0] + 1
        b0, b1 = self.boff + f0 * self.esz, self.boff + f1 * self.esz
        if self.space == "ps":
            p0, p1 = p0 // 32 * 32, (p1 + 31) // 32 * 32
            b0, b1 = b0 // 2048 * 2048, (b1 + 2047) // 2048 * 2048
        return V(self.ap[idx], (self.space, p0, p1, b0, b1))

    def all(self):
        return self[tuple(slice(None) for _ in self.shape)]


def _ovl(a, b):
    return a[0] == b[0] and a[1] < b[2] and b[1] < a[2] and a[3] < b[4] and b[3] < a[4]


def _contains(a, b):
    return a[0] == b[0] and a[1] <= b[1] and b[2] <= a[2] and a[3] <= b[3] and b[4] <= a[4]


class Op:
    __slots__ = ("eng", "emit", "idx", "deps", "dma", "tok", "sig", "eidx")


class Prog:
    def __init__(self, nc, sb_bytes=204288):
        self.nc = nc
        self.ops = []
        self.acc = {}
        self.sb_h = nc.alloc_sbuf_tensor("arena", [128, sb_bytes], U8)
        self.ps_h = nc.alloc_psum_tensor("psarena", [128, 4096], F32)
        self.sb_bytes = sb_bytes
        self.sb_off = 0
        self.marks = []

    def sb(self, shape, dtype, align=32):
        esz = DSZ[dtype]
        n = int(np.prod(shape[1:]))
        off = (self.sb_off + align - 1) // align * align
        nb = n * esz
        assert off + nb <= self.sb_bytes, ("SBUF arena overflow", off, nb)
        self.sb_off = off + nb
        ap = self.sb_h[:, off:off + nb].bitcast(dtype)
        if len(shape) > 2:
            names = " ".join("d%d" % i for i in range(1, len(shape)))
            ap = ap.rearrange("p (%s) -> p %s" % (names, names), **{"d%d" % i: shape[i] for i in range(1, len(shape))})
        return T("sb", ap, [128] + list(shape[1:]), esz, off)

    def mark(self):
        self.marks.append(self.sb_off)

    def release(self):
        self.sb_off = self.marks.pop()

    def ps(self, bank, shape, dtype=F32, boff=0):
        esz = DSZ[dtype]
        n = int(np.prod(shape[1:]))
        nb = n * esz
        assert boff + nb <= 2048
        e0 = (bank * 2048 + boff) // 4
        ap = self.ps_h[:, e0:e0 + (nb + 3) // 4]
        if dtype != F32:
            ap = ap.bitcast(dtype)
        if len(shape) > 2:
            names = " ".join("d%d" % i for i in range(1, len(shape)))
            ap = ap.rearrange("p (%s) -> p %s" % (names, names), **{"d%d" % i: shape[i] for i in range(1, len(shape))})
        return T("ps", ap, [128] + list(shape[1:]), esz, bank * 2048 + boff)

    def dram(self, name, shape, dtype, kind="Internal"):
        h = self.nc.dram_tensor(name, list(shape), dtype, kind=kind)
        return T(name, h, shape, 1, 0)

    def op(self, eng, emit, reads, writes, dma=False):
        o = Op()
        o.eng, o.emit, o.idx, o.dma, o.sig = eng, emit, len(self.ops), dma, False
        deps = set()
        for v in reads:
            r = v.reg
            for (reg, oi, w) in self.acc.setdefault(r[0], []):
                if w and _ovl(reg, r):
                    deps.add(oi)
        for v in writes:
            r = v.reg
            for (reg, oi, w) in self.acc.setdefault(r[0], []):
                if _ovl(reg, r):
                    deps.add(oi)
        deps.discard(o.idx)
        o.deps = deps
        self.ops.append(o)
        for v in writes:
            r = v.reg
            lst = self.acc[r[0]]
            lst[:] = [e for e in lst if not _contains(r, e[0])]
            lst.append((r, o.idx, True))
        for v in reads:
            r = v.reg
            lst = self.acc[r[0]]
            if not dma:
                lst[:] = [e for e in lst if not ((not e[2]) and _contains(r, e[0])
                                                 and (not self.ops[e[1]].dma) and self.ops[e[1]].eng == eng)]
            lst.append((r, o.idx, False))
        return o

    def dma(self, q, out, in_):
        return self.op(q, lambda e: e.dma_start(out=out.ap, in_=in_.ap), [in_], [out], dma=True)

    def mm(self, out, lhsT, rhs, start=True, stop=True, **kw):
        return self.op("pe", lambda e: e.matmul(out.ap, lhsT=lhsT.ap, rhs=rhs.ap, start=start, stop=stop, **kw), [lhsT, rhs], [out])

    def tr(self, out, in_, ident):
        return self.op("pe", lambda e: e.transpose(out.ap, in_.ap, ident.ap), [in_, ident], [out])

    def act(self, out, in_, func, bias=None, scale=None, accum=None, eng="act"):
        rd = [in_]
        kw = {}
        if bias is not None:
            if isinstance(bias, V):
                rd.append(bias)
                kw["bias"] = bias.ap
            else:
                kw["bias"] = float(bias)
        if scale is not None:
            if isinstance(scale, V):
                rd.append(scale)
                kw["scale"] = scale.ap
            else:
                kw["scale"] = float(scale)
        wr = [out]
        if accum is not None:
            wr.append(accum)
            kw["accum_out"] = accum.ap
        return self.op(eng, lambda e: e.activation(out=out.ap, in_=in_.ap, func=func, **kw), rd, wr)

    def tt(self, eng, out, in0, in1, op):
        return self.op(eng, lambda e: e.tensor_tensor(out=out.ap, in0=in0.ap, in1=in1.ap, op=op), [in0, in1], [out])

    def ts(self, eng, out, in0, s1, op0, s2=None, op1=None):
        rd = [in0]
        a1 = s1.ap if isinstance(s1, V) else float(s1)
        if isinstance(s1, V):
            rd.append(s1)
        kw = {}
        if op1 is not None:
            kw["op1"] = op1
            kw["scalar2"] = s2.ap if isinstance(s2, V) else float(s2)
            if isinstance(s2, V):
                rd.append(s2)
        else:
            kw["scalar2"] = None
        return self.op(eng, lambda e: e.tensor_scalar(out=out.ap, in0=in0.ap, scalar1=a1, op0=op0, **kw), rd, [out])

    def stt(self, out, in0, scalar, in1, op0, op1):
        rd = [in0, in1]
        a = scalar.ap if isinstance(scalar, V) else float(scalar)
        if isinstance(scalar, V):
            rd.append(scalar)
        return self.op("dve", lambda e: e.scalar_tensor_tensor(out=out.ap, in0=in0.ap, scalar=a, in1=in1.ap, op0=op0, op1=op1), rd, [out])

    def cp(self, eng, out, in_):
        if eng == "act":
            return self.op("act", lambda e: e.copy(out=out.ap, in_=in_.ap), [in_], [out])
        return self.op(eng, lambda e: e.tensor_copy(out=out.ap, in_=in_.ap), [in_], [out])

    def memset(self, eng, out, val):
        return self.op(eng, lambda e: e.memset(out.ap, val), [], [out])

    def recip(self, out, in_):
        return self.op("dve", lambda e: e.reciprocal(out=out.ap, in_=in_.ap), [in_], [out])

    def finalize(self):
        nc = self.nc
        ops = self.ops

        def pepe(a, b):
            return a.eng == "pe" and b.eng == "pe" and not a.dma and not b.dma

        for o in ops:
            for d in o.deps:
                if not pepe(ops[d], o):
                    ops[d].sig = True
            if o.dma:
                o.sig = True
        eng_sem = {e: nc.alloc_semaphore("s_" + e) for e in ENGS}
        dma_sems = {e: [nc.alloc_semaphore("d_%s_%d" % (e, i)) for i in range(DMA_POOL)] for e in ("sp", "act", "pool")}
        cnt = {e: 0 for e in ENGS}
        dcnt = {e: 0 for e in ENGS}
        for o in ops:
            if o.dma:
                n = dcnt[o.eng]
                dcnt[o.eng] += 1
                o.tok = (("d", o.eng, n % DMA_POOL), 16 * (n // DMA_POOL + 1))
                o.eidx = n
            elif o.sig:
                cnt[o.eng] += 1
                o.tok = (("e", o.eng), cnt[o.eng])
            else:
                o.tok = None

        def semof(k):
            return eng_sem[k[1]] if k[0] == "e" else dma_sems[k[1]][k[2]]

        streams = {e: [] for e in ENGS}
        known = {e: {} for e in ENGS}
        for o in ops:
            need = {}
            for d in o.deps:
                od = ops[d]
                if pepe(od, o):
                    continue
                k, v = od.tok
                if need.get(k, 0) < v:
                    need[k] = v
            if o.dma and o.eidx >= DMA_POOL:
                k = ("d", o.eng, o.eidx % DMA_POOL)
                v = 16 * (o.eidx // DMA_POOL)
                if need.get(k, 0) < v:
                    need[k] = v
            kn = known[o.eng]
            waits = []
            for k, v in need.items():
                if kn.get(k, 0) < v:
                    kn[k] = v
                    waits.append((semof(k), v))
            streams[o.eng].append((waits, o))
        final = []
        for e in ENGS:
            if cnt[e]:
                final.append((eng_sem[e], cnt[e]))
            n = dcnt[e]
            for i in range(min(n, DMA_POOL)):
                last = ((n - 1 - i) // DMA_POOL) * DMA_POOL + i
                final.append((dma_sems[e][i], 16 * (last // DMA_POOL + 1)))
        self.stats = {e: len(streams[e]) for e in ENGS}
        self.stats["sig"] = dict(cnt)
        self.stats["dma"] = dict(dcnt)

        def run(eng_obj, name):
            for waits, o in streams[name]:
                for s, v in waits:
                    eng_obj.wait_ge(s, v)
                ins = o.emit(eng_obj)
                if o.tok is not None:
                    ins.then_inc(semof(o.tok[0]), 16 if o.dma else 1)
            if name == "sp":
                for s, v in final:
                    eng_obj.wait_ge(s, v)

        with nc.Block() as block:
            @block.tensor
            def _(e):
                run(e, "pe")

            @block.scalar
            def _(e):
                run(e, "act")

            @block.vector
            def _(e):
                run(e, "dve")

            @block.gpsimd
            def _(e):
                run(e, "pool")

            @block.sync
            def _(e):
                run(e, "sp")


D = 1024
S = 4096
NT = 32
INW = 5408
FH = 2816
EPS = 1e-6
C_Z = 1536
C_BA = 2048
C_Q = 2056
C_KC = 2568
C_MA = 3360
NW1 = 3360


def _bf(a):
    return np.asarray(a, np.float32).astype(ml_dtypes.bfloat16)


def host_consts():
    p = np.arange(128)
    cf = {}
    cf["identf"] = np.eye(128, dtype=np.float32)
    cf["identfold"] = (p[:, None] % 64 == np.arange(64)[None, :]).astype(np.float32)
    cf["blockones"] = (p[:, None] // 64 == p[None, :] // 64).astype(np.float32)
    cf["ublk"] = ((p[:, None] // 64 == p[None, :] // 64) & (p[:, None] <= p[None, :])).astype(np.float32)
    cm = np.zeros((128, 2, 128), np.float32)
    for c in range(2):
        cm[c * 64:(c + 1) * 64, c, :] = 1.0
    cf["chunkmask"] = cm.reshape(128, 256)
    pl = (p % 64)[:, None]
    xx = np.arange(64)[None, :]
    mA = np.where(xx < pl, 0.0, NEG)
    mAT = np.where(xx > pl, 0.0, NEG)
    mQT = np.where(xx >= pl, 0.0, NEG)
    md = np.stack([np.repeat(m[:, None, :], 4, 1) for m in (mA, mAT, mQT)], 1)
    cf["maskD"] = md.reshape(128, 768).astype(np.float32)
    half = 8
    inv_freq = 500000.0 ** (-np.arange(half, dtype=np.float32) / half)
    pos = np.arange(S, dtype=np.float32)
    ang = pos[:, None] * inv_freq[None, :]
    cos = np.cos(ang).astype(np.float32).reshape(NT, 128, 8).transpose(1, 0, 2).reshape(128, NT * 8)
    sin = np.sin(ang).astype(np.float32).reshape(NT, 128, 8).transpose(1, 0, 2).reshape(128, NT * 8)
    cf["cosk"] = cos
    cf["sink"] = sin
    c = np.arange(128)[None, :]
    jr = c - 64
    cr = (p[:, None] >= 64).astype(np.int64)
    invalid = jr > cr
    forced = (jr == cr) | (jr == cr - 1)
    cf["keep"] = (~(invalid | forced)).astype(np.float32)
    cf["ovr"] = np.where(invalid, -1.0, np.where(forced, 1000.0, 0.0)).astype(np.float32)
    cb = {}
    cb["identb"] = np.eye(128, dtype=np.float32)
    fq = np.floor((p - 31) / 16.0).astype(np.int64)
    step = np.zeros((128, 768), np.float32)
    for vi in range(9):
        step[vi, :] = np.where((np.arange(768) - 384) > (vi - 2), NEG, 0.0)
    cb["step"] = step
    oh = np.zeros((128, 128), np.float32)
    for vi in range(9):
        oh[vi, :] = (fq == vi - 2)
    cb["onehot"] = np.tile(oh, (1, 4))
    be = np.zeros((128, 4096), np.float32)
    for j in range(64):
        be[j, j * 64:(j + 1) * 64] = 1.0
    cb["bigexp"] = be
    k = p[:, None]
    ql = p[None, :]
    cb["caus"] = np.tile(np.where(k > ql, NEG, 0.0), (1, 4))
    cb["winlo"] = np.tile(np.where(k <= ql, NEG, 0.0), (1, 4))
    cfo, cbo = {}, {}
    o = 0
    for kk, vv in cf.items():
        cfo[kk] = (o, vv.shape[1])
        o += vv.shape[1]
    ncf = o
    o = 0
    for kk, vv in cb.items():
        cbo[kk] = (o, vv.shape[1])
        o += vv.shape[1]
    ncb = o
    cfa = np.concatenate(list(cf.values()), 1).astype(np.float32)
    cba = _bf(np.concatenate(list(cb.values()), 1))
    return cfa, cba, cfo, cbo, ncf, ncb


_HC = host_consts()


def bc3(v, n):
    return v.w(lambda a: a.unsqueeze(2).broadcast_to([a.shape[0], a.shape[1], n]))


def bcm(v, m):
    return v.w(lambda a: a.unsqueeze(1).broadcast_to([a.shape[0], m, a.shape[1]]))


import os
STOP = int(os.environ.get('K_STOP', '0'))
NSTOP = int(os.environ.get('K_NSTOP', '0'))


class Ctx:
    pass


def setup_common(P, nseq, dbg):
    g = Ctx()
    ntok = nseq * S
    g.nseq = nseq
    g.ntok = ntok
    g.x = P.dram("x", [ntok, D], F32, kind="ExternalInput")
    g.w_in = P.dram("w_in", [D, INW], F32, kind="ExternalInput")
    g.w_gu = P.dram("w_gu", [D, 2 * FH], F32, kind="ExternalInput")
    g.w_down = P.dram("w_down", [FH, D], F32, kind="ExternalInput")
    g.w_a = P.dram("w_a", [512, D], F32, kind="ExternalInput")
    g.w_b = P.dram("w_b", [512, D], F32, kind="ExternalInput")
    g.w_out = P.dram("w_out", [D, D], F32, kind="ExternalInput")
    g.w1k = P.dram("w1k", [2048, 128], F32, kind="ExternalInput")
    g.w1v = P.dram("w1v", [2048, 128], F32, kind="ExternalInput")
    g.w2k = P.dram("w2k", [128, 64], F32, kind="ExternalInput")
    g.w2v = P.dram("w2v", [128, 64], F32, kind="ExternalInput")
    g.posk = P.dram("posk", [64, 32], F32, kind="ExternalInput")
    g.posv = P.dram("posv", [64, 32], F32, kind="ExternalInput")
    g.gain1 = P.dram("gain1", [1, D], F32, kind="ExternalInput")
    g.gain2 = P.dram("gain2", [1, D], F32, kind="ExternalInput")
    g.gain3 = P.dram("gain3", [1, D], F32, kind="ExternalInput")
    g.onorm = P.dram("onorm", [1, 128], F32, kind="ExternalInput")
    g.alog = P.dram("alog", [1, 4], F32, kind="ExternalInput")
    g.dtb = P.dram("dtb", [1, 4], F32, kind="ExternalInput")
    g.convw = P.dram("convw", [128, 48], F32, kind="ExternalInput")
    g.cf = P.dram("cf", [128, _HC[4]], F32, kind="ExternalInput")
    g.cb = P.dram("cb", [128, _HC[5]], BF16, kind="ExternalInput")
    g.out = P.dram("out", [ntok, D], F32, kind="ExternalOutput")
    ntile = ntok // 128
    g.xnT_d = P.dram("xnT_d", [ntile * 128, D], BF16)
    g.oaT_d = P.dram("oaT_d", [ntile * 128, 512], BF16)
    g.obT_d = P.dram("obT_d", [ntile * 128, 512], BF16)
    g.x1_d = P.dram("x1_d", [ntok, D], F32)
    g.dbg = {}
    for name, shape in dbg.items():
        g.dbg[name] = P.dram("dbg_" + name, list(shape), F32, kind="ExternalOutput")
    return g


def load_consts(P, g):
    cfo, cbo = _HC[2], _HC[3]
    cft = P.sb([128, _HC[4]], F32)
    cbt = P.sb([128, _HC[5]], BF16)
    P.dma("sp", cft.all(), g.cf.all())
    P.dma("sp", cbt.all(), g.cb.all())
    g.cft, g.cbt = cft, cbt

    def CF(name, *shape):
        o, n = cfo[name]
        t = T("sb", cft.ap[:, o:o + n], [128, n], 4, cft.boff + o * 4)
        if shape:
            names = " ".join("d%d" % i for i in range(len(shape)))
            t = T("sb", t.ap.rearrange("p (%s) -> p %s" % (names, names), **{"d%d" % i: s for i, s in enumerate(shape)}),
                  [128] + list(shape), 4, cft.boff + o * 4)
        return t

    def CB(name):
        o, n = cbo[name]
        return T("sb", cbt.ap[:, o:o + n], [128, n], 2, cbt.boff + o * 2)

    g.CF, g.CB = CF, CB


def cast_load(P, dst, src_t, r0, c0, ncols, kchunks, step=1024):
    for k in range(kchunks):
        for cc in range(0, ncols, step):
            n = min(step, ncols - cc)
            P.dma("pool", dst[:, k, cc:cc + n], src_t[r0 + k * 128:r0 + (k + 1) * 128, c0 + cc:c0 + cc + n])


def rstd_of(P, out, ssq, n, tmp):
    P.ts("dve", tmp, ssq, 1.0 / n, ALU.mult, EPS, ALU.add)
    P.act(tmp, tmp, AF.Ln)
    P.act(out, tmp, AF.Exp, scale=-0.5)


def norm_transpose(P, g, xt, gainB, xn, xnT, junk, sm, identb, bank):
    P.act(xn.all(), xt.all(), AF.Square, accum=sm[:, 0:1])
    rstd_of(P, sm[:, 1:2], sm[:, 0:1], D, sm[:, 2:3])
    P.stt(xn.all(), xt.all(), sm[:, 1:2], gainB.all(), ALU.mult, ALU.mult)
    pT = P.ps(bank, [128, 8, 128], BF16)
    for k in range(8):
        P.tr(pT[:, k, :], xn[:, k * 128:(k + 1) * 128], identb.all())
    P.cp("act", xnT.all(), pT.all())


def phase1(P, g, ntiles, do_nsa=True):
    P.mark()
    CF, CB = g.CF, g.CB
    identb = CB("identb")
    identfold = CF("identfold")
    blockones = CF("blockones")
    ublk = CF("ublk")
    chunkmask = CF("chunkmask", 2, 128)
    maskD = CF("maskD", 3, 4, 64)
    win = P.sb([128, 8, NW1], BF16)
    cast_load(P, win, g.w_in, 0, 0, NW1, 8, step=840)
    gainB = P.sb([128, D], F32)
    P.dma("sp", gainB.all(), g.gain1[0:1, :].bc([128, D]))
    ognB = P.sb([128, 128], F32)
    P.dma("sp", ognB.all(), g.onorm[0:1, :].bc([128, 128]))
    sc0 = P.sb([128, 16], F32)
    P.dma("sp", sc0[:, 0:4], g.alog[0:1, :].bc([128, 4]))
    P.dma("sp", sc0[:, 4:8], g.dtb[0:1, :].bc([128, 4]))
    P.act(sc0[:, 8:12], sc0[:, 0:4], AF.Exp)
    P.ts("dve", sc0[:, 8:12], sc0[:, 8:12], -1.0, ALU.mult)
    nAB = sc0[:, 8:12]
    dtbB = sc0[:, 4:8]
    convw = P.sb([128, 48], F32)
    P.dma("sp", convw.all(), g.convw.all())
    cdiag = P.sb([128, 48, 128], BF16)
    for q in range(48):
        P.ts("dve", cdiag[:, q, :], identb.all(), convw[:, q:q + 1], ALU.mult)
    xt = P.sb([128, D], F32)
    xn = P.sb([128, D], BF16)
    junk = P.sb([128, 128], BF16)
    xnT = P.sb([128, 8, 128], BF16)
    sm = P.sb([128, 8], F32)
    cbuf = P.sb([128, 12, 131], BF16)
    cs = P.sb([128, 12, 128], BF16)
    ssq = P.sb([128, 8], F32)
    rn = P.sb([128, 8], F32)
    sc = P.sb([128, 64], F32)
    k_n = P.sb([128, 4, 128], BF16)
    kbg = P.sb([128, 4, 128], BF16)
    kd = P.sb([128, 4, 128], BF16)
    vb = P.sb([128, 4, 128], BF16)
    q_n = P.sb([128, 4, 128], BF16)
    kqT = P.sb([128, 8, 128], BF16)
    dgt = P.sb([128, 2, 4, 64], F32)
    DD = P.sb([128, 3, 4, 64], F32)
    EE = DD
    ATf = P.sb([128, 4, 64], F32)
    TTf = P.sb([128, 4, 64], F32)
    TTb = P.sb([128, 4, 64], BF16)
    Pb = [P.sb([128, 2, 4, 64], BF16) for _ in range(2)]
    QKm = P.sb([128, 4, 64], BF16)
    uS = P.sb([128, 4, 128], F32)
    wT = P.sb([128, 2, 4, 64], BF16)
    Sf = P.sb([128, 4, 128], F32)
    Sb = P.sb([128, 4, 128], BF16)
    vn = P.sb([128, 4, 128], BF16)
    tmpo = P.sb([128, 4, 128], F32)
    o_raw = P.sb([128, 4, 128], F32)
    zs = P.sb([128, 512], F32)
    oa = P.sb([128, 4, 128], BF16)
    oaT = P.sb([128, 4, 128], BF16)
    g1 = Ctx()
    if do_nsa:
        nsa_setup(P, g, g1)

    bk = [2, 6]

    def nb():
        b = bk[0]
        bk[0] = 2 + (bk[0] - 2 + 1) % 4
        return b

    def nbt():
        b = bk[1]
        bk[1] = 6 + (bk[1] - 6 + 1) % 2
        return b

    for gt in range(ntiles):
        b, i = divmod(gt, NT)
        r0 = gt * 128
        if i == 0:
            P.memset("pool", cbuf[:, :, 0:3], 0.0)
            P.memset("pool", Sf.all(), 0.0)
            P.memset("pool", Sb.all(), 0.0)
        P.dma("sp", xt.all(), g.x[r0:r0 + 128, :])
        norm_transpose(P, g, xt, gainB, xn, xnT, junk, sm, identb, nb())
        P.dma("sp", g.xnT_d[r0:r0 + 128, :], xnT.all().w(lambda a: a.rearrange("p a b -> p (a b)")))
        if STOP == 1:
            continue
        for fb in range(3):
            pq = P.ps(nb(), [128, 4, 128])
            for ff in range(4):
                f = fb * 4 + ff
                for k in range(8):
                    P.mm(pq[:, ff, :], win[:, k, f * 128:(f + 1) * 128], xnT[:, k, :], start=(k == 0), stop=(k == 7))
            P.cp("act", cbuf[:, fb * 4:(fb + 1) * 4, 3:131], pq.all())
        for fb in range(3):
            pc = P.ps(nb(), [128, 4, 128])
            for ff in range(4):
                f = fb * 4 + ff
                for j in range(4):
                    P.mm(pc[:, ff, :], cdiag[:, f * 4 + j, :], cbuf[:, f, j:j + 128], start=(j == 0), stop=(j == 3))
            P.act(cs[:, fb * 4:(fb + 1) * 4, :], pc.all(), AF.Silu)
        P.cp("pool", cbuf[:, :, 0:3], cbuf[:, :, 128:131])
        if STOP == 2:
            continue
        psA = P.ps(nb(), [128, 8, 128], BF16)
        psB = P.ps(nb(), [128, 4, 128], BF16)
        for f in range(8):
            P.tr(psA[:, f, :], cs[:, f, :], identb.all())
        for f in range(4):
            P.tr(psB[:, f, :], cs[:, 8 + f, :], identb.all())
        for f in range(8):
            P.act(junk[:, 0:128], psA[:, f, :], AF.Square, accum=ssq[:, f:f + 1])
        P.ts("dve", rn.all(), ssq.all(), EPS, ALU.add)
        P.act(rn.all(), rn.all(), AF.Ln)
        P.act(rn.all(), rn.all(), AF.Exp, scale=-0.5)
        if STOP == 3:
            continue
        pba = P.ps(nb(), [128, 8])
        for k in range(8):
            P.mm(pba.all(), xnT[:, k, :], win[:, k, C_BA:C_BA + 8], start=(k == 0), stop=(k == 7))
        P.act(sc[:, 52:56], pba[:, 0:4], AF.Exp, scale=-1.0)
        P.act(sc[:, 0:4], sc[:, 52:56], AF.Ln, bias=1.0)
        P.act(sc[:, 4:8], sc[:, 0:4], AF.Exp, scale=-1.0)
        P.tt("dve", sc[:, 8:12], pba[:, 4:8], dtbB, ALU.add)
        P.act(sc[:, 8:12], sc[:, 8:12], AF.Exp)
        P.act(sc[:, 8:12], sc[:, 8:12], AF.Ln, bias=1.0)
        P.tt("dve", sc[:, 8:12], sc[:, 8:12], nAB, ALU.mult)
        pg = P.ps(nb(), [128, 16])
        P.mm(pg[:, 0:4], ublk.all(), sc[:, 8:12])
        P.mm(pg[:, 4:8], blockones.all(), sc[:, 8:12])
        for c in range(2):
            P.mm(pg[:, 8 + 4 * c:12 + 4 * c], chunkmask[:, c, :], sc[:, 8:12])
        P.cp("act", sc[:, 12:20], pg[:, 0:8])
        P.act(sc[:, 28:36], pg[:, 8:16], AF.Exp)
        P.act(sc[:, 20:24], sc[:, 12:16], AF.Exp)
        P.tt("dve", sc[:, 24:28], sc[:, 16:20], sc[:, 12:16], ALU.subtract)
        P.act(sc[:, 24:28], sc[:, 24:28], AF.Exp)
        P.tt("dve", sc[:, 36:40], sc[:, 12:16], sc[:, 0:4], ALU.subtract)
        P.tt("dve", sc[:, 40:44], rn[:, 4:8], sc[:, 4:8], ALU.mult)
        P.tt("dve", sc[:, 40:44], sc[:, 40:44], sc[:, 20:24], ALU.mult)
        P.tt("dve", sc[:, 44:48], rn[:, 4:8], sc[:, 24:28], ALU.mult)
        P.ts("dve", sc[:, 48:52], rn[:, 0:4], 128.0 ** -0.5, ALU.mult)
        if STOP == 4:
            continue
        P.tt("dve", k_n.all(), psA[:, 4:8, :], bc3(rn[:, 4:8], 128), ALU.mult)
        P.tt("dve", kbg.all(), psA[:, 4:8, :], bc3(sc[:, 40:44], 128), ALU.mult)
        P.tt("dve", kd.all(), psA[:, 4:8, :], bc3(sc[:, 44:48], 128), ALU.mult)
        P.tt("dve", q_n.all(), psA[:, 0:4, :], bc3(sc[:, 48:52], 128), ALU.mult)
        P.tt("dve", vb.all(), psB.all(), bc3(sc[:, 4:8], 128), ALU.mult)
        pkq = P.ps(nb(), [128, 8, 128], BF16)
        for h in range(4):
            P.tr(pkq[:, h, :], k_n[:, h, :], identb.all())
            P.tr(pkq[:, 4 + h, :], q_n[:, h, :], identb.all())
        P.cp("act", kqT.all(), pkq.all())
        if STOP == 5:
            continue
        pKQ = P.ps(nb(), [128, 2, 4, 64])
        for c in range(2):
            cs_ = slice(c * 64, (c + 1) * 64)
            for h in range(4):
                P.mm(pKQ[cs_, 0, h, :], kqT[:, h, cs_], kqT[:, h, cs_])
                P.mm(pKQ[cs_, 1, h, :], kqT[:, h, cs_], kqT[:, 4 + h, cs_])
        P.tt("pool", dgt[:, 0], bcm(identfold.all(), 4), bc3(sc[:, 12:16], 64), ALU.mult)
        P.tt("pool", dgt[:, 1], bcm(identfold.all(), 4), bc3(sc[:, 36:40], 64), ALU.mult)
        pBf = P.ps(nb(), [128, 2, 4, 64])
        P.mm(pBf.all().w(lambda a: a.rearrange("p a b c -> p (a b c)")), blockones.all(),
             dgt.all().w(lambda a: a.rearrange("p a b c -> p (a b c)")))
        P.tt("dve", DD[:, 0], bc3(sc[:, 36:40], 64), pBf[:, 0], ALU.subtract)
        P.tt("dve", DD[:, 1], pBf[:, 1], bc3(sc[:, 12:16], 64), ALU.subtract)
        P.tt("dve", DD[:, 2], pBf[:, 0], bc3(sc[:, 12:16], 64), ALU.subtract)
        P.tt("pool", DD.all(), DD.all(), maskD.all(), ALU.add)
        P.act(EE.all(), DD.all(), AF.Exp)
        if STOP == 6:
            continue
        P.tt("dve", Pb[0][:, 0], pKQ[:, 0], EE[:, 0], ALU.mult)
        P.tt("dve", ATf.all(), pKQ[:, 0], EE[:, 1], ALU.mult)
        P.tt("dve", QKm.all(), pKQ[:, 1], EE[:, 2], ALU.mult)
        P.cp("pool", Pb[0][:, 1], ATf.all())
        P.tt("pool", TTf.all(), bcm(identfold.all(), 4), ATf.all(), ALU.subtract)
        P.cp("pool", TTb.all(), TTf.all())
        if STOP == 7:
            continue

        def tail():
            cur = 0
            NL = 5
            for lvl in range(NL):
                last = lvl == NL - 1
                pN = P.ps(nbt(), [128, 2, 4, 64])
                Pc, Pn = Pb[cur], Pb[1 - cur]
                for c in range(2):
                    cs_ = slice(c * 64, (c + 1) * 64)
                    for h in range(4):
                        P.mm(pN[cs_, 0, h, :], Pc[cs_, 1, h, :], Pc[cs_, 0, h, :])
                        if not last:
                            P.mm(pN[cs_, 1, h, :], Pc[cs_, 0, h, :], Pc[cs_, 1, h, :])
                if last:
                    P.cp("act", Pn[:, 0], pN[:, 0])
                else:
                    P.cp("act", Pn.all(), pN.all())
                    yield
                pT_ = P.ps(nbt(), [128, 4, 64])
                for c in range(2):
                    cs_ = slice(c * 64, (c + 1) * 64)
                    for h in range(4):
                        P.mm(pT_[cs_, h, :], Pn[cs_, 0, h, :], TTb[cs_, h, :])
                P.tt("dve", TTf.all(), TTf.all(), pT_.all(), ALU.add)
                P.cp("pool", TTb.all(), TTf.all())
                yield
                cur = 1 - cur
            pU = P.ps(nbt(), [128, 4, 128])
            pW = P.ps(nbt(), [128, 2, 4, 64])
            for c in range(2):
                cs_ = slice(c * 64, (c + 1) * 64)
                for h in range(4):
                    P.mm(pU[cs_, h, :], TTb[cs_, h, :], vb[cs_, h, :])
                    P.mm(pW[:, c, h, :], kbg[cs_, h, :], TTb[cs_, h, :])
            P.cp("act", uS.all(), pU.all())
            P.cp("act", wT.all(), pW.all())
            yield
            for c in range(2):
                cs_ = slice(c * 64, (c + 1) * 64)
                pWS = P.ps(nbt(), [128, 4, 128])
                pQS = P.ps(nbt(), [128, 4, 128])
                for h in range(4):
                    P.mm(pWS[cs_, h, :], wT[:, c, h, :], Sb[:, h, :])
                    P.mm(pQS[cs_, h, :], kqT[:, 4 + h, cs_], Sb[:, h, :])
                P.tt("dve", vn[cs_], uS[cs_], pWS[cs_], ALU.subtract)
                P.tt("dve", tmpo[cs_], pQS[cs_], bc3(sc[cs_, 20:24], 128), ALU.mult)
                yield
                pO = P.ps(nbt(), [128, 4, 128])
                pS = P.ps(nbt(), [128, 4, 128])
                for h in range(4):
                    P.mm(pO[cs_, h, :], QKm[cs_, h, :], vn[cs_, h, :])
                    P.mm(pS[:, h, :], kd[cs_, h, :], vn[cs_, h, :])
                P.tt("dve", o_raw[cs_], tmpo[cs_], pO[cs_], ALU.add)
                yield
                P.tt("pool", Sf.all(), Sf.all(), bc3(sc[:, 28 + 4 * c:32 + 4 * c], 128), ALU.mult)
                P.tt("dve", Sf.all(), Sf.all(), pS.all(), ALU.add)
                P.cp("pool", Sb.all(), Sf.all())
                yield
            pz = P.ps(nbt(), [128, 512])
            for k in range(8):
                P.mm(pz.all(), xnT[:, k, :], win[:, k, C_Z:C_Z + 512], start=(k == 0), stop=(k == 7))
            P.act(zs.all(), pz.all(), AF.Silu)
            yield
            for h in range(4):
                P.act(junk[:, 0:128], o_raw[:, h, :], AF.Square, accum=sc[:, 52 + h:53 + h])
            rstd_of(P, sc[:, 56:60], sc[:, 52:56], 128, sc[:, 60:64])
            yield
            P.tt("dve", tmpo.all(), o_raw.all(), bc3(sc[:, 56:60], 128), ALU.mult)
            P.tt("pool", tmpo.all(), tmpo.all(), bcm(ognB.all(), 4), ALU.mult)
            P.tt("dve", oa.all().w(lambda a: a.rearrange("p a b -> p (a b)")), tmpo.all().w(lambda a: a.rearrange("p a b -> p (a b)")), zs.all(), ALU.mult)
            if "o_a" in g.dbg:
                P.tt("pool", tmpo.all().w(lambda a: a.rearrange("p a b -> p (a b)")), tmpo.all().w(lambda a: a.rearrange("p a b -> p (a b)")), zs.all(), ALU.mult)
                P.dma("sp", g.dbg["o_a"][r0:r0 + 128, :], tmpo.all().w(lambda a: a.rearrange("p a b -> p (a b)")))
            if "o_raw" in g.dbg:
                P.dma("sp", g.dbg["o_raw"][r0:r0 + 128, :], o_raw.all().w(lambda a: a.rearrange("p a b -> p (a b)")))
            pOT = P.ps(nbt(), [128, 4, 128], BF16)
            for h in range(4):
                P.tr(pOT[:, h, :], oa[:, h, :], identb.all())
            P.cp("act", oaT.all(), pOT.all())
            P.dma("sp", g.oaT_d[r0:r0 + 128, :], oaT.all().w(lambda a: a.rearrange("p a b -> p (a b)")))
            yield

        tg = tail()
        if do_nsa:
            nsa_tile(P, g, g1, gt, xnT, win, nb, bg=tg)
        for _ in tg:
            pass
    P.release()


def shared_inputs(inp):
    f = lambda a: np.ascontiguousarray(np.asarray(a, np.float32))
    cw = np.asarray(inp["gdn_conv_w"], np.float32)[0]
    convw = cw.reshape(4, 12, 128).transpose(2, 1, 0).reshape(128, 48)
    d = {
        "w_in": f(inp["w_in"][0]), "w_gu": f(inp["w_gate_up"][0]), "w_down": f(inp["w_down"][0]),
        "w_a": f(inp["w_branch_gdn"][0]), "w_b": f(inp["w_branch_nsa"][0]), "w_out": f(inp["w_out"][0]),
        "w1k": f(np.asarray(inp["cmp_w1_k"])[0].reshape(2048, 128)), "w1v": f(np.asarray(inp["cmp_w1_v"])[0].reshape(2048, 128)),
        "w2k": f(inp["cmp_w2_k"][0]), "w2v": f(inp["cmp_w2_v"][0]),
        "posk": f(np.asarray(inp["cmp_pos_k"])[0].T), "posv": f(np.asarray(inp["cmp_pos_v"])[0].T),
        "gain1": f(inp["mix_norm_gain"]).reshape(1, D), "gain2": f(inp["ffn_norm_gain"]).reshape(1, D),
        "gain3": f(inp["final_norm_gain"]).reshape(1, D), "onorm": f(inp["gdn_out_norm_gain"]).reshape(1, 128),
        "alog": f(inp["gdn_a_log"]).reshape(1, 4), "dtb": f(inp["gdn_dt_bias"]).reshape(1, 4),
        "convw": f(convw), "cf": _HC[0], "cb": _HC[1],
    }
    return d


def nsa_setup(P, g, n):
    CF, CB = g.CF, g.CB
    n.identb = CB("identb")
    n.step = CB("step")
    n.onehot = CB("onehot")
    n.bigexp = CB("bigexp")
    n.caus = CB("caus")
    n.winlo = CB("winlo")
    n.keep = CF("keep")
    n.ovr = CF("ovr")
    n.cosk = CF("cosk", NT, 8)
    n.sink = CF("sink", NT, 8)
    n.w1 = []
    n.w2 = []
    n.bias = []
    pb = P.ps(7, [128, 2])
    for kv, (w1d, w2d, posd, nm) in enumerate(((g.w1k, g.w2k, g.posk, "w1k"), (g.w1v, g.w2v, g.posv, "w1v"))):
        w1 = P.sb([128, 32, 128], BF16)
        P.dma("pool", w1[0:64], V(w1d.ap.rearrange("(l d) h -> d l h", d=64), (nm, 0, 2048, 0, 128)))
        w2 = P.sb([128, 64], BF16)
        P.dma("pool", w2.all(), w2d.all())
        pt = P.sb([128, 32], BF16)
        P.dma("pool", pt[0:64], posd.all())
        for l in range(32):
            P.mm(pb[:, kv:kv + 1], w1[0:64, l, :], pt[0:64, l:l + 1], start=(l == 0), stop=(l == 31))
        n.w1.append(w1)
        n.w2.append(w2)
    bias = P.sb([128, 2], F32)
    P.cp("act", bias.all(), pb.all())
    n.biasv = bias
    n.ksT = P.sb([128, S], BF16)
    n.qTz = [P.sb([128, 4, 128], BF16) for _ in range(2)]
    P.memset("pool", n.qTz[0].all(), 0.0)
    P.memset("pool", n.qTz[1].all(), 0.0)
    n.vsE = P.sb([128, NT, 2, 65], BF16)
    n.kwT = P.sb([128, 2, 5, 128], BF16)
    n.vwE = P.sb([128, 5, 2, 65], BF16)
    n.kcr = P.sb([128, 2, 144], BF16)
    n.vcr = P.sb([128, 2, 144], BF16)
    n.kcT = P.sb([128, 2, 256], BF16)
    n.vcT = P.sb([128, 2, 256], BF16)
    n.vcE = P.sb([128, 2, 2, 65], BF16)
    P.memset("pool", n.vsE.all(), 1.0)
    P.memset("pool", n.vwE.all(), 1.0)
    P.memset("pool", n.vcE.all(), 1.0)
    n.q_r = P.sb([128, 8, 64], BF16)
    n.kr1 = P.sb([128, 8, 64], BF16)
    n.kr2 = P.sb([128, 2, 64], BF16)
    n.vraw = P.sb([128, 2, 64], BF16)
    n.rt = P.sb([128, 4, 8, 8], F32)
    n.gsig = P.sb([128, 2, 4, 3], F32)
    n.qT = P.sb([128, 8, 128], BF16)
    n.hid = P.sb([128, 2, 2, 8], BF16)
    n.esc = P.sb([128, 4, 256], F32)
    n.den = P.sb([128, 16], F32)
    n.psp = P.sb([128, 260], F32)
    n.imp = P.sb([128, 64], F32)
    n.score = P.sb([128, 64], F32)
    n.score2 = P.sb([128, 64], F32)
    n.m8 = P.sb([128, 16], F32)
    n.negsel = P.sb([128, 64], BF16)
    n.nselT4 = P.sb([128, 4, 128], BF16)
    P.memset("pool", n.nselT4.all(), 0.0)
    n.eT = [P.sb([128, 512], BF16) for _ in range(2)]
    n.ob_f = P.sb([128, 8, 64], F32)
    n.obtmp = P.sb([128, 4, 64], F32)
    n.ob = P.sb([128, 512], BF16)
    n.obT = P.sb([128, 4, 128], BF16)
    n.ei = 0


def rope(P, n, src, dst, H, cos, sin, rest_scale):
    rt = n.rt
    c = bcm(cos, H)
    s = bcm(sin, H)
    P.tt("dve", rt[:, 0, 0:H, :], src[:, :, 0:8], c, ALU.mult)
    P.tt("dve", rt[:, 1, 0:H, :], src[:, :, 8:16], s, ALU.mult)
    P.tt("dve", rt[:, 2, 0:H, :], src[:, :, 8:16], c, ALU.mult)
    P.tt("dve", rt[:, 3, 0:H, :], src[:, :, 0:8], s, ALU.mult)
    P.tt("pool", dst[:, :, 0:8], rt[:, 0, 0:H, :], rt[:, 1, 0:H, :], ALU.subtract)
    P.tt("pool", dst[:, :, 8:16], rt[:, 2, 0:H, :], rt[:, 3, 0:H, :], ALU.add)
    P.act(dst[:, :, 16:64], src[:, :, 16:64], AF.Copy, scale=rest_scale)


def nsa_tile(P, g, n, gt, xnT, win, nb, bg=None):
    b, i = divmod(gt, NT)
    r0 = gt * 128
    identb = n.identb
    slot = i % 5
    if i == 0:
        P.memset("pool", n.kcT.all(), 0.0)
        P.memset("pool", n.vcT.all(), 0.0)
        P.memset("pool", n.vcE[:, :, :, 0:64], 0.0)
        P.memset("pool", n.kcr.all(), 0.0)
        P.memset("pool", n.vcr.all(), 0.0)
    pn = []
    for (c0, w) in ((C_Q, 512), (C_KC, 512), (C_KC + 512, 280)):
        pt = P.ps(nb(), [128, 512])
        for k in range(8):
            P.mm(pt[:, 0:w], xnT[:, k, :], win[:, k, c0:c0 + w], start=(k == 0), stop=(k == 7))
        pn.append(pt)

    def v3(t, c0, H):
        return T("ps", t.ap[:, c0:c0 + H * 64].rearrange("p (h d) -> p h d", d=64), [128, H, 64], 4, t.boff + c0 * 4)

    rope(P, n, v3(pn[0], 0, 8), n.q_r, 8, n.cosk[:, i, :], n.sink[:, i, :], 1.0)
    rope(P, n, v3(pn[1], 0, 8), n.kr1, 8, n.cosk[:, i, :], n.sink[:, i, :], 1.0)
    rope(P, n, v3(pn[2], 0, 2), n.kr2, 2, n.cosk[:, i, :], n.sink[:, i, :], 1.0)
    P.cp("act", n.vraw.all(), v3(pn[1], 128, 2).all())
    P.cp("act", n.vsE[:, i, :, 0:64], v3(pn[1], 384, 2).all())
    P.cp("act", n.vwE[:, slot, :, 0:64], v3(pn[2], 128, 2).all())
    gs = n.gsig.all().w(lambda a: a.rearrange("p a b c -> p (a b c)"))
    P.act(gs, pn[2][:, 256:280], AF.Exp, scale=-1.0)
    P.ts("dve", gs, gs, 1.0, ALU.add)
    P.recip(gs, gs)
    if NSTOP == 1:
        return
    pqT = P.ps(nb(), [128, 8, 128], BF16)
    for h in range(8):
        P.tr(pqT[0:64, h, :], n.q_r[:, h, :], identb.all())
    P.act(n.qT[0:64], pqT[0:64], AF.Copy, scale=0.125)
    pq2 = P.ps(nb(), [128, 4, 128], BF16)
    for r in range(4):
        P.tr(pq2[0:64, r, :], n.q_r[:, r, :], identb.all())
        P.tr(pq2[64:128, r, :], n.q_r[:, 4 + r, :], identb.all())
    P.act(n.qTz[0][0:64], pq2[0:64], AF.Copy, scale=0.125)
    P.act(n.qTz[1][64:128], pq2[64:128], AF.Copy, scale=0.125)
    pks = P.ps(nb(), [128, 128], BF16)
    P.tr(pks[0:64, :], n.kr1[:, 4, :], identb.all())
    P.tr(pks[64:128, :], n.kr1[:, 5, :], identb.all())
    P.cp("dve", n.ksT[:, i * 128:(i + 1) * 128], pks.all())
    pkT = P.ps(nb(), [128, 8, 128], BF16)
    srcs = [n.kr1[:, 0, :], n.kr1[:, 1, :], n.vraw[:, 0, :], n.vraw[:, 1, :], n.kr1[:, 4, :], n.kr1[:, 5, :], n.kr2[:, 0, :], n.kr2[:, 1, :]]
    for h in range(8):
        P.tr(pkT[0:64, h, :], srcs[h], identb.all())
    P.cp("dve", n.kcr[0:64, :, 16:144], pkT[0:64, 0:2, :])
    P.cp("dve", n.vcr[0:64, :, 16:144], pkT[0:64, 2:4, :])
    P.cp("act", n.kwT[0:64, :, slot, :], pkT[0:64, 6:8, :])
    if NSTOP == 2:
        return
    n0 = 8 * i - 1 if i > 0 else 0
    nn = 8 if i > 0 else 7
    m0 = 0 if i > 0 else 1
    ph = P.ps(nb(), [128, 2, 2, 8])
    for kv, cr in enumerate((n.kcr, n.vcr)):
        for l in range(32):
            P.mm(ph[:, kv, :, :], n.w1[kv][0:64, l, :], cr[0:64, :, l:l + 113:16], start=(l == 0), stop=(l == 31))
        P.act(n.hid[:, kv, :, :], ph[:, kv, :, :], AF.Silu, bias=n.biasv[:, kv:kv + 1])
    pk2 = P.ps(nb(), [128, 2, 2, 8])
    for kv in range(2):
        P.mm(pk2[0:64, kv, :, :], n.w2[kv].all(), n.hid[:, kv, :, :])
    P.cp("act", n.kcT[0:64, :, n0:n0 + nn], pk2[0:64, 0, :, m0:8])
    P.cp("act", n.vcT[0:64, :, n0:n0 + nn], pk2[0:64, 1, :, m0:8])
    P.cp("pool", n.kcr[0:64, :, 0:16], n.kcr[0:64, :, 128:144])
    P.cp("pool", n.vcr[0:64, :, 0:16], n.vcr[0:64, :, 128:144])
    for ch in sorted(set((n0 // 128, (n0 + nn - 1) // 128))):
        pvc = P.ps(nb(), [128, 2, 64], BF16)
        for gq in range(2):
            P.tr(pvc[:, gq, :], n.vcT[0:64, gq, ch * 128:(ch + 1) * 128], identb[0:64, 0:64])
        P.cp("act", n.vcE[:, ch, :, 0:64], pvc.all())
        if NSTOP == 3:
            pass
    if NSTOP == 3:
        return
    nch = 1 if 8 * i + 7 <= 128 else 2
    NC = nch * 128
    sh = 384 - 8 * i
    for gq in range(2):
        qT4 = n.qT[0:64, gq * 4:(gq + 1) * 4, :].w(lambda a: a.rearrange("p a b -> p (a b)"))
        if nch == 1:
            bankA = nb()
            regs = [P.ps(bankA, [128, 4, 128])[:, r, :] for r in range(4)]
        else:
            bankA, bankB = nb(), nb()
            regs = [P.ps(bankA if r < 2 else bankB, [128, 2, 256])[:, r % 2, :] for r in range(4)]
        for r in range(4):
            P.mm(regs[r], n.qT[0:64, gq * 4 + r, :], n.kcT[0:64, gq, 0:NC], start=True, stop=False)
            P.mm(regs[r], n.onehot[0:9, 0:128], n.step[0:9, sh:sh + NC], start=False, stop=True)
        for r in range(4):
            P.act(n.esc[:, r, 0:NC], regs[r], AF.Exp, accum=n.den[:, r:r + 1])
            if NSTOP == 4:
                pass
        if NSTOP == 4:
            return
        P.ts("dve", n.den[:, 4:8], n.den[:, 0:4], 1e-30, ALU.max)
        P.recip(n.den[:, 4:8], n.den[:, 4:8])
        P.memset("pool", n.psp.all(), 0.0)
        P.ts("dve", n.psp[:, 1:1 + NC], n.esc[:, 0, 0:NC], n.den[:, 4:5], ALU.mult)
        for r in range(1, 4):
            P.stt(n.psp[:, 1:1 + NC], n.esc[:, r, 0:NC], n.den[:, 4 + r:5 + r], n.psp[:, 1:1 + NC], ALU.mult, ALU.add)
        P.tt("dve", n.imp.all(), n.psp[:, 0:256:4], n.psp[:, 1:257:4], ALU.add)
        for m in range(2, 5):
            P.tt("dve", n.imp.all(), n.imp.all(), n.psp[:, m:m + 256:4], ALU.add)
        P.tt("dve", n.score.all(), n.imp.all(), n.keep[:, 64 - 2 * i:128 - 2 * i], ALU.mult)
        P.tt("dve", n.score.all(), n.score.all(), n.ovr[:, 64 - 2 * i:128 - 2 * i], ALU.add)
        P.memset("dve", n.score[:, 0:1], 1000.0)
        if NSTOP == 5:
            return
        sc_, sc2, m8 = n.score.all(), n.score2.all(), n.m8
        m8a, m8b = m8[:, 0:8], m8[:, 8:16]
        P.op("dve", lambda e, o=m8a, s=sc_: e.max(out=o.ap, in_=s.ap), [sc_], [m8a])
        P.op("dve", lambda e, o=sc2, a=m8a, s=sc_: e.match_replace(out=o.ap, in_to_replace=a.ap, in_values=s.ap, imm_value=-2.0), [m8a, sc_], [sc2])
        P.op("dve", lambda e, o=m8b, s=sc2: e.max(out=o.ap, in_=s.ap), [sc2], [m8b])
        P.ts("dve", n.negsel.all(), n.score.all(), m8[:, 15:16], ALU.is_lt, NEG, ALU.mult)
        if NSTOP == 6:
            return
        pns = P.ps(nb(), [128, 128], BF16)
        P.tr(pns[0:64, :], n.negsel.all(), identb.all())
        P.cp("act", n.nselT4[0:64], bcm(pns[0:64, :], 4))
        nsel = n.nselT4.all().w(lambda a: a.rearrange("p a b -> p (a b)"))
        if NSTOP == 7:
            return

        def branch(chunks, kfn, vfn, maskfn, br, qop=None):
            qop = qT4 if qop is None else qop
            po = P.ps(gq, [128, 4, 65])
            first = True

            def issue_scores(kc_):
                pst = P.ps(nb(), [128, 512])
                masks = maskfn(kc_)
                P.mm(pst.all(), kfn(kc_), qop, start=True, stop=(len(masks) == 0))
                for mi, (ml, mr) in enumerate(masks):
                    P.mm(pst.all(), ml, mr, start=False, stop=(mi == len(masks) - 1))
                return pst

            pst_next = issue_scores(chunks[0])
            for ci, kc_ in enumerate(chunks):
                lastc = ci == len(chunks) - 1
                pst = pst_next
                if bg is not None:
                    next(bg, None)
                eT = n.eT[n.ei % 2]
                n.ei += 1
                P.act(eT.all(), pst.all(), AF.Exp)
                if not lastc:
                    pst_next = issue_scores(chunks[ci + 1])
                for r in range(4):
                    P.mm(po[:, r, :], eT[:, r * 128:(r + 1) * 128], vfn(kc_), start=first, stop=(lastc and r == 3), skip_group_check=True)
                    first = False
            P.ts("dve", n.den[:, 8:12], po[:, :, 64], 1e-30, ALU.max)
            P.recip(n.den[:, 8:12], n.den[:, 8:12])
            P.tt("dve", n.den[:, 8:12], n.den[:, 8:12], n.gsig[:, gq, :, br], ALU.mult)
            dst = n.ob_f[:, gq * 4:(gq + 1) * 4, :]
            if br == 0:
                P.tt("dve", dst, po[:, :, 0:64], bc3(n.den[:, 8:12], 64), ALU.mult)
            else:
                P.tt("dve", n.obtmp.all(), po[:, :, 0:64], bc3(n.den[:, 8:12], 64), ALU.mult)
                P.tt("pool", dst, dst, n.obtmp.all(), ALU.add)

        branch(list(range(nch)),
               lambda c: n.kcT[0:64, gq, c * 128:(c + 1) * 128],
               lambda c: n.vcE[:, c, gq, :],
               lambda c: [(n.step[0:9, sh + c * 128:sh + (c + 1) * 128], n.onehot[0:9, :])], 0)
        if NSTOP == 8:
            return
        branch(list(range(i + 1)),
               lambda c: n.ksT[:, c * 128:(c + 1) * 128],
               lambda c: n.vsE[:, c, gq, :],
               lambda c: [(n.bigexp[:, c * 128:(c + 1) * 128], nsel)] + ([(identb.all(), n.caus.all())] if c == i else []), 1,
               qop=n.qTz[gq].all().w(lambda a: a.rearrange("p a b -> p (a b)")))
        if NSTOP == 9:
            return
        branch(list(range(max(0, i - 4), i + 1)),
               lambda c: n.kwT[0:64, gq, c % 5, :],
               lambda c: n.vwE[:, c % 5, gq, :],
               lambda c: ([(identb.all(), n.caus.all())] if c == i else []) + ([(identb.all(), n.winlo.all())] if c == i - 4 else []), 2)
        if NSTOP == 10:
            return
    P.cp("act", n.ob.all(), n.ob_f.all().w(lambda a: a.rearrange("p a b -> p (a b)")))
    if "o_b" in g.dbg:
        P.dma("sp", g.dbg["o_b"][r0:r0 + 128, :], n.ob_f.all().w(lambda a: a.rearrange("p a b -> p (a b)")))
    pOT = P.ps(nb(), [128, 4, 128], BF16)
    for k in range(4):
        P.tr(pOT[:, k, :], n.ob[:, k * 128:(k + 1) * 128], identb.all())
    P.cp("act", n.obT.all(), pOT.all())
    P.dma("sp", g.obT_d[r0:r0 + 128, :], n.obT.all().w(lambda a: a.rearrange("p a b -> p (a b)")))


def phase2(P, g, ntiles):
    P.mark()
    identb = g.CB("identb")
    wm = P.sb([128, 8, 2048], BF16)
    cast_load(P, wm, g.w_in, 0, C_MA, 2048, 8, step=1024)
    wa = P.sb([128, 4, D], BF16)
    cast_load(P, wa, g.w_a, 0, 0, D, 4)
    wb = P.sb([128, 4, D], BF16)
    cast_load(P, wb, g.w_b, 0, 0, D, 4)
    wo = P.sb([128, 8, D], BF16)
    cast_load(P, wo, g.w_out, 0, 0, D, 8)
    xnT = P.sb([128, 8, 128], BF16)
    oaT = P.sb([128, 4, 128], BF16)
    obT = P.sb([128, 4, 128], BF16)
    xt = P.sb([128, D], F32)
    sg = P.sb([128, 2048], F32)
    m1 = P.sb([128, 512], F32)
    m2 = P.sb([128, 512], F32)
    mg = P.sb([128, D], BF16)
    mT = P.sb([128, 8, 128], BF16)
    x1t = P.sb([128, D], F32)
    bk = [0]

    def nb():
        b = bk[0]
        bk[0] = (bk[0] + 1) % 8
        return b

    fl = lambda a: a.rearrange("p a b -> p (a b)")
    for gt in range(ntiles):
        r0 = gt * 128
        P.dma("sp", xnT.all().w(fl), g.xnT_d[r0:r0 + 128, :])
        P.dma("sp", oaT.all().w(fl), g.oaT_d[r0:r0 + 128, :])
        P.dma("sp", obT.all().w(fl), g.obT_d[r0:r0 + 128, :])
        P.dma("sp", xt.all(), g.x[r0:r0 + 128, :])
        for j in range(4):
            pg = P.ps(nb(), [128, 512])
            for k in range(8):
                P.mm(pg.all(), xnT[:, k, :], wm[:, k, j * 512:(j + 1) * 512], start=(k == 0), stop=(k == 7))
            P.act(sg[:, j * 512:(j + 1) * 512], pg.all(), AF.Exp, scale=-1.0)
        P.ts("pool", sg.all(), sg.all(), 1.0, ALU.add)
        P.recip(sg.all(), sg.all())
        for j in range(2):
            pa = P.ps(nb(), [128, 512])
            pb = P.ps(nb(), [128, 512])
            for k in range(4):
                P.mm(pa.all(), oaT[:, k, :], wa[:, k, j * 512:(j + 1) * 512], start=(k == 0), stop=(k == 3))
            for k in range(4):
                P.mm(pb.all(), obT[:, k, :], wb[:, k, j * 512:(j + 1) * 512], start=(k == 0), stop=(k == 3))
            P.tt("dve", m1.all(), pa.all(), sg[:, j * 512:(j + 1) * 512], ALU.mult)
            P.tt("dve", m2.all(), pb.all(), sg[:, 1024 + j * 512:1024 + (j + 1) * 512], ALU.mult)
            P.tt("pool", mg[:, j * 512:(j + 1) * 512], m1.all(), m2.all(), ALU.add)
        if "merged" in g.dbg:
            P.tt("pool", m1.all(), m1.all(), m2.all(), ALU.add)
            P.dma("sp", g.dbg["merged"][r0:r0 + 128, 512:1024], m1.all())
        pmT = P.ps(nb(), [128, 8, 128], BF16)
        for k in range(8):
            P.tr(pmT[:, k, :], mg[:, k * 128:(k + 1) * 128], identb.all())
        P.cp("act", mT.all(), pmT.all())
        for j in range(2):
            po = P.ps(nb(), [128, 512])
            for k in range(8):
                P.mm(po.all(), mT[:, k, :], wo[:, k, j * 512:(j + 1) * 512], start=(k == 0), stop=(k == 7))
            P.tt("dve", x1t[:, j * 512:(j + 1) * 512], xt[:, j * 512:(j + 1) * 512], po.all(), ALU.add)
        P.dma("sp", g.x1_d[r0:r0 + 128, :], x1t.all())
        if "x1" in g.dbg:
            P.dma("sp", g.dbg["x1"][r0:r0 + 128, :], x1t.all())
    P.release()


def phase3(P, g, ntiles, ST=256):
    P.sb_off = 0
    identb = P.sb([128, 128], BF16)
    P.dma("sp", identb.all(), g.cb[:, 0:128])
    wgu = P.sb([128, 8, 2 * FH], BF16)
    cast_load(P, wgu, g.w_gu, 0, 0, 2 * FH, 8, step=1408)
    wd = P.sb([128, 22, D], BF16)
    cast_load(P, wd, g.w_down, 0, 0, D, 22)
    gain2B = P.sb([128, D], F32)
    P.dma("sp", gain2B.all(), g.gain2[0:1, :].bc([128, D]))
    gain3B = P.sb([128, D], F32)
    P.dma("sp", gain3B.all(), g.gain3[0:1, :].bc([128, D]))
    nsub = ST // 128
    x1s = [P.sb([128, D], F32) for _ in range(nsub)]
    xn = P.sb([128, D], BF16)
    xnT = P.sb([128, 8, 128], BF16)
    xn2T = P.sb([128, 8, ST], BF16)
    hT = P.sb([128, 22, ST], BF16)
    sgt = P.sb([128, ST], F32)
    x2 = P.sb([128, D], F32)
    outt = P.sb([128, D], F32)
    sm = P.sb([128, 8], F32)
    bk = [0]

    def nb():
        b = bk[0]
        bk[0] = (bk[0] + 1) % 8
        return b

    for st in range(ntiles * 128 // ST):
        for sub in range(nsub):
            r0 = st * ST + sub * 128
            P.dma("sp", x1s[sub].all(), g.x1_d[r0:r0 + 128, :])
            norm_transpose(P, g, x1s[sub], gain2B, xn, xnT, None, sm, identb, nb())
            P.cp("pool", xn2T[:, :, sub * 128:(sub + 1) * 128], xnT.all())
        for f in range(22):
            pgt = P.ps(nb(), [128, ST])
            pup = P.ps(nb(), [128, ST])
            for k in range(8):
                P.mm(pgt.all(), wgu[:, k, f * 128:(f + 1) * 128], xn2T[:, k, :], start=(k == 0), stop=(k == 7))
            for k in range(8):
                P.mm(pup.all(), wgu[:, k, FH + f * 128:FH + (f + 1) * 128], xn2T[:, k, :], start=(k == 0), stop=(k == 7))
            P.act(sgt.all(), pgt.all(), AF.Silu)
            P.tt("dve", hT[:, f, :], sgt.all(), pup.all(), ALU.mult)
        for sub in range(nsub):
            r0 = st * ST + sub * 128
            for j in range(2):
                pd = P.ps(nb(), [128, 512])
                for f in range(22):
                    P.mm(pd.all(), hT[:, f, sub * 128:(sub + 1) * 128], wd[:, f, j * 512:(j + 1) * 512], start=(f == 0), stop=(f == 21))
                P.tt("dve", x2[:, j * 512:(j + 1) * 512], x1s[sub][:, j * 512:(j + 1) * 512], pd.all(), ALU.add)
            P.act(outt.all(), x2.all(), AF.Square, accum=sm[:, 4:5])
            rstd_of(P, sm[:, 5:6], sm[:, 4:5], D, sm[:, 6:7])
            P.stt(outt.all(), x2.all(), sm[:, 5:6], gain3B.all(), ALU.mult, ALU.mult)
            P.dma("sp", g.out[r0:r0 + 128, :], outt.all())


def build_program(nseq=2, ntiles=None, dbg=None):
    nc = bass.Bass("TRN2", target_bir_lowering=False)
    P = Prog(nc)
    g = setup_common(P, nseq, dbg or {})
    load_consts(P, g)
    nt = nseq * NT if ntiles is None else ntiles
    ph = os.environ.get("K_PH", "123")
    phase1(P, g, nt, do_nsa=("n" not in ph))
    if "2" in ph:
        phase2(P, g, nt)
    if "3" in ph:
        phase3(P, g, nt)
    P.finalize()
    return nc, P


def kernel(**inputs):
    x = np.asarray(inputs["x"], np.float32)
    B = x.shape[0]
    ncore = 8
    nseq = B // ncore
    nc, P = build_program(nseq=nseq)
    sh = shared_inputs(inputs)
    in_maps = []
    for c in range(ncore):
        m = dict(sh)
        m["x"] = np.ascontiguousarray(x[c * nseq:(c + 1) * nseq].reshape(nseq * S, D))
        in_maps.append(m)
    res = run_bass_kernel_spmd(nc, in_maps, core_ids=list(range(ncore)))
    out = np.stack([np.asarray(r["out"], np.float32).reshape(nseq, S, D) for r in res.results], 0)
    return out.reshape(B, S, D)
```
